# Optimizing a Trainium2 kernel written in Bass

```python
import jax
import jax.numpy as jnp
from jax import lax
import numpy as np

D_MODEL = 1024
BATCH = 16
SEQ = 256
DEPTH = 2
DEC_BATCH = 8
DEC_SEQ = 2048
PAST_LEN = 256

EPS = 1e-6
NEG_INF = -1e30
GRID_W = 64
N_A_LAYERS = (DEPTH + 1) // 2
N_C_LAYERS = DEPTH // 2
H_A = 4
DK_A = 128
DV_A = 128
A_KEY = H_A * DK_A
A_WIDTH = H_A * DV_A
HGRN_CHUNK = 32
H_B = 8
DH_B = 64
B_WIDTH = H_B * DH_B
NA_KH = 8
NA_KW = 16
NA_COL_BLOCK = 16
NA_BAND = NA_COL_BLOCK + NA_KW
NA_N_COL_BLOCKS = GRID_W // NA_COL_BLOCK
CTX_Q_BLOCK = 128
EVEN_SPLITS = (A_KEY, A_KEY, A_KEY, A_WIDTH, A_WIDTH, B_WIDTH, B_WIDTH, B_WIDTH, B_WIDTH)
EVEN_IN = 3 * A_KEY + 2 * A_WIDTH + 4 * B_WIDTH
W_C = 1024
H_C = 8
BW_C = W_C // H_C
CONV_W = 4
RG_C = 8.0

kernel_name = 'bidir_hgrn2_natten_rglru_prefix_dit_step'


def rmsnorm(x, g):
    x32 = x.astype(jnp.float32)
    y = x32 * lax.rsqrt(jnp.mean(x32 * x32, axis=-1, keepdims=True) + EPS)
    return (y * g.astype(jnp.float32)).astype(x.dtype)


def adaln(cond, w_mod, b_mod):
    m = (jax.nn.silu(cond) @ w_mod + b_mod)[:, None, :]
    return jnp.split(m, 3, axis=-1)


def split_heads(t, n_heads):
    B, T, _ = t.shape
    return t.reshape(B, T, n_heads, -1).transpose(0, 2, 1, 3)


def merge_heads(t):
    B, H, T, d = t.shape
    return t.transpose(0, 2, 1, 3).reshape(B, T, H * d)


def hgrn2_chunkwise(q, k, log_f, v, s0):
    B, H, T, DK = q.shape
    DV = v.shape[-1]
    n = T // HGRN_CHUNK
    shp = (B, H, n, HGRN_CHUNK)
    q, k, log_f = (t.reshape(shp + (DK,)) for t in (q, k, log_f))
    v = v.reshape(shp + (DV,))
    b = jnp.cumsum(log_f, axis=3)
    b_end = b[:, :, :, -1:, :]
    q_dec = q * jnp.exp(b)
    k_inv = k * jnp.exp(-b)
    lower = jnp.tril(jnp.ones((HGRN_CHUNK, HGRN_CHUNK), dtype=bool))
    att = jnp.where(lower, jnp.einsum('bhntd,bhnsd->bhnts', q_dec, k_inv), 0.0)
    o_intra = jnp.einsum('bhnts,bhnsv->bhntv', att, v)
    chunk_kv = jnp.einsum('bhnsd,bhnsv->nbhdv', k * jnp.exp(b_end - b), v)
    chunk_decay = jnp.moveaxis(jnp.exp(b_end[:, :, :, 0, :]), 2, 0)

    def step(S, inp):
        dec, kv = inp
        return dec[..., None] * S + kv, S

    s_fin, s_prev = lax.scan(step, s0, (chunk_decay, chunk_kv))
    o_inter = jnp.einsum('bhntd,nbhdv->bhntv', q_dec, s_prev)
    return (o_intra + o_inter).reshape(B, H, T, DV), s_fin


def hgrn2_bidirectional(q, z_fwd, z_bwd, v, lb, s0):
    qh = split_heads(q.astype(jnp.float32), H_A)
    vh = split_heads(v.astype(jnp.float32), H_A)
    outs, finals = [], []
    for d, z in enumerate((z_fwd, z_bwd)):
        z32 = z.astype(jnp.float32)
        f = lb[d] + (1.0 - lb[d]) * jax.nn.sigmoid(z32)
        k = (1.0 - lb[d]) * jax.nn.sigmoid(-z32)
        args = [qh, split_heads(k, H_A), split_heads(jnp.log(f), H_A), vh]
        if d == 1:
            args = [jnp.flip(t, axis=2) for t in args]
        o, s_fin = hgrn2_chunkwise(*args, s0[:, d].astype(jnp.float32))
        outs.append(jnp.flip(o, axis=2) if d == 1 else o)
        finals.append(s_fin)
    return outs[0] + outs[1], jnp.stack(finals, axis=1)


def context_attention(q, k, v):
    B, H, Tc, dh = q.shape
    scale = dh ** -0.5
    qb = jnp.moveaxis(q.reshape(B, H, Tc // CTX_Q_BLOCK, CTX_Q_BLOCK, dh), 2, 0)

    def block(qi):
        s = jnp.einsum('bhqd,bhkd->bhqk', qi, k).astype(jnp.float32) * scale
        p = jax.nn.softmax(s, axis=-1).astype(v.dtype)
        return jnp.einsum('bhqk,bhkd->bhqd', p, v)

    o = lax.map(block, qb)
    return jnp.moveaxis(o, 0, 2).reshape(B, H, Tc, dh)


def neighbourhood_attention(q, k, v, k_ctx, v_ctx, rel_bias):
    B, H, T, dh = q.shape
    rows = T // GRID_W
    kh = min(NA_KH, rows)
    scale = dh ** -0.5
    cols = np.arange(GRID_W).reshape(NA_N_COL_BLOCKS, NA_COL_BLOCK)
    band_start = np.clip(np.arange(NA_N_COL_BLOCKS) * NA_COL_BLOCK - NA_KW // 2, 0, GRID_W - NA_BAND)
    col_idx = (band_start[:, None] + np.arange(NA_BAND)).astype(np.int32)
    win_start = np.clip(cols - NA_KW // 2, 0, GRID_W - NA_KW)[..., None]
    key_col = col_idx[:, None, :]
    valid = jnp.asarray((key_col >= win_start) & (key_col < win_start + NA_KW))[:, :, None, :]
    dc_idx = np.clip(key_col - cols[..., None] + NA_KW - 1, 0, 2 * NA_KW - 2).astype(np.int32)[:, :, None, :]
    bias_tab = rel_bias.astype(jnp.float32)
    k_grid = k.reshape(B, H, rows, GRID_W, dh)
    v_grid = v.reshape(B, H, rows, GRID_W, dh)
    k_ctx = k_ctx.astype(q.dtype)
    v_ctx = v_ctx.astype(q.dtype)
    n_loc = kh * NA_BAND

    def row_block(args):
        r, q_r = args
        rs = jnp.clip(r - kh // 2, 0, rows - kh)
        k_blk = lax.dynamic_slice_in_dim(k_grid, rs, kh, axis=2)[:, :, :, col_idx]
        v_blk = lax.dynamic_slice_in_dim(v_grid, rs, kh, axis=2)[:, :, :, col_idx]
        qb = q_r.reshape(B, H, NA_N_COL_BLOCKS, NA_COL_BLOCK, dh)
        dr_idx = rs + jnp.arange(kh) - r + NA_KH - 1
        bias = bias_tab[:, dr_idx[None, None, :, None], dc_idx]
        s_loc = jnp.einsum('bhjqd,bhkjnd->bhjqkn', qb, k_blk).astype(jnp.float32) * scale + bias
        s_loc = jnp.where(valid, s_loc, NEG_INF)
        s_ctx = jnp.einsum('bhjqd,bhcd->bhjqc', qb, k_ctx).astype(jnp.float32) * scale
        s = jnp.concatenate([s_loc.reshape(B, H, NA_N_COL_BLOCKS, NA_COL_BLOCK, n_loc), s_ctx], axis=-1)
        p = jax.nn.softmax(s, axis=-1).astype(v.dtype)
        p_loc = p[..., :n_loc].reshape(B, H, NA_N_COL_BLOCKS, NA_COL_BLOCK, kh, NA_BAND)
        o = (jnp.einsum('bhjqkn,bhkjnd->bhjqd', p_loc, v_blk)
             + jnp.einsum('bhjqc,bhcd->bhjqd', p[..., n_loc:], v_ctx))
        return o.reshape(B, H, GRID_W, dh)

    q_rows = jnp.moveaxis(q.reshape(B, H, rows, GRID_W, dh), 2, 0)
    o = lax.map(row_block, (jnp.arange(rows), q_rows))
    return jnp.moveaxis(o, 0, 2).reshape(B, H, T, dh)


def even_mixer(h, w_in, w_out, lb, out_gain, rel_bias, s0, ctx_kv):
    parts = jnp.split(h @ w_in, np.cumsum(EVEN_SPLITS)[:-1].tolist(), axis=-1)
    q_a, z_f, z_b, v_a, g_a, q_b, k_b, v_b, g_b = parts
    o_a, s_fin = hgrn2_bidirectional(q_a, z_f, z_b, v_a, lb, s0)
    o_a = merge_heads(rmsnorm(o_a, out_gain[None, :, None, :])).astype(h.dtype)
    qh, kh, vh = (split_heads(t, H_B) for t in (q_b, k_b, v_b))
    if ctx_kv is None:
        o_b = context_attention(qh, kh, vh)
    else:
        o_b = neighbourhood_attention(qh, kh, vh, ctx_kv[0], ctx_kv[1], rel_bias)
    y = jnp.concatenate([o_a * jax.nn.silu(g_a), merge_heads(o_b) * jax.nn.silu(g_b)], axis=-1)
    return y @ w_out, s_fin, kh, vh


def centred_dwconv(x, w, b):
    left = CONV_W // 2
    T = x.shape[1]
    xp = jnp.pad(x, ((0, 0), (left, CONV_W - 1 - left), (0, 0)))
    return sum(xp[:, j:j + T] * w[j] for j in range(CONV_W)) + b


def linear_recurrence(a, u, h0):
    u = u.at[:, 0].add(a[:, 0] * h0)

    def combine(left, right):
        a_l, u_l = left
        a_r, u_r = right
        return a_l * a_r, a_r * u_l + u_r

    _, hs = lax.associative_scan(combine, (a, u), axis=1)
    return hs, hs[:, -1]


def odd_mixer(h, w_in, w_out, conv_w, conv_b, gate_w, gate_b, lam, s0):
    xb, g = jnp.split(h @ w_in, 2, axis=-1)
    xc = centred_dwconv(xb, conv_w, conv_b).astype(jnp.float32)
    B, T, _ = xc.shape
    xblk = xc.reshape(B, T, H_C, BW_C)
    outs, finals = [], []
    for d in range(2):
        gates = jnp.einsum('bthi,ghij->gbthj', xblk, gate_w[d].astype(jnp.float32)).reshape(2, B, T, W_C)
        gates = gates + gate_b[d].astype(jnp.float32)[:, None, None, :]
        r = jax.nn.sigmoid(gates[0])
        i = jax.nn.sigmoid(gates[1])
        log_a = -RG_C * r * jax.nn.softplus(-lam[d].astype(jnp.float32))
        a = jnp.exp(log_a)
        u = jnp.sqrt(-jnp.expm1(2.0 * log_a)) * (i * xc)
        if d == 1:
            a, u = jnp.flip(a, axis=1), jnp.flip(u, axis=1)
        hs, h_fin = linear_recurrence(a, u, s0[:, d].astype(jnp.float32))
        outs.append(jnp.flip(hs, axis=1) if d == 1 else hs)
        finals.append(h_fin)
    y = ((outs[0] + outs[1]) * jax.nn.silu(g.astype(jnp.float32))).astype(h.dtype)
    return y @ w_out, jnp.stack(finals, axis=1)


def setup_inputs(seed: int = 0) -> dict:
    key = jax.random.key(seed)
    ks = jax.random.split(key, 24)

    def nrm(k, shape, s):
        return jax.random.normal(k, shape, jnp.float32) * s

    x_prompt = nrm(ks[0], (BATCH, SEQ, D_MODEL), 1.0)
    x_sample = nrm(ks[1], (DEC_BATCH, DEC_SEQ, D_MODEL), 1.0)
    state_hgrn = nrm(ks[2], (DEC_BATCH, N_A_LAYERS, 2, H_A, DK_A, DV_A), 0.5)
    cache_na_k = nrm(ks[3], (DEC_BATCH, N_A_LAYERS, H_B, PAST_LEN, DH_B), 1.0)
    cache_na_v = nrm(ks[4], (DEC_BATCH, N_A_LAYERS, H_B, PAST_LEN, DH_B), 1.0)
    state_rglru = nrm(ks[5], (DEC_BATCH, N_C_LAYERS, 2, W_C), 0.5)
    c = nrm(ks[6], (DEC_BATCH, D_MODEL), 1.0)
    c_ctx = nrm(ks[7], (D_MODEL,), 1.0)
    norm_gain = 1.0 + nrm(ks[8], (DEPTH, D_MODEL), 0.02)
    w_mod = nrm(ks[9], (DEPTH, D_MODEL, 3 * D_MODEL), D_MODEL ** -0.5)
    b_mod = nrm(ks[10], (DEPTH, 3 * D_MODEL), 0.02)
    w_in_even = nrm(ks[11], (N_A_LAYERS, D_MODEL, EVEN_IN), D_MODEL ** -0.5)
    w_out_even = nrm(ks[12], (N_A_LAYERS, A_WIDTH + B_WIDTH, D_MODEL), (A_WIDTH + B_WIDTH) ** -0.5)
    hgrn_lb_logits = nrm(ks[13], (2, N_A_LAYERS + 1, A_KEY), 0.5)
    hgrn_out_gain = 1.0 + nrm(ks[14], (N_A_LAYERS, H_A, DV_A), 0.02)
    na_rel_bias = nrm(ks[15], (N_A_LAYERS, H_B, 2 * NA_KH - 1, 2 * NA_KW - 1), 0.1)
    w_in_odd = nrm(ks[16], (N_C_LAYERS, D_MODEL, 2 * W_C), D_MODEL ** -0.5)
    w_out_odd = nrm(ks[17], (N_C_LAYERS, W_C, D_MODEL), W_C ** -0.5)
    conv_w = nrm(ks[18], (N_C_LAYERS, CONV_W, W_C), CONV_W ** -0.5)
    conv_b = nrm(ks[19], (N_C_LAYERS, W_C), 0.02)
    rg_gate_w = nrm(ks[20], (N_C_LAYERS, 2, 2, H_C, BW_C, BW_C), BW_C ** -0.5)
    rg_gate_b = nrm(ks[21], (N_C_LAYERS, 2, 2, W_C), 0.02)
    a_pow = jax.random.uniform(ks[22], (N_C_LAYERS, 2, W_C), jnp.float32, 0.9, 0.999)
    a_base = a_pow ** (1.0 / RG_C)
    rg_lambda = jnp.log(a_base) - jnp.log1p(-a_base)
    final_gain = 1.0 + nrm(ks[23], (D_MODEL,), 0.02)
    return {'x_prompt': x_prompt, 'x_sample': x_sample, 'state_hgrn': state_hgrn,
            'cache_na_k': cache_na_k, 'cache_na_v': cache_na_v, 'state_rglru': state_rglru,
            'c': c, 'c_ctx': c_ctx, 'norm_gain': norm_gain, 'w_mod': w_mod, 'b_mod': b_mod,
            'w_in_even': w_in_even, 'w_out_even': w_out_even, 'hgrn_lb_logits': hgrn_lb_logits,
            'hgrn_out_gain': hgrn_out_gain, 'na_rel_bias': na_rel_bias,
            'w_in_odd': w_in_odd, 'w_out_odd': w_out_odd, 'conv_w': conv_w, 'conv_b': conv_b,
            'rg_gate_w': rg_gate_w, 'rg_gate_b': rg_gate_b, 'rg_lambda': rg_lambda,
            'final_gain': final_gain}


def reference(x_prompt, x_sample, state_hgrn, cache_na_k, cache_na_v, state_rglru, c,
              c_ctx, norm_gain, w_mod, b_mod, w_in_even, w_out_even, hgrn_lb_logits,
              hgrn_out_gain, na_rel_bias, w_in_odd, w_out_odd, conv_w, conv_b,
              rg_gate_w, rg_gate_b, rg_lambda, final_gain):
    lb_all = jnp.cumsum(jax.nn.softmax(hgrn_lb_logits.astype(jnp.float32), axis=1), axis=1)
    xc, xs = x_prompt, x_sample
    cond_ctx = c_ctx[None, :]
    new_hgrn, new_k, new_v, new_rg = [], [], [], []
    for l in range(DEPTH):
        sh_c, sc_c, gt_c = adaln(cond_ctx, w_mod[l], b_mod[l])
        sh_s, sc_s, gt_s = adaln(c, w_mod[l], b_mod[l])
        h_c = rmsnorm(xc, norm_gain[l]) * (1 + sc_c) + sh_c
        h_s = rmsnorm(xs, norm_gain[l]) * (1 + sc_s) + sh_s
        j = l // 2
        if l % 2 == 0:
            s_zero = jnp.zeros((xc.shape[0], 2, H_A, DK_A, DV_A), jnp.float32)
            m_c, s_fin, k_c, v_c = even_mixer(h_c, w_in_even[j], w_out_even[j], lb_all[:, j],
                                              hgrn_out_gain[j], na_rel_bias[j], s_zero, None)
            m_s, _, _, _ = even_mixer(h_s, w_in_even[j], w_out_even[j], lb_all[:, j],
                                      hgrn_out_gain[j], na_rel_bias[j], state_hgrn[:, j],
                                      (cache_na_k[:, j], cache_na_v[:, j]))
            new_hgrn.append(s_fin)
            new_k.append(k_c)
            new_v.append(v_c)
        else:
            s_zero = jnp.zeros((xc.shape[0], 2, W_C), jnp.float32)
            m_c, s_fin = odd_mixer(h_c, w_in_odd[j], w_out_odd[j], conv_w[j], conv_b[j],
                                   rg_gate_w[j], rg_gate_b[j], rg_lambda[j], s_zero)
            m_s, _ = odd_mixer(h_s, w_in_odd[j], w_out_odd[j], conv_w[j], conv_b[j],
                               rg_gate_w[j], rg_gate_b[j], rg_lambda[j], state_rglru[:, j])
            new_rg.append(s_fin)
        xc = xc + gt_c * m_c
        xs = xs + gt_s * m_s
    y_prompt = rmsnorm(xc, final_gain)
    y_sample = rmsnorm(xs, final_gain)
    return (y_prompt, y_sample, jnp.stack(new_hgrn, axis=1), jnp.stack(new_k, axis=1),
            jnp.stack(new_v, axis=1), jnp.stack(new_rg, axis=1))
```

```python
import numpy as np
import concourse.bass as bass
import concourse.mybir as mybir
from concourse.bass_utils import run_bass_kernel_spmd

F32 = mybir.dt.float32
BF16 = mybir.dt.bfloat16
AF = mybir.ActivationFunctionType
ALU = mybir.AluOpType
ENGS = ("pe", "act", "dve", "pool", "sp")
EPS = 1e-6


class Prog:
    def __init__(self, nc, n_dma_sems=24):
        self.nc = nc
        self.ops = []
        self.lastw = {}
        self.readers = {}
        self.n_dma_sems = n_dma_sems
        self.last_eng = {}
        self.open_dmas = []
        self.pending_bar = {}

    def op(self, eng, fn, reads=(), writes=(), dma=False):
        idx = len(self.ops)
        deps = set()
        for r in reads:
            w = self.lastw.get(r)
            if w is not None:
                deps.add(w)
        for w_ in writes:
            w = self.lastw.get(w_)
            if w is not None:
                deps.add(w)
            rd = self.readers.get(w_)
            if rd:
                for k, v in rd.items():
                    if k == "dma":
                        deps.update(v)
                    else:
                        deps.add(v)
        for r in reads:
            rd = self.readers.setdefault(r, {})
            if dma:
                rd.setdefault("dma", []).append(idx)
            else:
                rd[eng] = idx
        for w_ in writes:
            self.lastw[w_] = idx
            self.readers[w_] = {}
        pb = self.pending_bar.pop(eng, None)
        if pb is not None:
            deps.add(pb)
        self.ops.append(dict(eng=eng, fn=fn, deps=deps, dma=dma))
        if dma:
            self.open_dmas.append(idx)
        else:
            self.last_eng[eng] = idx
        return idx

    def dma(self, q, out, in_, reads=(), writes=(), **kw):
        return self.op(q, lambda e: e.dma_start(out=out, in_=in_, **kw), reads, writes, dma=True)

    def barrier(self, scratch):
        deps = set(self.last_eng.values()) | set(self.open_dmas)
        idx = len(self.ops)
        self.ops.append(dict(eng="dve", fn=lambda e: e.memset(scratch, 7.0), deps=deps, dma=False))
        self.open_dmas = []
        self.last_eng = {"dve": idx}
        self.lastw = {"__bar__": idx}
        self.readers = {}
        self.bar = idx
        self.pending_bar = {e: idx for e in ENGS}
        return idx

    def finalize(self):
        nc = self.nc
        ops = self.ops
        needed = [False] * len(ops)
        for i, o in enumerate(ops):
            for d in o["deps"]:
                po = ops[d]
                if po["dma"]:
                    continue
                if po["eng"] == "pe" and o["eng"] == "pe" and not o["dma"]:
                    continue
                needed[d] = True
        cnt = {e: 0 for e in ENGS}
        for i, o in enumerate(ops):
            if o["dma"]:
                continue
            if needed[i]:
                cnt[o["eng"]] += 1
                o["seq"] = cnt[o["eng"]]
            else:
                o["seq"] = None
        half = self.n_dma_sems // 2
        kk = {True: 0, False: 0}
        dcount = [0] * self.n_dma_sems
        for i, o in enumerate(ops):
            if o["dma"]:
                sw = o["eng"] == "pool"
                k = kk[sw] + (half if sw else 0)
                kk[sw] = (kk[sw] + 1) % half
                o["dsem"] = k
                dcount[k] += 1
                o["dval"] = 16 * dcount[k]
        sems = {e: nc.alloc_semaphore(name=f"s_{e}") for e in ENGS}
        dsems = [nc.alloc_semaphore(name=f"s_dma{j}") for j in range(self.n_dma_sems)]
        by_eng = {e: [i for i, o in enumerate(ops) if o["eng"] == e] for e in ENGS}

        def emit_engine(ename, eh):
            waited = {}

            def wait(sem, val, key):
                if waited.get(key, 0) >= val:
                    return
                eh.wait_ge(sem, val)
                waited[key] = val

            for i in by_eng[ename]:
                o = ops[i]
                for d in sorted(o["deps"]):
                    po = ops[d]
                    if po["dma"]:
                        wait(dsems[po["dsem"]], po["dval"], ("d", po["dsem"]))
                    else:
                        if po["eng"] == "pe" and ename == "pe" and not o["dma"]:
                            continue
                        wait(sems[po["eng"]], po["seq"], ("e", po["eng"]))
                if o["dma"]:
                    if o["dval"] > 16:
                        wait(dsems[o["dsem"]], o["dval"] - 16, ("d", o["dsem"]))
                    ins = o["fn"](eh)
                    ins.then_inc(dsems[o["dsem"]], 16)
                else:
                    ins = o["fn"](eh)
                    if o["seq"] is not None:
                        ins.then_inc(sems[ename], 1)
            if ename == "sp":
                for j in range(self.n_dma_sems):
                    if dcount[j] > 0:
                        eh.wait_ge(dsems[j], 16 * dcount[j])
                for e in ENGS:
                    if cnt[e] > 0:
                        eh.wait_ge(sems[e], cnt[e])

        with nc.Block() as block:
            @block.tensor
            def _(e):
                emit_engine("pe", e)

            @block.scalar
            def _(e):
                emit_engine("act", e)

            @block.vector
            def _(e):
                emit_engine("dve", e)

            @block.gpsimd
            def _(e):
                emit_engine("pool", e)

            @block.sync
            def _(e):
                emit_engine("sp", e)
        return dict(n_ops=len(ops), sig=dict(cnt), ndma=sum(dcount))


class Arena:
    def __init__(self, nc, name, nfloats):
        self.t = nc.alloc_sbuf_tensor(name, [128, nfloats], F32).ap()
        self.n = nfloats
        self.off = 0
        self.name = name
        self.gen = 0

    def f32(self, n):
        n = (n + 1) // 2 * 2
        assert self.off + n <= self.n, (self.name, self.off, n, self.n)
        v = self.t[:, self.off:self.off + n]
        self.off += n
        return v

    def bf16(self, n):
        m = (n + 3) // 4 * 2
        assert self.off + m <= self.n, (self.name, self.off, m, self.n)
        v = self.t[:, self.off:self.off + m].bitcast(BF16)
        self.off += m
        return v[:, 0:n]

    def reset(self):
        self.off = 0
        self.gen += 1


V_COND, V_NG, V_BMOD, V_LBL, V_OG, V_CW, V_CB, V_GB, V_LAM, V_SRG, NV = 0, 16, 32, 80, 96, 100, 132, 140, 172, 188, 204
SEQS = [(0, 2048, 0, [(0, 2048, -1)]), (2048, 512, 1, [(0, 256, 0), (256, 256, 1)])]


def build_nc():
    nc = bass.Bass("TRN2", target_bir_lowering=False)

    def din(name, shape):
        return nc.dram_tensor(name, list(shape), F32, kind="ExternalInput").ap()

    def dout(name, shape):
        return nc.dram_tensor(name, list(shape), F32, kind="ExternalOutput").ap()

    xall = din("xall", [2560, 1024])
    st_h = din("st_h", [2, 4, 128, 128])
    ck = din("ck", [8, 256, 64])
    cv = din("cv", [8, 256, 64])
    vecs = din("vecs", [128, NV])
    rows = din("rows", [128, 3072])
    consts = din("consts", [128, 1536])
    pm = din("pm", [128, 8 * 14 * 64])
    w_mod = din("w_mod", [2, 1024, 3072])
    w_in_e = din("w_in_e", [1024, 4608])
    w_out_e = din("w_out_e", [1024, 1024])
    w_in_o = din("w_in_o", [1024, 2048])
    w_out_o = din("w_out_o", [1024, 1024])
    gwd = din("gw", [32, 128, 128])
    y_all = dout("y_all", [2560, 1024])
    o_sh = dout("o_sh", [2, 2, 4, 128, 128])
    o_k = dout("o_k", [2, 8, 256, 64])
    o_v = dout("o_v", [2, 8, 256, 64])
    o_rg = dout("o_rg", [2, 2, 1024])
    xres = nc.dram_tensor("xres", [2560, 1024], F32, kind="Internal").ap()
    gscr = nc.dram_tensor("gscr", [2, 128, 2048], F32, kind="Internal").ap()

    P = Prog(nc)
    banks = [nc.alloc_psum_tensor(f"bank{i}", [128, 512], F32).ap() for i in range(8)]
    rr = {}

    def nb(group, ids):
        j = rr.get(group, 0)
        rr[group] = j + 1
        b = ids[j % len(ids)]
        return banks[b], ("bank", b)

    def mm(out, lhsT, rhs, start, stop, r, w, tp=None):
        kw = {"tile_position": tp} if tp is not None else {}
        P.op("pe", lambda e: e.matmul(out, lhsT=lhsT, rhs=rhs, start=start, stop=stop, **kw), r, w)

    def tr(out, in_, ident, r, w):
        P.op("pe", lambda e: e.transpose(out=out, in_=in_, identity=ident), r, w)

    def act(out, in_, func, r, w, scale=None, bias=None, accum=None):
        kw = {}
        if scale is not None:
            kw["scale"] = scale
        if bias is not None:
            kw["bias"] = bias
        if accum is not None:
            kw["accum_out"] = accum
        P.op("act", lambda e: e.activation(out=out, in_=in_, func=func, **kw), r, w)

    def tt(out, in0, in1, op, r, w, eng="dve"):
        P.op(eng, lambda e: e.tensor_tensor(out=out, in0=in0, in1=in1, op=op), r, w)

    def ts(out, in0, s1, op0, r, w, s2=None, op1=None, eng="dve"):
        if op1 is None:
            P.op(eng, lambda e: e.tensor_scalar(out=out, in0=in0, scalar1=s1, scalar2=None, op0=op0), r, w)
        else:
            P.op(eng, lambda e: e.tensor_scalar(out=out, in0=in0, scalar1=s1, scalar2=s2, op0=op0, op1=op1), r, w)

    def stt(out, in0, scalar, in1, op0, op1, r, w):
        P.op("dve", lambda e: e.scalar_tensor_tensor(out=out, in0=in0, scalar=scalar, in1=in1, op0=op0, op1=op1), r, w)

    def cp(out, in_, r, w, eng="dve"):
        if eng == "act":
            P.op("act", lambda e: e.activation(out=out, in_=in_, func=AF.Copy), r, w)
        else:
            P.op(eng, lambda e: e.tensor_copy(out=out, in_=in_), r, w)

    def recip(out, in_, r, w):
        P.op("dve", lambda e: e.reciprocal(out=out, in_=in_), r, w)

    def recipf(out, in_, r, w):
        P.op("dve", lambda e: e.reciprocal_approx_fast(out=out, in_=in_), r, w)

    def scan(out, d0, d1, initial, r, w):
        P.op("dve", lambda e: e.tensor_tensor_scan(out=out, data0=d0, data1=d1, initial=initial,
                                                   op0=ALU.mult, op1=ALU.add), r, w)

    def memset(ap, val, w, eng="dve"):
        P.op(eng, lambda e: e.memset(ap, val), (), w)

    PA = Arena(nc, "persist", 27900)
    LA = Arena(nc, "phase", 25200)
    barscr = PA.f32(2)
    hT = PA.bf16(8 * 2048).rearrange("p (k t) -> p k t", k=8)
    yT = PA.bf16(8 * 2048).rearrange("p (k t) -> p k t", k=8)
    cst = PA.f32(1536)
    ident_f, ones_f, maskF_f, maskB_f = cst[:, 0:128], cst[:, 128:256], cst[:, 256:384], cst[:, 384:512]
    m01 = [cst[:, 512:1024], cst[:, 1024:1536]]
    cbf = PA.bf16(384)
    ident_b, maskF_b, maskB_b = cbf[:, 0:128], cbf[:, 128:256], cbf[:, 256:384]
    vt = PA.f32(NV)
    siluc = PA.f32(16).rearrange("p (k c) -> p k c", c=2)
    lb = PA.f32(8).rearrange("p (d h) -> p d h", d=2)
    oml = PA.f32(8).rearrange("p (d h) -> p d h", d=2)
    noml = PA.f32(8).rearrange("p (d h) -> p d h", d=2)
    cc1 = PA.f32(16).rearrange("p (d c) -> p d c", d=2)
    cc2 = PA.f32(16).rearrange("p (d c) -> p d c", d=2)
    fm = PA.f32(96).rearrange("p (l j c) -> p l j c", l=2, j=24)
    Amod = PA.f32(32).rearrange("p (l c k) -> p l c k", l=2, c=2)
    small = PA.f32(64)
    rgfin = PA.f32(32).rearrange("p (s d c) -> p s d c", s=2, d=2)
    wbuf = [PA.bf16(8 * 640).rearrange("p (k n) -> p k n", k=8) for _ in range(2)]
    xin = [PA.f32(2048).rearrange("p (j n) -> p j n", j=2) for _ in range(2)]
    wrr = [0]

    def wslot():
        j = wrr[0] % 2
        wrr[0] += 1
        return wbuf[j], ("wbuf", j)

    xrr = [0]

    def xslot():
        j = xrr[0] % 2
        xrr[0] += 1
        return xin[j], ("xin", j)

    wstate = {}
    PF = [None]

    def wissue(wid, loads):
        w, wk = wslot()
        for (src, c0, n, off) in loads:
            wload(w, wk, src, c0, n, off)
        wstate[wid] = (w, wk)

    def wget(wid, loads):
        if wid not in wstate:
            wissue(wid, loads)
        return wstate.pop(wid)

    def hgrn_loads(h):
        wsrc = w_in_e.rearrange("(k p) n -> p k n", p=128)
        return [(wsrc, c0, 128, j * 128) for j, c0 in
                enumerate([h * 128, 512 + h * 128, 1024 + h * 128, 1536 + h * 128, 2048 + h * 128])]

    def attn_loads(hp):
        wsrc = w_in_e.rearrange("(k p) n -> p k n", p=128)
        return [(wsrc, c0, 128, j * 128) for j, c0 in
                enumerate([2560 + hp * 128, 3072 + hp * 128, 3584 + hp * 128, 4096 + hp * 128])]

    def rg_loads(c):
        wsrc = w_in_o.rearrange("(k p) n -> p k n", p=128)
        return [(wsrc, c * 128, 128, 0), (wsrc, 1024 + c * 128, 128, 128)]

    def phase():
        P.barrier(barscr)
        LA.reset()
        if PF[0] is not None:
            f_ = PF[0]
            PF[0] = None
            f_()

    P.dma("sp", cst, consts, (), ["cst"])
    P.dma("sp", vt, vecs, (), ["vt"])
    cp(cbf[:, 0:128], ident_f, ["cst"], ["cbf"])
    cp(cbf[:, 128:384], cst[:, 256:512], ["cst"], ["cbf"])
    condT = vt[:, V_COND:V_COND + 16].rearrange("p (k c) -> p k c", c=2)
    act(siluc, condT, AF.Silu, ["vt"], ["siluc"])
    lbl = vt[:, V_LBL:V_LBL + 16].rearrange("p (d l h) -> p d l h", d=2, l=2)
    tt(lb, lbl[:, :, 0, :], lbl[:, :, 1, :], ALU.subtract, ["vt"], ["lb"])
    act(lb, lb, AF.Sigmoid, ["lb"], ["lb"])
    ts(oml, lb, -1.0, ALU.mult, ["lb"], ["oml"], s2=1.0, op1=ALU.add)
    ts(noml, oml, -1.0, ALU.mult, ["oml"], ["noml"])
    lam = vt[:, V_LAM:V_LAM + 16].rearrange("p (d c) -> p d c", d=2)
    act(cc1, lam, AF.Exp, ["vt"], ["cc1"], scale=-1.0)
    act(cc1, cc1, AF.Ln, ["cc1"], ["cc1"], bias=1.0)
    ts(cc2, cc1, -16.0, ALU.mult, ["cc1"], ["cc2"])
    ts(cc1, cc1, -8.0, ALU.mult, ["cc1", "cc2"], ["cc1"])
    ngain = vt[:, V_NG:V_NG + 16].rearrange("p (l k) -> p l k", l=2)
    bmod = vt[:, V_BMOD:V_BMOD + 48].rearrange("p (l j) -> p l j", l=2)

    def mod_phase(l):
        phase()
        gate_bc = LA.f32(2048).rearrange("p (c n) -> p c n", c=2)
        screp = LA.bf16(8 * 2 * 128).rearrange("p (k c m) -> p k c m", k=8, c=2)
        silub = LA.bf16(16).rearrange("p (k c) -> p k c", c=2)
        zer = LA.f32(128)
        rowt = LA.f32(1024)
        wm = [LA.bf16(8 * 512).rearrange("p (k n) -> p k n", k=8) for _ in range(3)]
        memset(zer, 0.0, ["zer"])
        cp(silub, siluc, ["siluc"], ["silub"])
        for k in range(8):
            for c in range(2):
                ts(screp[:, k, c, :], zer, siluc[:, k, c:c + 1], ALU.add, ["zer", "siluc"], ["screp"])
        P.dma("sp", rowt, rows[:, l * 1024:(l + 1) * 1024], (), ["rowt"])
        wsrc = w_mod[l].rearrange("(k p) n -> p k n", p=128)
        for cb in range(6):
            w = wm[cb % 3]
            wk = ("wm", cb % 3)
            P.dma("pool", w, wsrc[:, :, cb * 512:(cb + 1) * 512], (), [wk])
            for jj in range(4):
                j = cb * 4 + jj
                bk, bkey = nb("misc", [7, 6])
                for k in range(8):
                    mm(bk[:, 0:2], w[:, k, jj * 128:(jj + 1) * 128], silub[:, k, :], k == 0, k == 7,
                       [wk, "silub"], [bkey])
                ts(fm[:, l, j, :], bk[:, 0:2], bmod[:, l, j:j + 1], ALU.add, [bkey, "vt"], ["fm"])
            if cb >= 4:
                for c in range(2):
                    bk, bkey = nb("proj", [0, 1])
                    for k in range(8):
                        mm(bk[:, :], screp[:, k, c, :], w[:, k, :], k == 0, k == 7, [wk, "screp"], [bkey])
                    tt(gate_bc[:, c, (cb - 4) * 512:(cb - 3) * 512], bk[:, :],
                       rowt[:, (cb - 4) * 512:(cb - 3) * 512], ALU.add, [bkey, "rowt"], ["gate_bc"])
        for c in range(2):
            stt(Amod[:, l, c, :], fm[:, l, 8:16, c], 1.0, ngain[:, l, :], ALU.add, ALU.mult, ["fm", "vt"], ["Amod"])
        P.dma("sp", gscr[l], gate_bc.rearrange("p c n -> p (c n)"), ["gate_bc"], [("gscr", l)])

    def norm_phase(l, seq):
        tok0, T, cond, segs = seq
        pidx = segs[0][2]
        phase()
        NT = T // 128
        xn = [LA.bf16(1024) for _ in range(2)]
        junk = LA.f32(1024)
        stats = LA.f32(2 * NT)
        src = xall if l == 0 else xres
        fr = {}
        xp = {}

        def front(i):
            if i % 2 == 0:
                x2, xk = xslot()
                P.dma("sp", x2,
                      src[tok0 + i * 128: tok0 + (i + 2) * 128, :].rearrange("(j p) n -> p j n", p=128), (), [xk])
                xp[i // 2] = (x2, xk)
            x2, xk = xp[i // 2]
            xt = x2[:, i % 2, :]
            ssq = stats[:, 2 * i:2 * i + 1]
            rstd = stats[:, 2 * i + 1:2 * i + 2]
            sk = ("stats", i)
            act(junk, xt, AF.Square, [xk], [sk], accum=ssq)
            act(ssq, ssq, AF.Sqrt, [sk], [sk], scale=1.0 / 1024, bias=EPS)
            recip(rstd, ssq, [sk], [sk])
            xnb = xn[i % 2]
            xnk = ("xn", i % 2)
            ts(xnb, xt, rstd, ALU.mult, [xk, sk], [xnk])
            bk, bkey = nb("tp", [6, 7])
            bkb = bk.bitcast(BF16)
            for k in range(8):
                tr(bkb[:, k * 128:(k + 1) * 128], xnb[:, k * 128:(k + 1) * 128], ident_b, [xnk, "cbf"], [bkey])
            fr[i] = (bkb, bkey)

        def back(i):
            bkb, bkey = fr[i]
            for k in range(8):
                dst = hT[:, k, i * 128:(i + 1) * 128]
                if bkey[1] == 6:
                    act(dst, bkb[:, k * 128:(k + 1) * 128], AF.Identity, [bkey, "Amod", "fm"], [("hTt", i, k)],
                        scale=Amod[:, l, cond, k:k + 1], bias=fm[:, l, k, cond:cond + 1])
                else:
                    ts(dst, bkb[:, k * 128:(k + 1) * 128], Amod[:, l, cond, k:k + 1], ALU.mult,
                       [bkey, "Amod", "fm"], [("hTt", i, k)], s2=fm[:, l, k, cond:cond + 1], op1=ALU.add)

        front(0)
        for i in range(NT):
            if i + 1 < NT:
                front(i + 1)
            back(i)

    def proj_fm(bk, bkey, w, wk, c0, t0, n):
        for k in range(8):
            mm(bk[:, 0:n], w[:, k, c0:c0 + 128], hT[:, k, t0:t0 + n], k == 0, k == 7, [wk, "hT"], [bkey])

    def wload(dst, dkey, wsrc, c0, n, off=0):
        P.dma("pool", dst[:, :, off:off + n], wsrc[:, :, c0:c0 + n], (), [dkey])

    def hgrn_head(seq, h):
        tok0, T, cond, segs = seq
        pidx = segs[0][2]
        phase()
        BS = min(512, T)
        NBk = T // BS
        NT = T // 128
        wsrc = w_in_e.rearrange("(k p) n -> p k n", p=128)
        w, wk = wget(("hg", tok0, h), hgrn_loads(h))
        qdec = [LA.bf16(T) for _ in range(2)]
        kinv = [LA.bf16(T) for _ in range(2)]
        kd = [LA.bf16(T).rearrange("p (i d) -> p i d", d=128) for _ in range(2)]
        vsb = LA.bf16(T).rearrange("p (i d) -> p i d", d=128)
        vx = LA.bf16(4 * T).rearrange("p (i c d) -> p i c d", c=4, d=128)
        oT = LA.f32(T)
        dec = [LA.f32(64) for _ in range(2)]
        S32 = [[LA.f32(128) for _ in range(2)] for _ in range(2)]
        Sbf = [[LA.bf16(128) for _ in range(2)] for _ in range(2)]
        attb = [[LA.bf16(128) for _ in range(2)] for _ in range(2)]
        HB = min(1024, T)
        NHB = T // HB
        sets = [{n_: LA.f32(HB) for n_ in "AKBE"} for _ in range(2)]
        VT = oT[:, 0:T // 2].bitcast(BF16)
        for b in range(NBk):
            sl = slice(b * BS, (b + 1) * BS)
            bk, bkey = nb("proj", [0, 1])
            proj_fm(bk, bkey, w, wk, 384, b * BS, BS)
            cp(VT[:, sl], bk[:, 0:BS], [bkey], ["VT"], eng="act")
        for g0 in range(0, NT, 4):
            ng = min(4, NT - g0)
            bt, btkey = nb("tp", [6, 7])
            btb = bt.bitcast(BF16)
            for j in range(ng):
                tr(btb[:, j * 128:(j + 1) * 128], VT[:, (g0 + j) * 128:(g0 + j + 1) * 128], ident_b, ["VT", "cbf"], [btkey])
            cp(vsb[:, g0:g0 + ng, :], btb[:, 0:ng * 128].rearrange("p (j d) -> p j d", d=128), [btkey],
               [("vsb", g0 + j) for j in range(ng)], eng="act")
            for j in range(ng):
                i = g0 + j
                P.op("pool", lambda e, o_=vx[:, i, :, :], a_=vsb[:, i, :].unsqueeze(1).to_broadcast([128, 4, 128]),
                     b_=maskF_f[:, 31::32].unsqueeze(2).to_broadcast([128, 4, 128]):
                     e.tensor_tensor(out=o_, in0=a_, in1=b_, op=ALU.mult), [("vsb", i), "cst"], [("vx", i)])
        iters = [(d, hb) for d in range(2) for hb in range(NHB)]

        def stage1(n):
            d, hb = iters[n]
            si = n % 2
            st = sets[si]
            A, Kt, Bt, Et = (st[n_] for n_ in "AKBE")
            kA, kK, kB, kE = (("t" + n_, si) for n_ in "AKBE")
            t0 = hb * HB
            nbl = HB // BS
            for b in range(nbl):
                bk, bkey = nb("proj", [0, 1])
                proj_fm(bk, bkey, w, wk, 128 * (1 + d), t0 + b * BS, BS)
                act(A[:, b * BS:(b + 1) * BS], bk[:, 0:BS], AF.Sigmoid, [bkey], [kA])
            ts(Kt, A, noml[:, d, h:h + 1], ALU.mult, [kA, "noml", "oml"], [kK], s2=oml[:, d, h:h + 1], op1=ALU.add)
            act(A, A, AF.Ln, [kA, "oml", "lb"], [kA], scale=oml[:, d, h:h + 1], bias=lb[:, d, h:h + 1])
            for b in range(nbl):
                bsl = slice(b * BS, (b + 1) * BS)
                if d == 0:
                    scan(Bt[:, bsl], m01[0][:, 0:BS], A[:, bsl], 0.0, [kA, "cst"], [kB])
                else:
                    scan(Bt[:, bsl][:, ::-1], m01[1][:, 0:BS][:, ::-1], A[:, bsl][:, ::-1], 0.0, [kA, "cst"], [kB])
            act(Et, Bt, AF.Exp, [kB], [kE])
            act(Bt, Bt, AF.Exp, [kB, kE], [kB], scale=-1.0)

        def stage2(n):
            d, hb = iters[n]
            si = n % 2
            st = sets[si]
            A, Kt, Bt, Et = (st[n_] for n_ in "AKBE")
            kA, kK, kB, kE = (("t" + n_, si) for n_ in "AKBE")
            t0 = hb * HB
            nbl = HB // BS
            for b in range(nbl):
                bq, bqkey = nb("proj", [0, 1])
                proj_fm(bq, bqkey, w, wk, 0, t0 + b * BS, BS)
                tt(qdec[d][:, t0 + b * BS:t0 + (b + 1) * BS], bq[:, 0:BS], Et[:, b * BS:(b + 1) * BS], ALU.mult,
                   [bqkey, kE], [("qdec", d)])
            tt(Kt, Kt, Bt, ALU.mult, [kK, kB], [kK])
            cp(kinv[d][:, t0:t0 + HB], Kt, [kK], [("kinv", d)], eng="act")
            nch = HB // 32
            dsl = dec[d][:, t0 // 32:t0 // 32 + nch]
            cp(dsl, Et.rearrange("p (c i) -> p c i", i=32)[:, :, 31 if d == 0 else 0], [kE], [("dec", d)])
            kdT = A[:, 0:HB // 2].bitcast(BF16)
            tt(kdT.rearrange("p (c i) -> p c i", i=32), Kt.rearrange("p (c i) -> p c i", i=32),
               dsl.unsqueeze(2).to_broadcast([128, nch, 32]), ALU.mult, [kK, ("dec", d), kB], [kA])
            for b in range(nbl):
                bt, btkey = nb("tp", [6, 7])
                btb = bt.bitcast(BF16)
                for j in range(BS // 128):
                    tr(btb[:, j * 128:(j + 1) * 128], kdT[:, b * BS + j * 128:b * BS + (j + 1) * 128], ident_b,
                       [kA, "cbf"], [btkey])
                i0 = (t0 + b * BS) // 128
                cp(kd[d][:, i0:i0 + BS // 128, :],
                   btb[:, 0:BS].rearrange("p (j d) -> p j d", d=128), [btkey], [("kd", d)], eng="act")

        stage1(0)
        for n in range(len(iters)):
            if n + 1 < len(iters):
                stage1(n + 1)
            stage2(n)
        for d in range(2):
            if pidx < 0:
                P.dma("sp", S32[d][0], st_h[d, h], (), [("S32", d, 0)])
            else:
                memset(S32[d][0], 0.0, [("S32", d, 0)])
            cp(Sbf[d][0], S32[d][0], [("S32", d, 0)], [("Sbf", d, 0)], eng="act")
        seg_first = [{t0 // 128: sp_ for (t0, Ts, sp_) in segs}, {(t0 + Ts) // 128 - 1: sp_ for (t0, Ts, sp_) in segs}]
        seg_last = [{(t0 + Ts) // 128 - 1: sp_ for (t0, Ts, sp_) in segs}, {t0 // 128: sp_ for (t0, Ts, sp_) in segs}]
        fr = {}
        KB = [0, 1, 5, 6]
        kvs = [[LA.f32(512).rearrange("p (c d) -> p c d", d=128) for _ in range(2)] for _ in range(2)]

        def front(step):
            for d in range(2):
                i = step if d == 0 else NT - 1 - step
                tsl = slice(i * 128, (i + 1) * 128)
                ba, bakey = nb("att", [2])
                mm(ba[:, 0:128], kinv[d][:, tsl], qdec[d][:, tsl], True, True, [("kinv", d), ("qdec", d)], [bakey])
                ab = attb[d][step % 2]
                abk = ("attb", d, step % 2)
                tt(ab, ba[:, 0:128], maskF_f if d == 0 else maskB_f, ALU.mult, [bakey, "cst"], [abk])
                ks3 = kvs[d][step % 2]
                ksb = [ks3[:, c, :] for c in range(4)]
                kskey = [("kvs", d, step % 2)] * 4
                bkv_, bkvk = nb("kv", [5, 6])
                mm(bkv_[:, :], kd[d][:, i, :], vx[:, i, :, :].rearrange("p c d -> p (c d)"), True, True,
                   [("kd", d), ("vx", i)], [bkvk])
                cp(ks3, bkv_[:, :].rearrange("p (c d) -> p c d", d=128), [bkvk], [kskey[0]], eng="act")
                fr[(step, d)] = (i, tsl, ab, abk, ksb, kskey)

        sbi = [0, 0]

        def back(step):
            bos = []
            for d in range(2):
                i, tsl, ab, abk, bkv, bkvkey = fr[(step, d)]
                bo, bokey = nb("o", [3, 4])
                mm(bo[:, 0:128], vsb[:, i, :], ab, True, False, [("vsb", i), abk], [bokey])
                bos.append((bo, bokey))
            for d in range(2):
                i = fr[(step, d)][0]
                if step > 0 and i in seg_first[d]:
                    cur = sbi[d] % 2
                    memset(S32[d][cur], 0.0, [("S32", d, cur)])
                    memset(Sbf[d][cur], 0.0, [("Sbf", d, cur)])
            for n_ in range(4):
                for d in range(2):
                    i, tsl, ab, abk, bkv, bkvkey = fr[(step, d)]
                    bo, bokey = bos[d]
                    c = n_ if d == 0 else 3 - n_
                    cur = sbi[d] % 2
                    nxt = 1 - cur
                    csl = slice(i * 128 + c * 32, i * 128 + c * 32 + 32)
                    mm(bo[:, c * 32:(c + 1) * 32], Sbf[d][cur], qdec[d][:, csl], False, n_ == 3,
                       [("Sbf", d, cur), ("qdec", d)], [bokey])
                    dcs = dec[d][:, i * 4 + c:i * 4 + c + 1]
                    stt(S32[d][nxt], S32[d][cur], dcs, bkv[c], ALU.mult, ALU.add,
                        [("S32", d, cur), ("dec", d), bkvkey[c]], [("S32", d, nxt)])
                    cp(Sbf[d][nxt], S32[d][nxt], [("S32", d, nxt)], [("Sbf", d, nxt)], eng="act")
                    sbi[d] += 1
            for d in range(2):
                i = fr[(step, d)][0]
                if i in seg_last[d] and seg_last[d][i] >= 0:
                    fin = sbi[d] % 2
                    sp_ = seg_last[d][i]
                    P.dma("sp", o_sh[sp_, d, h], S32[d][fin], [("S32", d, fin)], [("o_sh", sp_, d, h)])
            for d in range(2):
                i, tsl, ab, abk, bkv, bkvkey = fr[(step, d)]
                bo, bokey = bos[d]
                first = (i <= NT - 1 - i) if d == 0 else (NT - 1 - i < i)
                if first:
                    cp(oT[:, tsl], bo[:, 0:128], [bokey], [("oT", i)], eng="act")
                else:
                    tt(oT[:, tsl], bo[:, 0:128], oT[:, tsl], ALU.add, [bokey, ("oT", i)], [("oT", i)])

        front(0)
        for step in range(NT):
            if step + 1 < NT:
                front(step + 1)
            back(step)
        gs = kinv[0]
        for b in range(NBk):
            bk, bkey = nb("proj", [0, 1])
            proj_fm(bk, bkey, w, wk, 512, b * BS, BS)
            act(gs[:, b * BS:(b + 1) * BS], bk[:, 0:BS], AF.Silu, [bkey], [("kinv", 0)])
        ogain = vt[:, V_OG + h:V_OG + h + 1]
        for b in range(NBk):
            sl = slice(b * BS, (b + 1) * BS)
            st = sets[b % 2]
            sq, rs_, t1 = st["A"][:, 0:BS], st["K"][:, 0:BS], st["B"][:, 0:BS]
            kA, kK, kB = (("t" + n_, b % 2) for n_ in "AKB")
            rs2 = st["E"][:, 0:BS]
            kE = ("tE", b % 2)
            okeys = [("oT", i) for i in range(b * BS // 128, (b + 1) * BS // 128)]
            act(sq, oT[:, sl], AF.Square, okeys, [kA])
            bk, bkey = nb("misc", [7])
            mm(bk[:, 0:BS], ones_f, sq, True, True, [kA, "cst"], [bkey])
            act(rs_, bk[:, 0:BS], AF.Ln, [bkey], [kK], scale=1.0 / 128, bias=EPS)
            act(rs2, rs_, AF.Exp, [kK], [kE], scale=-0.5)
            tt(t1, oT[:, sl], rs2, ALU.mult, okeys + [kE], [kB])
            stt(yT[:, h, sl], t1, ogain, gs[:, sl], ALU.mult, ALU.mult, [kB, "vt", ("kinv", 0)], ["yT"])

    def attn_pair(seq, hp):
        tok0, T, cond, segs = seq
        pidx = segs[0][2]
        phase()
        BS = min(512, T)
        NBk = T // BS
        NT = T // 128
        sample = pidx < 0
        w, wk = wget(("at", tok0, hp), attn_loads(hp))
        QT = LA.bf16(T)
        KT = LA.bf16(T)
        gs = LA.bf16(T)
        Ve = LA.bf16(NT * 256).rearrange("p (i h m) -> p i h m", h=2, m=128)
        Vo = LA.bf16(NT * 256).rearrange("p (i h m) -> p i h m", h=2, m=128) if sample else None
        memset(Ve, 1.0, [("Ve", i_, h_) for i_ in range(NT) for h_ in range(2)], eng="pool")
        if sample:
            memset(Vo, 1.0, [("Vo", i_, h_) for i_ in range(NT) for h_ in range(2)], eng="pool")
        if sample:
            kc = LA.bf16(2 * 128).rearrange("p (c m) -> p c m", c=2)
            vc = LA.bf16(2 * 128).rearrange("p (c m) -> p c m", c=2)
            Vc = LA.bf16(2 * 256).rearrange("p (c h m) -> p c h m", c=2, h=2)
            KcT = LA.bf16(256)
            pmh = [LA.f32(14 * 64).rearrange("p (d q) -> p d q", q=64) for _ in range(2)]
            memset(Vc, 1.0, [("Vc", c_, h_) for c_ in range(2) for h_ in range(2)], eng="pool")
            for hh in range(2):
                P.dma("pool", kc[:, :, hh * 64:(hh + 1) * 64], ck[2 * hp + hh].rearrange("(c p) d -> p c d", p=128), (), ["kc"])
                P.dma("pool", vc[:, :, hh * 64:(hh + 1) * 64], cv[2 * hp + hh].rearrange("(c p) d -> p c d", p=128), (), ["vc"])
                P.dma("sp", pmh[hh], pm[:, (2 * hp + hh) * 896:(2 * hp + hh + 1) * 896].rearrange("p (d q) -> p d q", q=64), (), [("pmh", hh)])
        for b in range(NBk):
            sl = slice(b * BS, (b + 1) * BS)
            bk, bkey = nb("proj", [0, 1])
            proj_fm(bk, bkey, w, wk, 0, b * BS, BS)
            act(QT[:, sl], bk[:, 0:BS], AF.Identity, [bkey], ["QT"], scale=0.125)
            bk, bkey = nb("proj", [0, 1])
            proj_fm(bk, bkey, w, wk, 128, b * BS, BS)
            cp(KT[:, sl], bk[:, 0:BS], [bkey], ["KT"])
            bk, bkey = nb("proj", [0, 1])
            proj_fm(bk, bkey, w, wk, 384, b * BS, BS)
            act(gs[:, sl], bk[:, 0:BS], AF.Silu, [bkey], ["gs"])

        VT = LA.bf16(T)
        for b in range(NBk):
            sl = slice(b * BS, (b + 1) * BS)
            bk, bkey = nb("proj", [0, 1])
            proj_fm(bk, bkey, w, wk, 256, b * BS, BS)
            if bkey[1] == 0:
                act(VT[:, sl], bk[:, 0:BS], AF.Identity, [bkey], ["VT"])
            else:
                cp(VT[:, sl], bk[:, 0:BS], [bkey], ["VT"])

        def vtrans(dst, dname, ntile, off):
            for g0 in range(0, ntile, 4):
                ng = min(4, ntile - g0)
                bt, btkey = nb("tp", [6, 7])
                btb = bt.bitcast(BF16)
                for j in range(ng):
                    t0 = off + (g0 + j) * 128
                    tr(btb[:, j * 128:(j + 1) * 128], VT[:, t0:t0 + 128], ident_b, ["VT", "cbf"], [btkey])
                src3 = btb[:, 0:ng * 128].rearrange("p (j m) -> p j m", m=128)
                eng_ = "act" if btkey[1] == 6 else "dve"
                cp(dst[:, g0:g0 + ng, 0, 0:64], src3[:, :, 0:64], [btkey], [(dname, g0 + j, 0) for j in range(ng)], eng=eng_)
                cp(dst[:, g0:g0 + ng, 1, 64:128], src3[:, :, 64:128], [btkey], [(dname, g0 + j, 1) for j in range(ng)], eng=eng_)

        vtrans(Ve, "Ve", NT, 0)
        if sample:
            vtrans(Vo, "Vo", NT - 1, 64)
        if sample:
            bt, btkey = nb("tp", [6])
            btb = bt.bitcast(BF16)
            for c in range(2):
                tr(btb[:, c * 128:(c + 1) * 128], kc[:, c, :], ident_b, ["kc", "cbf"], [btkey])
            cp(KcT, btb[:, 0:256], [btkey], ["KcT"])
            for c in range(2):
                cp(Vc[:, c, 0, 0:64], vc[:, c, 0:64], ["vc"], [("Vc", c, 0)])
                cp(Vc[:, c, 1, 64:128], vc[:, c, 64:128], ["vc"], [("Vc", c, 1)], eng="act")
            KsrcT, Vsrc, kkeys = KcT, Vc, ["KcT", "Vc"]
        else:
            KsrcT, Vsrc, kkeys = KT, Ve, ["KT", "Ve"]
        Pc = [LA.bf16(2 * 512).rearrange("p (c q) -> p c q", c=2) for _ in range(2)]
        NS = 3
        sbt = [LA.f32(256) for _ in range(NS)]
        plt = [LA.bf16(256).rearrange("p (a q) -> p a q", q=64) for _ in range(NS)]
        rden = [LA.f32(512) for _ in range(2)]
        tnum = [LA.f32(512) for _ in range(2)]
        if sample:
            groups = [(hh, slice(G * BS, (G + 1) * BS), BS, 0, G) for hh in range(2) for G in range(NBk)]
        else:
            groups = [(hh, slice(t0, t0 + Ts), Ts, t0 // 128, 0) for hh in range(2) for (t0, Ts, sp_) in segs]
        gstate = {}

        def ctx_front(gi):
            hh, qsl, nq, kb, G = groups[gi]
            hs = slice(hh * 64, (hh + 1) * 64)
            pcb = Pc[gi % 2]
            pck = ("Pc", gi % 2)
            for c in range(2):
                bc, bckey = nb("ctx", [0, 1])
                mm(bc[:, 0:nq], KsrcT[hs, (kb + c) * 128:(kb + c + 1) * 128], QT[hs, qsl], True, True, [kkeys[0], "QT"], [bckey])
                act(pcb[:, c, 0:nq], bc[:, 0:nq], AF.Exp, [bckey], [pck])
            gstate[gi] = (pcb, pck)

        def ctx_pv(gi):
            hh, qsl, nq, kb, G = groups[gi]
            pcb, pck = gstate[gi]
            bo, bokey = nb("o", [3, 4])
            for c in range(2):
                vk = ("Vc", c, hh) if sample else ("Ve", kb + c, hh)
                mm(bo[:, 0:nq], Vsrc[:, kb + c, hh, :], pcb[:, c, 0:nq], c == 0, (not sample) and c == 1, [vk, pck], [bokey])
            gstate[gi] = (pcb, pck, bo, bokey)

        def finish(gi):
            hh, qsl, nq, kb, G = groups[gi]
            hs = slice(hh * 64, (hh + 1) * 64)
            ds = slice((1 - hh) * 64, (2 - hh) * 64)
            pcb, pck, bo, bokey = gstate[gi]
            rd, tn = rden[gi % 2], tnum[gi % 2]
            act(tn[hs, 0:nq], bo[ds, 0:nq], AF.Ln, [bokey], [("tnum", gi % 2)])
            act(rd[hs, 0:nq], tn[hs, 0:nq], AF.Exp, [("tnum", gi % 2)], [("rden", gi % 2)], scale=-1.0)
            tt(tn[hs, 0:nq], bo[hs, 0:nq], rd[hs, 0:nq], ALU.mult, [bokey, ("rden", gi % 2)], [("tnum", gi % 2)])
            tt(yT[hs, 4 + hp, qsl], tn[hs, 0:nq], gs[hs, qsl], ALU.mult, [("tnum", gi % 2), "gs"], [("yTa", hh)])

        rowl = [(gi, rl) for gi in range(len(groups)) for rl in range(8)] if sample else []
        rstate = {}

        def row_front(n):
            gi, rl = rowl[n]
            hh, qsl_, nq_, kb_, G = groups[gi]
            hs = slice(hh * 64, (hh + 1) * 64)
            r = G * 8 + rl
            rs = min(max(r - 4, 0), 24)
            j = n % NS
            bs_, bskey = nb("s", [2, 5, 7])
            for i in range(4):
                k0 = (rs + 2 * i) * 64
                mm(bs_[:, i * 64:(i + 1) * 64], KT[hs, k0:k0 + 128], QT[hs, r * 64:(r + 1) * 64], True, True,
                   ["KT", "QT"], [bskey])
            idx0 = rs - r + 7
            tt(sbt[j].rearrange("p (a q) -> p a q", q=64), bs_[:, 0:256].rearrange("p (a q) -> p a q", q=64),
               pmh[hh][:, idx0:idx0 + 7:2, :], ALU.add, [bskey, ("pmh", hh)], [("sbt", j)])
            act(plt[j], sbt[j].rearrange("p (a q) -> p a q", q=64), AF.Exp, [("sbt", j)], [("plt", j)])
            rstate[n] = (rs, j)

        def row_back(n):
            gi, rl = rowl[n]
            hh, qsl_, nq_, kb_, G = groups[gi]
            rs, j = rstate[n]
            pcb, pck, bo, bokey = gstate[gi]
            for i in range(4):
                kr = rs + 2 * i
                Vt = Ve[:, kr // 2, hh, :] if kr % 2 == 0 else Vo[:, (kr - 1) // 2, hh, :]
                vk = ("Ve", kr // 2, hh) if kr % 2 == 0 else ("Vo", (kr - 1) // 2, hh)
                mm(bo[:, rl * 64:(rl + 1) * 64], Vt, plt[j][:, i, :], False, (rl == 7 and i == 3),
                   [vk, ("plt", j)], [bokey])

        if sample:
            LOOK = 2
            ctx_front(0)
            for n in range(min(LOOK, len(rowl))):
                row_front(n)
            for n in range(len(rowl)):
                gi, rl = rowl[n]
                if rl == 0:
                    ctx_pv(gi)
                    if gi + 1 < len(groups):
                        ctx_front(gi + 1)
                if n + LOOK < len(rowl):
                    row_front(n + LOOK)
                row_back(n)
                if rl == 7:
                    finish(gi)
        else:
            for gi in range(len(groups)):
                ctx_front(gi)
                ctx_pv(gi)
                finish(gi)

    def kv_out(seq):
        tok0, T, cond, segs = seq
        pidx = segs[0][2]
        phase()
        wsrc = w_in_e.rearrange("(k p) n -> p k n", p=128)
        stage = [LA.f32(512) for _ in range(2)]
        n = 0
        for which, c0, dst in ((0, 3072, o_k), (1, 3584, o_v)):
            w, wk = wslot()
            wload(w, wk, wsrc, c0, 512)
            for i in range(T // 128):
                bk, bkey = nb("proj", [0, 1])
                for k in range(8):
                    mm(bk[:, :], hT[:, k, i * 128:(i + 1) * 128], w[:, k, 0:512], k == 0, k == 7, [wk, "hT"], [bkey])
                st = stage[n % 2]
                sk = ("stage", n % 2)
                n += 1
                cp(st, bk[:, :], [bkey], [sk], eng="act")
                sp_, il = [(sp2, i - t0 // 128) for (t0, Ts, sp2) in segs if t0 // 128 <= i < (t0 + Ts) // 128][0]
                P.dma("sp", dst[sp_, :, il * 128:(il + 1) * 128, :].rearrange("h t d -> t h d"),
                      st.rearrange("p (h d) -> p h d", d=64), [sk], [("okv", which, sp_, il)])

    def rg_frontA(seq, c, gwb, L, par, gate=True):
        tok0, T, cond, segs = seq
        BS = min(512, T)
        NBk = T // BS
        kx = L["kx"]
        w, wk = wget(("rg", tok0, c), rg_loads(c))
        if c + 1 < 8:
            wissue(("rg", tok0, c + 1), rg_loads(c + 1))
        L.setdefault("w", {})[c] = (w, wk)
        xb, xc, xcb, gs = L["xb"], L["xc"][par], L["xcb"][par], L["gs"][par]
        kxc, kxcb, kgs = ("xc", kx, par), ("xcb", kx, par), ("gs", kx, par)
        for b in range(NBk):
            sl = slice(b * BS, (b + 1) * BS)
            bk, bkey = nb("projx", [2, 3])
            proj_fm(bk, bkey, w, wk, 0, b * BS, BS)
            cp(xb[:, sl], bk[:, 0:BS], [bkey], [("xb", kx)])
        cw = vt[:, V_CW:V_CW + 32].rearrange("p (j c) -> p j c", j=4)
        cb_ = vt[:, V_CB + c:V_CB + c + 1]
        ts(xc[:, 0:T], xb[:, 0:T], cw[:, 2, c:c + 1], ALU.mult, [("xb", kx), "vt"], [kxc], s2=cb_, op1=ALU.add)
        for (t0, Ts, sp_) in segs:
            e_ = t0 + Ts
            stt(xc[:, t0 + 2:e_], xb[:, t0:e_ - 2], cw[:, 0, c:c + 1], xc[:, t0 + 2:e_], ALU.mult, ALU.add, [("xb", kx), kxc, "vt"], [kxc])
            stt(xc[:, t0 + 1:e_], xb[:, t0:e_ - 1], cw[:, 1, c:c + 1], xc[:, t0 + 1:e_], ALU.mult, ALU.add, [("xb", kx), kxc, "vt"], [kxc])
            stt(xc[:, t0:e_ - 1], xb[:, t0 + 1:e_], cw[:, 3, c:c + 1], xc[:, t0:e_ - 1], ALU.mult, ALU.add, [("xb", kx), kxc, "vt"], [kxc])
        cp(xcb[:, 0:T], xc[:, 0:T], [kxc], [kxcb], eng="pool" if L["pipe"] else "act")
        if gate:
            rg_gate(seq, c, L, par)

    def rg_gate(seq, c, L, par):
        tok0, T, cond, segs = seq
        BS = min(512, T)
        w, wk = L["w"][c]
        gs = L["gs"][par]
        for b in range(T // BS):
            sl = slice(b * BS, (b + 1) * BS)
            bk, bkey = nb("projx", [2, 3])
            proj_fm(bk, bkey, w, wk, 128, b * BS, BS)
            act(gs[:, sl], bk[:, 0:BS], AF.Silu, [bkey], [("gs", L["kx"], par)])

    def rg_frontB(seq, c, gwb, L, par, part="all"):
        tok0, T, cond, segs = seq
        BS = min(512, T)
        NBk = T // BS
        kx = L["kx"]
        xcb = L["xcb"][par]
        kxcb = ("xcb", kx, par)
        gb = vt[:, V_GB:V_GB + 32].rearrange("p (d g c) -> p d g c", d=2, g=2)

        def sig(d):
            aT, uT = L["aT"][d], L["uT"][d]
            for b in range(NBk):
                sl = slice(b * BS, (b + 1) * BS)
                bk, bkey = nb("proj", [0, 1])
                mm(bk[:, 0:BS], gwb[:, (d * 2 + 0) * 8 + c, :], xcb[:, sl], True, True, ["gwb", kxcb], [bkey])
                act(aT[:, sl], bk[:, 0:BS], AF.Sigmoid, [bkey, "vt"], [("aT", d, kx)], bias=gb[:, d, 0, c:c + 1])
                bk, bkey = nb("proj", [0, 1])
                mm(bk[:, 0:BS], gwb[:, (d * 2 + 1) * 8 + c, :], xcb[:, sl], True, True, ["gwb", kxcb], [bkey])
                act(uT[:, sl], bk[:, 0:BS], AF.Sigmoid, [bkey, "vt"], [("uT", d, kx)], bias=gb[:, d, 1, c:c + 1])

        def expo(d):
            aT, a2 = L["aT"][d], L["a2"][d]
            act(a2[:, 0:T], aT[:, 0:T], AF.Exp, [("aT", d, kx), "cc2"], [L["a2key"][d]], scale=cc2[:, d, c:c + 1])
            act(aT[:, 0:T], aT[:, 0:T], AF.Exp, [("aT", d, kx), "cc1"], [("aT", d, kx)], scale=cc1[:, d, c:c + 1])

        def sqr(d):
            a2 = L["a2"][d]
            act(a2[:, 0:T], a2[:, 0:T], AF.Sqrt, [L["a2key"][d]], [L["a2key"][d]], scale=-1.0, bias=1.0)

        if L["split"]:
            sig(0); sig(1); expo(0); expo(1); sqr(0); sqr(1)
        else:
            if part in ("all", "B1"):
                sig(0); sig(1)
            if part in ("all", "B2"):
                expo(0); sqr(0)
                rg_tail(seq, c, L, 0, par)
                expo(1); sqr(1)
                rg_tail(seq, c, L, 1, par)

    def rg_tail(seq, c, L, d, par):
        tok0, T, cond, segs = seq
        kx = L["kx"]
        sample = segs[0][2] < 0
        xc, hs = L["xc"][par], L["hs"]
        kxc = ("xc", kx, par)
        aT, uT, a2 = L["aT"][d], L["uT"][d], L["a2"][d]
        srg = vt[:, V_SRG:V_SRG + 16].rearrange("p (d c) -> p d c", d=2)
        H = T // 2
        for hi, (hf, eng_) in enumerate(((slice(0, H), "dve"), (slice(H, T), "pool"))):
            tt(uT[:, hf], uT[:, hf], a2[:, hf], ALU.mult, [("uT", d, kx), L["a2key"][d]], [("uT", d, kx, hi)], eng=eng_)
            tt(uT[:, hf], uT[:, hf], xc[:, hf], ALU.mult, [("uT", d, kx, hi), kxc], [("uT", d, kx, hi)], eng=eng_)
        init = srg[:, d, c:c + 1] if sample else 0.0
        ukeys = [("uT", d, kx, 0), ("uT", d, kx, 1), ("uT", d, kx)]
        for (t0, Ts, sp_) in segs:
            e_ = t0 + Ts
            if d == 0:
                scan(hs[0][:, t0:e_], aT[:, t0:e_], uT[:, t0:e_], init, [("aT", 0, kx), "vt"] + ukeys, [("hs", 0, kx)])
                if not sample:
                    cp(rgfin[:, sp_, 0, c:c + 1], hs[0][:, e_ - 1:e_], [("hs", 0, kx)], ["rgfin"])
            else:
                scan(hs[1][:, t0:e_][:, ::-1], aT[:, t0:e_][:, ::-1], uT[:, t0:e_][:, ::-1], init,
                     [("aT", 1, kx), "vt"] + ukeys, [L["hs1key"]])
                if not sample:
                    cp(rgfin[:, sp_, 1, c:c + 1], hs[1][:, t0:t0 + 1], [L["hs1key"]], ["rgfin"])

    def rg_back(seq, c, L, par):
        tok0, T, cond, segs = seq
        kx = L["kx"]
        hs, gs = L["hs"], L["gs"][par]
        kgs = ("gs", kx, par)
        if L["split"]:
            for d in range(2):
                rg_tail(seq, c, L, d, par)
        H = T // 2
        for hf, eng_ in ((slice(0, H), "dve"), (slice(H, T), "pool")):
            tt(hs[0][:, hf], hs[0][:, hf], hs[1][:, hf], ALU.add, [("hs", 0, kx), L["hs1key"]], [("hsum", kx, eng_)], eng=eng_)
            tt(yT[:, c, hf], hs[0][:, hf], gs[:, hf], ALU.mult, [("hsum", kx, eng_), kgs], [("yTc", c, eng_)], eng=eng_)

    def rglru_layer(seq):
        tok0, T, cond, segs = seq
        phase()
        small_ = T <= 512
        gwb = LA.bf16(32 * 128).rearrange("p (n j) -> p n j", j=128)
        P.dma("pool", gwb, gwd.rearrange("n i j -> i n j"), (), ["gwb"])
        if small_:
            sets = []
            for j in range(2):
                L = {"kx": j, "split": True, "pipe": False, "hs1key": ("hs", 1, j)}
                L["xb"] = LA.f32(T)
                L["xc"] = [LA.f32(T)]
                L["aT"] = [LA.f32(T), LA.f32(T)]
                L["uT"] = [LA.f32(T), LA.f32(T)]
                L["a2"] = [LA.f32(T), LA.f32(T)]
                L["a2key"] = [("a2", 0, j), ("a2", 1, j)]
                L["hs"] = [LA.f32(T), LA.f32(T)]
                L["xcb"] = [LA.bf16(T)]
                L["gs"] = [LA.bf16(T)]
                sets.append(L)
            rg_frontA(seq, 0, gwb, sets[0], 0)
            rg_frontB(seq, 0, gwb, sets[0], 0)
            for c in range(8):
                if c + 1 < 8:
                    rg_frontA(seq, c + 1, gwb, sets[(c + 1) % 2], 0)
                    rg_frontB(seq, c + 1, gwb, sets[(c + 1) % 2], 0)
                rg_back(seq, c, sets[c % 2], 0)
        else:
            L = {"kx": 0, "split": False, "pipe": True, "hs1key": ("a2", 0)}
            L["xb"] = LA.f32(T)
            L["xc"] = [LA.f32(T), LA.f32(T)]
            L["aT"] = [LA.f32(T), LA.f32(T)]
            L["uT"] = [LA.f32(T), LA.f32(T)]
            a2_ = LA.f32(T)
            L["a2"] = [a2_, a2_]
            L["a2key"] = [("a2", 0), ("a2", 0)]
            L["hs"] = [LA.f32(T), a2_]
            L["xcb"] = [LA.bf16(T), LA.bf16(T)]
            L["gs"] = [LA.bf16(T), LA.bf16(T)]
            rg_frontA(seq, 0, gwb, L, 0)
            for c in range(8):
                rg_frontB(seq, c, gwb, L, c % 2, part="B1")
                if c + 1 < 8:
                    rg_frontA(seq, c + 1, gwb, L, (c + 1) % 2, gate=False)
                rg_frontB(seq, c, gwb, L, c % 2, part="B2")
                if c + 1 < 8:
                    rg_gate(seq, c + 1, L, (c + 1) % 2)
                rg_back(seq, c, L, c % 2)

    def out_phase(l, seq, w_out):
        tok0, T, cond, segs = seq
        pidx = segs[0][2]
        phase()
        woutg = LA.bf16(8 * 1024).rearrange("p (k n) -> p k n", k=8)
        stg = [LA.f32(1024) for _ in range(2)]
        fg = LA.f32(1024)
        junk = LA.f32(1024)
        gate_t = LA.f32(1024)
        P.dma("sp", gate_t, gscr[l][:, cond * 1024:(cond + 1) * 1024], (), ["gate_t"])
        if l == 1:
            P.dma("sp", fg, rows[:, 2048:3072], (), ["fg"])
        for k in range(8):
            s_ = stg[k % 2]
            sk = ("stg", k % 2)
            P.dma("sp", s_, w_out[k * 128:(k + 1) * 128, :], (), [sk])
            tt(woutg[:, k, :], s_, gate_t, ALU.mult, [sk, "gate_t"], [("woutg", k)])
        src = xall if l == 0 else xres
        xn = [LA.bf16(1024) for _ in range(2)]
        stats = LA.f32(2 * (T // 128))
        nfr = {}

        def nfront(i, xt, xk):
            ssq = stats[:, 2 * i:2 * i + 1]
            rstd = stats[:, 2 * i + 1:2 * i + 2]
            sk = ("stats", i)
            act(junk, xt, AF.Square, [xk], [sk], accum=ssq)
            act(ssq, ssq, AF.Sqrt, [sk], [sk], scale=1.0 / 1024, bias=EPS)
            recip(rstd, ssq, [sk], [sk])
            xnb = xn[i % 2]
            xnk = ("xn", i % 2)
            ts(xnb, xt, rstd, ALU.mult, [xk, sk], [xnk])

        def nback(i):
            xnb = xn[i % 2]
            xnk = ("xn", i % 2)
            bk, bkey = nb("tp", [6, 7])
            bkb = bk.bitcast(BF16)
            for k in range(8):
                tr(bkb[:, k * 128:(k + 1) * 128], xnb[:, k * 128:(k + 1) * 128], ident_b, [xnk, "cbf"], [bkey])
            for k in range(8):
                dst_ = hT[:, k, i * 128:(i + 1) * 128]
                if bkey[1] == 6:
                    act(dst_, bkb[:, k * 128:(k + 1) * 128], AF.Identity, [bkey, "Amod", "fm"], [("hTt", i, k)],
                        scale=Amod[:, 1, cond, k:k + 1], bias=fm[:, 1, k, cond:cond + 1])
                else:
                    ts(dst_, bkb[:, k * 128:(k + 1) * 128], Amod[:, 1, cond, k:k + 1], ALU.mult,
                       [bkey, "Amod", "fm"], [("hTt", i, k)], s2=fm[:, 1, k, cond:cond + 1], op1=ALU.add)

        for jp in range(T // 256):
            rsl = slice(tok0 + jp * 256, tok0 + (jp + 1) * 256)
            x2, xk = xslot()
            P.dma("pool", x2, src[rsl, :].rearrange("(j p) n -> p j n", p=128), (), [xk])
            for t_ in range(2):
                i = jp * 2 + t_
                xt = x2[:, t_, :]
                for half in range(2):
                    bk, bkey = nb("proj", [0, 1])
                    for k in range(8):
                        mm(bk[:, :], yT[:, k, i * 128:(i + 1) * 128], woutg[:, k, half * 512:(half + 1) * 512], k == 0, k == 7,
                           ["yT", ("woutg", k)], [bkey])
                    tt(xt[:, half * 512:(half + 1) * 512], bk[:, :], xt[:, half * 512:(half + 1) * 512], ALU.add, [bkey, xk], [xk])
                if l == 1:
                    ssq = small[:, 8 + 2 * (i % 2):8 + 2 * (i % 2) + 1]
                    rstd = small[:, 9 + 2 * (i % 2):9 + 2 * (i % 2) + 1]
                    sk = ("small2", i % 2)
                    act(junk, xt, AF.Square, [xk], [sk], accum=ssq)
                    act(ssq, ssq, AF.Sqrt, [sk], [sk], scale=1.0 / 1024, bias=EPS)
                    recip(rstd, ssq, [sk], [sk])
                    stt(xt, xt, rstd, fg, ALU.mult, ALU.mult, [xk, sk, "fg"], [xk])
                else:
                    nfront(i, xt, xk)
                    if i > 0:
                        nback(i - 1)
            dst = xres if l == 0 else y_all
            P.dma("sp", dst[rsl, :].rearrange("(j p) n -> p j n", p=128), x2, [xk], [("xo", l, tok0, jp)])
        if l == 0:
            nback(T // 128 - 1)

    plan = []
    plan.append((lambda: mod_phase(0), None))
    plan.append((lambda: mod_phase(1), None))
    for seq in SEQS:
        plan.append((lambda seq=seq: norm_phase(0, seq), None))
        for h in range(4):
            plan.append((lambda seq=seq, h=h: hgrn_head(seq, h), (("hg", seq[0], h), hgrn_loads(h))))
        for hp in range(4):
            plan.append((lambda seq=seq, hp=hp: attn_pair(seq, hp), (("at", seq[0], hp), attn_loads(hp))))
        if seq[3][0][2] >= 0:
            plan.append((lambda seq=seq: kv_out(seq), None))
        plan.append((lambda seq=seq: out_phase(0, seq, w_out_e), None))
        plan.append((lambda seq=seq: rglru_layer(seq), (("rg", seq[0], 0), rg_loads(0))))
        plan.append((lambda seq=seq: out_phase(1, seq, w_out_o), None))
    for j, (fn, wl) in enumerate(plan):
        if j + 1 < len(plan) and plan[j + 1][1] is not None:
            nid, nloads = plan[j + 1][1]
            PF[0] = (lambda nid=nid, nloads=nloads: wissue(nid, nloads))
        fn()
    phase()
    for s_ in range(2):
        for d in range(2):
            P.dma("sp", o_rg[s_, d].rearrange("(c p o) -> p c o", p=128, o=1), rgfin[:, s_, d, :].unsqueeze(2), ["rgfin"], [("o_rg", s_, d)],
                  allow_slow_non_contiguous=True)
    info = P.finalize()
    return nc, info


_CACHE = {}


def _host_consts():
    c = np.zeros((128, 1536), np.float32)
    c[:, 0:128] = np.eye(128, dtype=np.float32)
    c[:, 128:256] = 1.0
    s = np.arange(128)[:, None]
    t = np.arange(128)[None, :]
    same = (s // 32) == (t // 32)
    c[:, 256:384] = (same & (s <= t)).astype(np.float32)
    c[:, 384:512] = (same & (s >= t)).astype(np.float32)
    tt_ = np.arange(512)
    c[:, 512:1024] = (tt_ % 32 != 0).astype(np.float32)[None, :]
    c[:, 1024:1536] = (tt_ % 32 != 31).astype(np.float32)[None, :]
    return c


def _bias_table(rel_bias):
    kc = np.arange(64)[:, None]
    qc = np.arange(64)[None, :]
    ws = np.clip(qc - 8, 0, 48)
    valid = (kc >= ws) & (kc < ws + 16)
    dc = np.clip(kc - qc + 15, 0, 30)
    M = np.full((8, 16, 64, 64), -1e30, np.float32)
    for dd in range(-7, 8):
        g = rel_bias[:, dd + 7][:, dc]
        M[:, dd + 7] = np.where(valid[None], g, np.float32(-1e30))
    PMt = np.empty((8, 14, 128, 64), np.float32)
    for idx in range(14):
        PMt[:, idx, 0:64] = M[:, idx]
        PMt[:, idx, 64:128] = M[:, idx + 1]
    return np.ascontiguousarray(PMt.transpose(2, 0, 1, 3).reshape(128, 8 * 14 * 64))


def fm(v):
    v = np.asarray(v, np.float32)
    lead = v.shape[:-1]
    n = v.shape[-1] // 128
    return np.moveaxis(v.reshape(lead + (n, 128)), -1, 0)


def kernel(x_prompt, x_sample, state_hgrn, cache_na_k, cache_na_v, state_rglru, c, c_ctx, norm_gain,
           w_mod, b_mod, w_in_even, w_out_even, hgrn_lb_logits, hgrn_out_gain, na_rel_bias,
           w_in_odd, w_out_odd, conv_w, conv_b, rg_gate_w, rg_gate_b, rg_lambda, final_gain):
    f = lambda a: np.ascontiguousarray(np.asarray(a, np.float32))
    if "nc" not in _CACHE:
        _CACHE["nc"] = build_nc()
    nc, info = _CACHE["nc"]
    consts = _host_consts()
    pmt = _bias_table(f(na_rel_bias)[0])
    rows = np.empty((128, 3072), np.float32)
    rows[:, 0:1024] = f(b_mod)[0, 2048:3072][None]
    rows[:, 1024:2048] = f(b_mod)[1, 2048:3072][None]
    rows[:, 2048:3072] = f(final_gain)[None]
    in_maps = []
    for b in range(8):
        vecs = np.zeros((128, NV), np.float32)
        cond2 = np.stack([f(c)[b], f(c_ctx)], 0)
        vecs[:, V_COND:V_COND + 16] = fm(cond2).transpose(0, 2, 1).reshape(128, 16)
        vecs[:, V_NG:V_NG + 16] = fm(norm_gain).reshape(128, 16)
        vecs[:, V_BMOD:V_BMOD + 48] = fm(b_mod).reshape(128, 48)
        vecs[:, V_LBL:V_LBL + 16] = fm(hgrn_lb_logits).reshape(128, 16)
        vecs[:, V_OG:V_OG + 4] = f(hgrn_out_gain)[0].T
        vecs[:, V_CW:V_CW + 32] = fm(f(conv_w)[0]).reshape(128, 32)
        vecs[:, V_CB:V_CB + 8] = fm(f(conv_b)[0]).reshape(128, 8)
        vecs[:, V_GB:V_GB + 32] = fm(f(rg_gate_b)[0]).reshape(128, 32)
        vecs[:, V_LAM:V_LAM + 16] = fm(f(rg_lambda)[0]).reshape(128, 16)
        vecs[:, V_SRG:V_SRG + 16] = fm(f(state_rglru)[b, 0]).reshape(128, 16)
        xall = np.concatenate([f(x_sample)[b], f(x_prompt)[2 * b], f(x_prompt)[2 * b + 1]], 0)
        in_maps.append({
            "xall": np.ascontiguousarray(xall),
            "st_h": f(state_hgrn)[b, 0], "ck": f(cache_na_k)[b, 0], "cv": f(cache_na_v)[b, 0],
            "vecs": vecs, "rows": rows, "consts": consts, "pm": pmt,
            "w_mod": f(w_mod), "w_in_e": f(w_in_even)[0], "w_out_e": f(w_out_even)[0],
            "w_in_o": f(w_in_odd)[0], "w_out_o": f(w_out_odd)[0],
            "gw": f(rg_gate_w)[0].reshape(32, 128, 128),
        })
    res = run_bass_kernel_spmd(nc, in_maps, core_ids=list(range(8)))
    R = res.results
    y_sample = np.stack([R[b]["y_all"][0:2048] for b in range(8)], 0)
    y_prompt = np.stack([R[b]["y_all"][2048 + 256 * s:2048 + 256 * (s + 1)] for b in range(8) for s in range(2)], 0)
    n_sh = np.concatenate([R[b]["o_sh"] for b in range(8)], 0)[:, None]
    n_k = np.concatenate([R[b]["o_k"] for b in range(8)], 0)[:, None]
    n_v = np.concatenate([R[b]["o_v"] for b in range(8)], 0)[:, None]
    n_rg = np.concatenate([R[b]["o_rg"] for b in range(8)], 0)[:, None]
    return (y_prompt.astype(np.float32), y_sample.astype(np.float32), n_sh.astype(np.float32),
            n_k.astype(np.float32), n_v.astype(np.float32), n_rg.astype(np.float32))
```

```python
import numpy as np
import concourse.bass as bass
import concourse.mybir as mybir
from concourse.bass_utils import run_bass_kernel_spmd

F32 = mybir.dt.float32
BF16 = mybir.dt.bfloat16
AF = mybir.ActivationFunctionType
ALU = mybir.AluOpType
ENGS = ("pe", "act", "dve", "pool", "sp")
EPS = 1e-6


class Prog:
    def __init__(self, nc, n_dma_sems=24):
        self.nc = nc
        self.ops = []
        self.lastw = {}
        self.readers = {}
        self.n_dma_sems = n_dma_sems
        self.last_eng = {}
        self.open_dmas = []
        self.pending_bar = {}

    def op(self, eng, fn, reads=(), writes=(), dma=False):
        idx = len(self.ops)
        deps = set()
        for r in reads:
            w = self.lastw.get(r)
            if w is not None:
                deps.add(w)
        for w_ in writes:
            w = self.lastw.get(w_)
            if w is not None:
                deps.add(w)
            rd = self.readers.get(w_)
            if rd:
                for k, v in rd.items():
                    if k == "dma":
                        deps.update(v)
                    else:
                        deps.add(v)
        for r in reads:
            rd = self.readers.setdefault(r, {})
            if dma:
                rd.setdefault("dma", []).append(idx)
            else:
                rd[eng] = idx
        for w_ in writes:
            self.lastw[w_] = idx
            self.readers[w_] = {}
        pb = self.pending_bar.pop(eng, None)
        if pb is not None:
            deps.add(pb)
        self.ops.append(dict(eng=eng, fn=fn, deps=deps, dma=dma))
        if dma:
            self.open_dmas.append(idx)
        else:
            self.last_eng[eng] = idx
        return idx

    def dma(self, q, out, in_, reads=(), writes=(), **kw):
        return self.op(q, lambda e: e.dma_start(out=out, in_=in_, **kw), reads, writes, dma=True)

    def barrier(self, scratch):
        deps = set(self.last_eng.values()) | set(self.open_dmas)
        idx = len(self.ops)
        self.ops.append(dict(eng="dve", fn=lambda e: e.memset(scratch, 0.0), deps=deps, dma=False))
        self.open_dmas = []
        self.last_eng = {"dve": idx}
        self.lastw = {"__bar__": idx}
        self.readers = {}
        self.bar = idx
        self.pending_bar = {e: idx for e in ENGS}
        return idx

    def finalize(self):
        nc = self.nc
        ops = self.ops
        needed = [False] * len(ops)
        for i, o in enumerate(ops):
            for d in o["deps"]:
                po = ops[d]
                if po["dma"]:
                    continue
                if po["eng"] == "pe" and o["eng"] == "pe" and not o["dma"]:
                    continue
                needed[d] = True
        cnt = {e: 0 for e in ENGS}
        for i, o in enumerate(ops):
            if o["dma"]:
                continue
            if needed[i]:
                cnt[o["eng"]] += 1
                o["seq"] = cnt[o["eng"]]
            else:
                o["seq"] = None
        half = self.n_dma_sems // 2
        kk = {True: 0, False: 0}
        dcount = [0] * self.n_dma_sems
        for i, o in enumerate(ops):
            if o["dma"]:
                sw = o["eng"] == "pool"
                k = kk[sw] + (half if sw else 0)
                kk[sw] = (kk[sw] + 1) % half
                o["dsem"] = k
                dcount[k] += 1
                o["dval"] = 16 * dcount[k]
        sems = {e: nc.alloc_semaphore(name=f"s_{e}") for e in ENGS}
        dsems = [nc.alloc_semaphore(name=f"s_dma{j}") for j in range(self.n_dma_sems)]
        by_eng = {e: [i for i, o in enumerate(ops) if o["eng"] == e] for e in ENGS}

        def emit_engine(ename, eh):
            waited = {}

            def wait(sem, val, key):
                if waited.get(key, 0) >= val:
                    return
                eh.wait_ge(sem, val)
                waited[key] = val

            for i in by_eng[ename]:
                o = ops[i]
                for d in sorted(o["deps"]):
                    po = ops[d]
                    if po["dma"]:
                        wait(dsems[po["dsem"]], po["dval"], ("d", po["dsem"]))
                    else:
                        if po["eng"] == "pe" and ename == "pe" and not o["dma"]:
                            continue
                        wait(sems[po["eng"]], po["seq"], ("e", po["eng"]))
                if o["dma"]:
                    if o["dval"] > 16:
                        wait(dsems[o["dsem"]], o["dval"] - 16, ("d", o["dsem"]))
                    ins = o["fn"](eh)
                    ins.then_inc(dsems[o["dsem"]], 16)
                else:
                    ins = o["fn"](eh)
                    if o["seq"] is not None:
                        ins.then_inc(sems[ename], 1)
            if ename == "sp":
                for j in range(self.n_dma_sems):
                    if dcount[j] > 0:
                        eh.wait_ge(dsems[j], 16 * dcount[j])
                for e in ENGS:
                    if cnt[e] > 0:
                        eh.wait_ge(sems[e], cnt[e])

        with nc.Block() as block:
            @block.tensor
            def _(e):
                emit_engine("pe", e)

            @block.scalar
            def _(e):
                emit_engine("act", e)

            @block.vector
            def _(e):
                emit_engine("dve", e)

            @block.gpsimd
            def _(e):
                emit_engine("pool", e)

            @block.sync
            def _(e):
                emit_engine("sp", e)
        return dict(n_ops=len(ops), sig=dict(cnt), ndma=sum(dcount))


class Arena:
    def __init__(self, nc, name, nfloats):
        self.t = nc.alloc_sbuf_tensor(name, [128, nfloats], F32).ap()
        self.n = nfloats
        self.off = 0
        self.name = name
        self.gen = 0

    def f32(self, n):
        n = (n + 1) // 2 * 2
        assert self.off + n <= self.n, (self.name, self.off, n, self.n)
        v = self.t[:, self.off:self.off + n]
        self.off += n
        return v

    def bf16(self, n):
        m = (n + 3) // 4 * 2
        assert self.off + m <= self.n, (self.name, self.off, m, self.n)
        v = self.t[:, self.off:self.off + m].bitcast(BF16)
        self.off += m
        return v[:, 0:n]

    def reset(self):
        self.off = 0
        self.gen += 1


V_COND, V_NG, V_BMOD, V_LBL, V_OG, V_CW, V_CB, V_GB, V_LAM, V_SRG, NV = 0, 16, 32, 80, 96, 100, 132, 140, 172, 188, 204
SEQS = [(0, 2048, 0, [(0, 2048, -1)]), (2048, 512, 1, [(0, 256, 0), (256, 256, 1)])]


def build_nc():
    nc = bass.Bass("TRN2", target_bir_lowering=False)

    def din(name, shape):
        return nc.dram_tensor(name, list(shape), F32, kind="ExternalInput").ap()

    def dout(name, shape):
        return nc.dram_tensor(name, list(shape), F32, kind="ExternalOutput").ap()

    xall = din("xall", [2560, 1024])
    st_h = din("st_h", [2, 4, 128, 128])
    ck = din("ck", [8, 256, 64])
    cv = din("cv", [8, 256, 64])
    vecs = din("vecs", [128, NV])
    rows = din("rows", [128, 3072])
    consts = din("consts", [128, 1536])
    pm = din("pm", [128, 8 * 14 * 64])
    w_mod = din("w_mod", [2, 1024, 3072])
    w_in_e = din("w_in_e", [1024, 4608])
    w_out_e = din("w_out_e", [1024, 1024])
    w_in_o = din("w_in_o", [1024, 2048])
    w_out_o = din("w_out_o", [1024, 1024])
    gwd = din("gw", [32, 128, 128])
    y_all = dout("y_all", [2560, 1024])
    o_sh = dout("o_sh", [2, 2, 4, 128, 128])
    o_k = dout("o_k", [2, 8, 256, 64])
    o_v = dout("o_v", [2, 8, 256, 64])
    o_rg = dout("o_rg", [2, 2, 1024])
    xres = nc.dram_tensor("xres", [2560, 1024], F32, kind="Internal").ap()
    gscr = nc.dram_tensor("gscr", [2, 128, 2048], F32, kind="Internal").ap()

    P = Prog(nc)
    banks = [nc.alloc_psum_tensor(f"bank{i}", [128, 512], F32).ap() for i in range(8)]
    rr = {}

    def nb(group, ids):
        j = rr.get(group, 0)
        rr[group] = j + 1
        b = ids[j % len(ids)]
        return banks[b], ("bank", b)

    def mm(out, lhsT, rhs, start, stop, r, w, tp=None):
        kw = {"tile_position": tp} if tp is not None else {}
        P.op("pe", lambda e: e.matmul(out, lhsT=lhsT, rhs=rhs, start=start, stop=stop, **kw), r, w)

    def tr(out, in_, ident, r, w):
        P.op("pe", lambda e: e.transpose(out=out, in_=in_, identity=ident), r, w)

    def act(out, in_, func, r, w, scale=None, bias=None, accum=None):
        kw = {}
        if scale is not None:
            kw["scale"] = scale
        if bias is not None:
            kw["bias"] = bias
        if accum is not None:
            kw["accum_out"] = accum
        P.op("act", lambda e: e.activation(out=out, in_=in_, func=func, **kw), r, w)

    def tt(out, in0, in1, op, r, w, eng="dve"):
        P.op(eng, lambda e: e.tensor_tensor(out=out, in0=in0, in1=in1, op=op), r, w)

    def ts(out, in0, s1, op0, r, w, s2=None, op1=None, eng="dve"):
        if op1 is None:
            P.op(eng, lambda e: e.tensor_scalar(out=out, in0=in0, scalar1=s1, scalar2=None, op0=op0), r, w)
        else:
            P.op(eng, lambda e: e.tensor_scalar(out=out, in0=in0, scalar1=s1, scalar2=s2, op0=op0, op1=op1), r, w)

    def stt(out, in0, scalar, in1, op0, op1, r, w):
        P.op("dve", lambda e: e.scalar_tensor_tensor(out=out, in0=in0, scalar=scalar, in1=in1, op0=op0, op1=op1), r, w)

    def cp(out, in_, r, w, eng="dve"):
        if eng == "act":
            P.op("act", lambda e: e.activation(out=out, in_=in_, func=AF.Copy), r, w)
        else:
            P.op(eng, lambda e: e.tensor_copy(out=out, in_=in_), r, w)

    def recip(out, in_, r, w):
        P.op("dve", lambda e: e.reciprocal(out=out, in_=in_), r, w)

    def recipf(out, in_, r, w):
        P.op("dve", lambda e: e.reciprocal_approx_fast(out=out, in_=in_), r, w)

    def scan(out, d0, d1, initial, r, w):
        P.op("dve", lambda e: e.tensor_tensor_scan(out=out, data0=d0, data1=d1, initial=initial,
                                                   op0=ALU.mult, op1=ALU.add), r, w)

    def memset(ap, val, w, eng="dve"):
        P.op(eng, lambda e: e.memset(ap, val), (), w)

    PA = Arena(nc, "persist", 27900)
    LA = Arena(nc, "phase", 25200)
    barscr = PA.f32(2)
    hT = PA.bf16(8 * 2048).rearrange("p (k t) -> p k t", k=8)
    yT = PA.bf16(8 * 2048).rearrange("p (k t) -> p k t", k=8)
    cst = PA.f32(1536)
    ident_f, ones_f, maskF_f, maskB_f = cst[:, 0:128], cst[:, 128:256], cst[:, 256:384], cst[:, 384:512]
    m01 = [cst[:, 512:1024], cst[:, 1024:1536]]
    cbf = PA.bf16(384)
    ident_b, maskF_b, maskB_b = cbf[:, 0:128], cbf[:, 128:256], cbf[:, 256:384]
    vt = PA.f32(NV)
    siluc = PA.f32(16).rearrange("p (k c) -> p k c", c=2)
    lb = PA.f32(8).rearrange("p (d h) -> p d h", d=2)
    oml = PA.f32(8).rearrange("p (d h) -> p d h", d=2)
    noml = PA.f32(8).rearrange("p (d h) -> p d h", d=2)
    cc1 = PA.f32(16).rearrange("p (d c) -> p d c", d=2)
    cc2 = PA.f32(16).rearrange("p (d c) -> p d c", d=2)
    fm = PA.f32(96).rearrange("p (l j c) -> p l j c", l=2, j=24)
    Amod = PA.f32(32).rearrange("p (l c k) -> p l c k", l=2, c=2)
    small = PA.f32(64)
    rgfin = PA.f32(32).rearrange("p (s d c) -> p s d c", s=2, d=2)
    wbuf = [PA.bf16(8 * 640).rearrange("p (k n) -> p k n", k=8) for _ in range(2)]
    xin = [PA.f32(2048).rearrange("p (j n) -> p j n", j=2) for _ in range(2)]
    wrr = [0]

    def wslot():
        j = wrr[0] % 2
        wrr[0] += 1
        return wbuf[j], ("wbuf", j)

    xrr = [0]

    def xslot():
        j = xrr[0] % 2
        xrr[0] += 1
        return xin[j], ("xin", j)

    wstate = {}
    PF = [None]

    def wissue(wid, loads):
        w, wk = wslot()
        for (src, c0, n, off) in loads:
            wload(w, wk, src, c0, n, off)
        wstate[wid] = (w, wk)

    def wget(wid, loads):
        if wid not in wstate:
            wissue(wid, loads)
        return wstate.pop(wid)

    def hgrn_loads(h):
        wsrc = w_in_e.rearrange("(k p) n -> p k n", p=128)
        return [(wsrc, c0, 128, j * 128) for j, c0 in
                enumerate([h * 128, 512 + h * 128, 1024 + h * 128, 1536 + h * 128, 2048 + h * 128])]

    def attn_loads(hp):
        wsrc = w_in_e.rearrange("(k p) n -> p k n", p=128)
        return [(wsrc, c0, 128, j * 128) for j, c0 in
                enumerate([2560 + hp * 128, 3072 + hp * 128, 3584 + hp * 128, 4096 + hp * 128])]

    def rg_loads(c):
        wsrc = w_in_o.rearrange("(k p) n -> p k n", p=128)
        return [(wsrc, c * 128, 128, 0), (wsrc, 1024 + c * 128, 128, 128)]

    def phase():
        P.barrier(barscr)
        LA.reset()
        if PF[0] is not None:
            f_ = PF[0]
            PF[0] = None
            f_()

    P.dma("sp", cst, consts, (), ["cst"])
    P.dma("sp", vt, vecs, (), ["vt"])
    cp(cbf[:, 0:128], ident_f, ["cst"], ["cbf"])
    cp(cbf[:, 128:384], cst[:, 256:512], ["cst"], ["cbf"])
    condT = vt[:, V_COND:V_COND + 16].rearrange("p (k c) -> p k c", c=2)
    act(siluc, condT, AF.Silu, ["vt"], ["siluc"])
    lbl = vt[:, V_LBL:V_LBL + 16].rearrange("p (d l h) -> p d l h", d=2, l=2)
    tt(lb, lbl[:, :, 0, :], lbl[:, :, 1, :], ALU.subtract, ["vt"], ["lb"])
    act(lb, lb, AF.Sigmoid, ["lb"], ["lb"])
    ts(oml, lb, -1.0, ALU.mult, ["lb"], ["oml"], s2=1.0, op1=ALU.add)
    ts(noml, oml, -1.0, ALU.mult, ["oml"], ["noml"])
    lam = vt[:, V_LAM:V_LAM + 16].rearrange("p (d c) -> p d c", d=2)
    act(cc1, lam, AF.Exp, ["vt"], ["cc1"], scale=-1.0)
    act(cc1, cc1, AF.Ln, ["cc1"], ["cc1"], bias=1.0)
    ts(cc2, cc1, -16.0, ALU.mult, ["cc1"], ["cc2"])
    ts(cc1, cc1, -8.0, ALU.mult, ["cc1", "cc2"], ["cc1"])
    ngain = vt[:, V_NG:V_NG + 16].rearrange("p (l k) -> p l k", l=2)
    bmod = vt[:, V_BMOD:V_BMOD + 48].rearrange("p (l j) -> p l j", l=2)

    def mod_phase(l):
        if l == 0:
            phase()
        gate_bc = LA.f32(2048).rearrange("p (c n) -> p c n", c=2)
        screp = LA.bf16(8 * 2 * 128).rearrange("p (k c m) -> p k c m", k=8, c=2)
        silub = LA.bf16(16).rearrange("p (k c) -> p k c", c=2)
        zer = LA.f32(128)
        rowt = LA.f32(1024)
        wm = [LA.bf16(8 * 512).rearrange("p (k n) -> p k n", k=8) for _ in range(3)]
        memset(zer, 0.0, ["zer"])
        cp(silub, siluc, ["siluc"], ["silub"])
        for k in range(8):
            for c in range(2):
                ts(screp[:, k, c, :], zer, siluc[:, k, c:c + 1], ALU.add, ["zer", "siluc"], ["screp"])
        P.dma("sp", rowt, rows[:, l * 1024:(l + 1) * 1024], (), ["rowt"])
        wsrc = w_mod[l].rearrange("(k p) n -> p k n", p=128)
        for cb in range(6):
            w = wm[cb % 3]
            wk = ("wm", cb % 3)
            P.dma("pool", w, wsrc[:, :, cb * 512:(cb + 1) * 512], (), [wk])
            for jj in range(4):
                j = cb * 4 + jj
                bk, bkey = nb("misc", [7, 6])
                for k in range(8):
                    mm(bk[:, 0:2], w[:, k, jj * 128:(jj + 1) * 128], silub[:, k, :], k == 0, k == 7,
                       [wk, "silub"], [bkey])
                ts(fm[:, l, j, :], bk[:, 0:2], bmod[:, l, j:j + 1], ALU.add, [bkey, "vt"], ["fm"])
            if cb >= 4:
                for c in range(2):
                    bk, bkey = nb("proj", [0, 1])
                    for k in range(8):
                        mm(bk[:, :], screp[:, k, c, :], w[:, k, :], k == 0, k == 7, [wk, "screp"], [bkey])
                    tt(gate_bc[:, c, (cb - 4) * 512:(cb - 3) * 512], bk[:, :],
                       rowt[:, (cb - 4) * 512:(cb - 3) * 512], ALU.add, [bkey, "rowt"], ["gate_bc"])
        for c in range(2):
            stt(Amod[:, l, c, :], fm[:, l, 8:16, c], 1.0, ngain[:, l, :], ALU.add, ALU.mult, ["fm", "vt"], ["Amod"])
        P.dma("sp", gscr[l], gate_bc.rearrange("p c n -> p (c n)"), ["gate_bc"], [("gscr", l)])

    def norm_phase(l, seq):
        tok0, T, cond, segs = seq
        pidx = segs[0][2]
        phase()
        NT = T // 128
        xn = [LA.bf16(1024) for _ in range(2)]
        junk = LA.f32(1024)
        stats = LA.f32(2 * NT)
        src = xall if l == 0 else xres
        fr = {}
        xp = {}

        def front(i):
            if i % 2 == 0:
                x2, xk = xslot()
                P.dma("sp", x2,
                      src[tok0 + i * 128: tok0 + (i + 2) * 128, :].rearrange("(j p) n -> p j n", p=128), (), [xk])
                xp[i // 2] = (x2, xk)
            x2, xk = xp[i // 2]
            xt = x2[:, i % 2, :]
            ssq = stats[:, 2 * i:2 * i + 1]
            rstd = stats[:, 2 * i + 1:2 * i + 2]
            sk = ("stats", i)
            act(junk, xt, AF.Square, [xk], [sk], accum=ssq)
            act(ssq, ssq, AF.Sqrt, [sk], [sk], scale=1.0 / 1024, bias=EPS)
            recip(rstd, ssq, [sk], [sk])
            xnb = xn[i % 2]
            xnk = ("xn", i % 2)
            ts(xnb, xt, rstd, ALU.mult, [xk, sk], [xnk])
            bk, bkey = nb("tp", [6, 7])
            bkb = bk.bitcast(BF16)
            for k in range(8):
                tr(bkb[:, k * 128:(k + 1) * 128], xnb[:, k * 128:(k + 1) * 128], ident_b, [xnk, "cbf"], [bkey])
            fr[i] = (bkb, bkey)

        def back(i):
            bkb, bkey = fr[i]
            for k in range(8):
                dst = hT[:, k, i * 128:(i + 1) * 128]
                if bkey[1] == 6:
                    act(dst, bkb[:, k * 128:(k + 1) * 128], AF.Identity, [bkey, "Amod", "fm"], [("hTt", i, k)],
                        scale=Amod[:, l, cond, k:k + 1], bias=fm[:, l, k, cond:cond + 1])
                else:
                    ts(dst, bkb[:, k * 128:(k + 1) * 128], Amod[:, l, cond, k:k + 1], ALU.mult,
                       [bkey, "Amod", "fm"], [("hTt", i, k)], s2=fm[:, l, k, cond:cond + 1], op1=ALU.add)

        front(0)
        for i in range(NT):
            if i + 1 < NT:
                front(i + 1)
            back(i)

    def proj_fm(bk, bkey, w, wk, c0, t0, n):
        for k in range(8):
            mm(bk[:, 0:n], w[:, k, c0:c0 + 128], hT[:, k, t0:t0 + n], k == 0, k == 7, [wk, "hT"], [bkey])

    def wload(dst, dkey, wsrc, c0, n, off=0):
        P.dma("pool", dst[:, :, off:off + n], wsrc[:, :, c0:c0 + n], (), [dkey])

    def hgrn_head(seq, h):
        tok0, T, cond, segs = seq
        pidx = segs[0][2]
        phase()
        BS = min(512, T)
        NBk = T // BS
        NT = T // 128
        wsrc = w_in_e.rearrange("(k p) n -> p k n", p=128)
        w, wk = wget(("hg", tok0, h), hgrn_loads(h))
        qdec = [LA.bf16(T) for _ in range(2)]
        kinv = [LA.bf16(T) for _ in range(2)]
        kd = [LA.bf16(T).rearrange("p (i d) -> p i d", d=128) for _ in range(2)]
        vsb = LA.bf16(T).rearrange("p (i d) -> p i d", d=128)
        vx = LA.bf16(4 * T).rearrange("p (i c d) -> p i c d", c=4, d=128)
        oT = LA.f32(T)
        dec = [LA.f32(64) for _ in range(2)]
        S32 = [[LA.f32(128) for _ in range(2)] for _ in range(2)]
        Sbf = [[LA.bf16(128) for _ in range(2)] for _ in range(2)]
        attb = [[LA.bf16(128) for _ in range(2)] for _ in range(2)]
        HB = min(1024, T)
        NHB = T // HB
        sets = [{n_: LA.f32(HB) for n_ in "AKBE"} for _ in range(2)]
        VT = oT[:, 0:T // 2].bitcast(BF16)
        for b in range(NBk):
            sl = slice(b * BS, (b + 1) * BS)
            bk, bkey = nb("proj", [0, 1])
            proj_fm(bk, bkey, w, wk, 384, b * BS, BS)
            cp(VT[:, sl], bk[:, 0:BS], [bkey], ["VT"], eng="act")
        for g0 in range(0, NT, 4):
            ng = min(4, NT - g0)
            bt, btkey = nb("tp", [6, 7])
            btb = bt.bitcast(BF16)
            for j in range(ng):
                tr(btb[:, j * 128:(j + 1) * 128], VT[:, (g0 + j) * 128:(g0 + j + 1) * 128], ident_b, ["VT", "cbf"], [btkey])
            cp(vsb[:, g0:g0 + ng, :], btb[:, 0:ng * 128].rearrange("p (j d) -> p j d", d=128), [btkey],
               [("vsb", g0 + j) for j in range(ng)], eng="act")
            for j in range(ng):
                i = g0 + j
                P.op("pool", lambda e, o_=vx[:, i, :, :], a_=vsb[:, i, :].unsqueeze(1).to_broadcast([128, 4, 128]),
                     b_=maskF_f[:, 31::32].unsqueeze(2).to_broadcast([128, 4, 128]):
                     e.tensor_tensor(out=o_, in0=a_, in1=b_, op=ALU.mult), [("vsb", i), "cst"], [("vx", i)])
        iters = [(d, hb) for d in range(2) for hb in range(NHB)]

        def stage1(n):
            d, hb = iters[n]
            si = n % 2
            st = sets[si]
            A, Kt, Bt, Et = (st[n_] for n_ in "AKBE")
            kA, kK, kB, kE = (("t" + n_, si) for n_ in "AKBE")
            t0 = hb * HB
            nbl = HB // BS
            for b in range(nbl):
                bk, bkey = nb("proj", [0, 1])
                proj_fm(bk, bkey, w, wk, 128 * (1 + d), t0 + b * BS, BS)
                act(A[:, b * BS:(b + 1) * BS], bk[:, 0:BS], AF.Sigmoid, [bkey], [kA])
            ts(Kt, A, noml[:, d, h:h + 1], ALU.mult, [kA, "noml", "oml"], [kK], s2=oml[:, d, h:h + 1], op1=ALU.add)
            act(A, A, AF.Ln, [kA, "oml", "lb"], [kA], scale=oml[:, d, h:h + 1], bias=lb[:, d, h:h + 1])
            for b in range(nbl):
                bsl = slice(b * BS, (b + 1) * BS)
                if d == 0:
                    scan(Bt[:, bsl], m01[0][:, 0:BS], A[:, bsl], 0.0, [kA, "cst"], [kB])
                else:
                    scan(Bt[:, bsl][:, ::-1], m01[1][:, 0:BS][:, ::-1], A[:, bsl][:, ::-1], 0.0, [kA, "cst"], [kB])
            act(Et, Bt, AF.Exp, [kB], [kE])
            act(Bt, Bt, AF.Exp, [kB, kE], [kB], scale=-1.0)

        def stage2(n):
            d, hb = iters[n]
            si = n % 2
            st = sets[si]
            A, Kt, Bt, Et = (st[n_] for n_ in "AKBE")
            kA, kK, kB, kE = (("t" + n_, si) for n_ in "AKBE")
            t0 = hb * HB
            nbl = HB // BS
            for b in range(nbl):
                bq, bqkey = nb("proj", [0, 1])
                proj_fm(bq, bqkey, w, wk, 0, t0 + b * BS, BS)
                tt(qdec[d][:, t0 + b * BS:t0 + (b + 1) * BS], bq[:, 0:BS], Et[:, b * BS:(b + 1) * BS], ALU.mult,
                   [bqkey, kE], [("qdec", d)])
            tt(Kt, Kt, Bt, ALU.mult, [kK, kB], [kK])
            cp(kinv[d][:, t0:t0 + HB], Kt, [kK], [("kinv", d)], eng="act")
            nch = HB // 32
            dsl = dec[d][:, t0 // 32:t0 // 32 + nch]
            cp(dsl, Et.rearrange("p (c i) -> p c i", i=32)[:, :, 31 if d == 0 else 0], [kE], [("dec", d)])
            kdT = A[:, 0:HB // 2].bitcast(BF16)
            tt(kdT.rearrange("p (c i) -> p c i", i=32), Kt.rearrange("p (c i) -> p c i", i=32),
               dsl.unsqueeze(2).to_broadcast([128, nch, 32]), ALU.mult, [kK, ("dec", d), kB], [kA])
            for b in range(nbl):
                bt, btkey = nb("tp", [6, 7])
                btb = bt.bitcast(BF16)
                for j in range(BS // 128):
                    tr(btb[:, j * 128:(j + 1) * 128], kdT[:, b * BS + j * 128:b * BS + (j + 1) * 128], ident_b,
                       [kA, "cbf"], [btkey])
                i0 = (t0 + b * BS) // 128
                cp(kd[d][:, i0:i0 + BS // 128, :],
                   btb[:, 0:BS].rearrange("p (j d) -> p j d", d=128), [btkey], [("kd", d)], eng="act")

        stage1(0)
        for n in range(len(iters)):
            if n + 1 < len(iters):
                stage1(n + 1)
            stage2(n)
        for d in range(2):
            if pidx < 0:
                P.dma("sp", S32[d][0], st_h[d, h], (), [("S32", d, 0)])
            else:
                memset(S32[d][0], 0.0, [("S32", d, 0)])
            cp(Sbf[d][0], S32[d][0], [("S32", d, 0)], [("Sbf", d, 0)], eng="act")
        seg_first = [{t0 // 128: sp_ for (t0, Ts, sp_) in segs}, {(t0 + Ts) // 128 - 1: sp_ for (t0, Ts, sp_) in segs}]
        seg_last = [{(t0 + Ts) // 128 - 1: sp_ for (t0, Ts, sp_) in segs}, {t0 // 128: sp_ for (t0, Ts, sp_) in segs}]
        fr = {}
        KB = [0, 1, 5, 6]
        kvs = [[LA.f32(512).rearrange("p (c d) -> p c d", d=128) for _ in range(2)] for _ in range(2)]

        def front(step):
            for d in range(2):
                i = step if d == 0 else NT - 1 - step
                tsl = slice(i * 128, (i + 1) * 128)
                ba, bakey = nb("att", [2])
                mm(ba[:, 0:128], kinv[d][:, tsl], qdec[d][:, tsl], True, True, [("kinv", d), ("qdec", d)], [bakey])
                ab = attb[d][step % 2]
                abk = ("attb", d, step % 2)
                tt(ab, ba[:, 0:128], maskF_f if d == 0 else maskB_f, ALU.mult, [bakey, "cst"], [abk])
                ks3 = kvs[d][step % 2]
                ksb = [ks3[:, c, :] for c in range(4)]
                kskey = [("kvs", d, step % 2)] * 4
                bkv_, bkvk = nb("kv", [5, 6])
                mm(bkv_[:, :], kd[d][:, i, :], vx[:, i, :, :].rearrange("p c d -> p (c d)"), True, True,
                   [("kd", d), ("vx", i)], [bkvk])
                cp(ks3, bkv_[:, :].rearrange("p (c d) -> p c d", d=128), [bkvk], [kskey[0]], eng="act")
                fr[(step, d)] = (i, tsl, ab, abk, ksb, kskey)

        sbi = [0, 0]

        def back(step):
            bos = []
            for d in range(2):
                i, tsl, ab, abk, bkv, bkvkey = fr[(step, d)]
                bo, bokey = nb("o", [3, 4])
                mm(bo[:, 0:128], vsb[:, i, :], ab, True, False, [("vsb", i), abk], [bokey])
                bos.append((bo, bokey))
            for d in range(2):
                i = fr[(step, d)][0]
                if step > 0 and i in seg_first[d]:
                    cur = sbi[d] % 2
                    memset(S32[d][cur], 0.0, [("S32", d, cur)])
                    memset(Sbf[d][cur], 0.0, [("Sbf", d, cur)])
            for n_ in range(4):
                for d in range(2):
                    i, tsl, ab, abk, bkv, bkvkey = fr[(step, d)]
                    bo, bokey = bos[d]
                    c = n_ if d == 0 else 3 - n_
                    cur = sbi[d] % 2
                    nxt = 1 - cur
                    csl = slice(i * 128 + c * 32, i * 128 + c * 32 + 32)
                    mm(bo[:, c * 32:(c + 1) * 32], Sbf[d][cur], qdec[d][:, csl], False, n_ == 3,
                       [("Sbf", d, cur), ("qdec", d)], [bokey])
                    dcs = dec[d][:, i * 4 + c:i * 4 + c + 1]
                    stt(S32[d][nxt], S32[d][cur], dcs, bkv[c], ALU.mult, ALU.add,
                        [("S32", d, cur), ("dec", d), bkvkey[c]], [("S32", d, nxt)])
                    cp(Sbf[d][nxt], S32[d][nxt], [("S32", d, nxt)], [("Sbf", d, nxt)], eng="act")
                    sbi[d] += 1
            for d in range(2):
                i = fr[(step, d)][0]
                if i in seg_last[d] and seg_last[d][i] >= 0:
                    fin = sbi[d] % 2
                    sp_ = seg_last[d][i]
                    P.dma("sp", o_sh[sp_, d, h], S32[d][fin], [("S32", d, fin)], [("o_sh", sp_, d, h)])
            for d in range(2):
                i, tsl, ab, abk, bkv, bkvkey = fr[(step, d)]
                bo, bokey = bos[d]
                first = (i <= NT - 1 - i) if d == 0 else (NT - 1 - i < i)
                if first:
                    cp(oT[:, tsl], bo[:, 0:128], [bokey], [("oT", i)], eng="act")
                else:
                    tt(oT[:, tsl], bo[:, 0:128], oT[:, tsl], ALU.add, [bokey, ("oT", i)], [("oT", i)])

        front(0)
        for step in range(NT):
            if step + 1 < NT:
                front(step + 1)
            back(step)
        gs = kinv[0]
        for b in range(NBk):
            bk, bkey = nb("proj", [0, 1])
            proj_fm(bk, bkey, w, wk, 512, b * BS, BS)
            act(gs[:, b * BS:(b + 1) * BS], bk[:, 0:BS], AF.Silu, [bkey], [("kinv", 0)])
        ogain = vt[:, V_OG + h:V_OG + h + 1]
        for b in range(NBk):
            sl = slice(b * BS, (b + 1) * BS)
            st = sets[b % 2]
            sq, rs_, t1 = st["A"][:, 0:BS], st["K"][:, 0:BS], st["B"][:, 0:BS]
            kA, kK, kB = (("t" + n_, b % 2) for n_ in "AKB")
            rs2 = st["E"][:, 0:BS]
            kE = ("tE", b % 2)
            okeys = [("oT", i) for i in range(b * BS // 128, (b + 1) * BS // 128)]
            act(sq, oT[:, sl], AF.Square, okeys, [kA])
            bk, bkey = nb("misc", [7])
            mm(bk[:, 0:BS], ones_f, sq, True, True, [kA, "cst"], [bkey])
            act(rs_, bk[:, 0:BS], AF.Ln, [bkey], [kK], scale=1.0 / 128, bias=EPS)
            act(rs2, rs_, AF.Exp, [kK], [kE], scale=-0.5)
            tt(t1, oT[:, sl], rs2, ALU.mult, okeys + [kE], [kB])
            stt(yT[:, h, sl], t1, ogain, gs[:, sl], ALU.mult, ALU.mult, [kB, "vt", ("kinv", 0)], ["yT"])

    def attn_pair(seq, hp):
        tok0, T, cond, segs = seq
        pidx = segs[0][2]
        phase()
        BS = min(512, T)
        NBk = T // BS
        NT = T // 128
        sample = pidx < 0
        w, wk = wget(("at", tok0, hp), attn_loads(hp))
        QT = LA.bf16(T)
        KT = LA.bf16(T)
        gs = LA.bf16(T)
        Ve = LA.bf16(NT * 256).rearrange("p (i h m) -> p i h m", h=2, m=128)
        Vo = LA.bf16(NT * 256).rearrange("p (i h m) -> p i h m", h=2, m=128) if sample else None
        memset(Ve, 1.0, [("Ve", i_, h_) for i_ in range(NT) for h_ in range(2)], eng="pool")
        if sample:
            memset(Vo, 1.0, [("Vo", i_, h_) for i_ in range(NT) for h_ in range(2)], eng="pool")
        if sample:
            kc = LA.bf16(2 * 128).rearrange("p (c m) -> p c m", c=2)
            vc = LA.bf16(2 * 128).rearrange("p (c m) -> p c m", c=2)
            Vc = LA.bf16(2 * 256).rearrange("p (c h m) -> p c h m", c=2, h=2)
            KcT = LA.bf16(256)
            pmh = [LA.f32(14 * 64).rearrange("p (d q) -> p d q", q=64) for _ in range(2)]
            memset(Vc, 1.0, [("Vc", c_, h_) for c_ in range(2) for h_ in range(2)], eng="pool")
            for hh in range(2):
                P.dma("pool", kc[:, :, hh * 64:(hh + 1) * 64], ck[2 * hp + hh].rearrange("(c p) d -> p c d", p=128), (), ["kc"])
                P.dma("pool", vc[:, :, hh * 64:(hh + 1) * 64], cv[2 * hp + hh].rearrange("(c p) d -> p c d", p=128), (), ["vc"])
                P.dma("sp", pmh[hh], pm[:, (2 * hp + hh) * 896:(2 * hp + hh + 1) * 896].rearrange("p (d q) -> p d q", q=64), (), [("pmh", hh)])
        for b in range(NBk):
            sl = slice(b * BS, (b + 1) * BS)
            bk, bkey = nb("proj", [0, 1])
            proj_fm(bk, bkey, w, wk, 0, b * BS, BS)
            act(QT[:, sl], bk[:, 0:BS], AF.Identity, [bkey], ["QT"], scale=0.125)
            bk, bkey = nb("proj", [0, 1])
            proj_fm(bk, bkey, w, wk, 128, b * BS, BS)
            cp(KT[:, sl], bk[:, 0:BS], [bkey], ["KT"])
            bk, bkey = nb("proj", [0, 1])
            proj_fm(bk, bkey, w, wk, 384, b * BS, BS)
            act(gs[:, sl], bk[:, 0:BS], AF.Silu, [bkey], ["gs"])

        VT = LA.bf16(T)
        for b in range(NBk):
            sl = slice(b * BS, (b + 1) * BS)
            bk, bkey = nb("proj", [0, 1])
            proj_fm(bk, bkey, w, wk, 256, b * BS, BS)
            if bkey[1] == 0:
                act(VT[:, sl], bk[:, 0:BS], AF.Identity, [bkey], ["VT"])
            else:
                cp(VT[:, sl], bk[:, 0:BS], [bkey], ["VT"])

        def vtrans(dst, dname, ntile, off):
            for g0 in range(0, ntile, 4):
                ng = min(4, ntile - g0)
                bt, btkey = nb("tp", [6, 7])
                btb = bt.bitcast(BF16)
                for j in range(ng):
                    t0 = off + (g0 + j) * 128
                    tr(btb[:, j * 128:(j + 1) * 128], VT[:, t0:t0 + 128], ident_b, ["VT", "cbf"], [btkey])
                src3 = btb[:, 0:ng * 128].rearrange("p (j m) -> p j m", m=128)
                eng_ = "act" if btkey[1] == 6 else "dve"
                cp(dst[:, g0:g0 + ng, 0, 0:64], src3[:, :, 0:64], [btkey], [(dname, g0 + j, 0) for j in range(ng)], eng=eng_)
                cp(dst[:, g0:g0 + ng, 1, 64:128], src3[:, :, 64:128], [btkey], [(dname, g0 + j, 1) for j in range(ng)], eng=eng_)

        vtrans(Ve, "Ve", NT, 0)
        if sample:
            vtrans(Vo, "Vo", NT - 1, 64)
        if sample:
            bt, btkey = nb("tp", [6])
            btb = bt.bitcast(BF16)
            for c in range(2):
                tr(btb[:, c * 128:(c + 1) * 128], kc[:, c, :], ident_b, ["kc", "cbf"], [btkey])
            cp(KcT, btb[:, 0:256], [btkey], ["KcT"])
            for c in range(2):
                cp(Vc[:, c, 0, 0:64], vc[:, c, 0:64], ["vc"], [("Vc", c, 0)])
                cp(Vc[:, c, 1, 64:128], vc[:, c, 64:128], ["vc"], [("Vc", c, 1)], eng="act")
            KsrcT, Vsrc, kkeys = KcT, Vc, ["KcT", "Vc"]
        else:
            KsrcT, Vsrc, kkeys = KT, Ve, ["KT", "Ve"]
        Pc = [LA.bf16(2 * 512).rearrange("p (c q) -> p c q", c=2) for _ in range(2)]
        NS = 3
        sbt = [LA.f32(256) for _ in range(NS)]
        plt = [LA.bf16(256).rearrange("p (a q) -> p a q", q=64) for _ in range(NS)]
        rden = [LA.f32(512) for _ in range(2)]
        tnum = [LA.f32(512) for _ in range(2)]
        if sample:
            groups = [(hh, slice(G * BS, (G + 1) * BS), BS, 0, G) for hh in range(2) for G in range(NBk)]
        else:
            groups = [(hh, slice(t0, t0 + Ts), Ts, t0 // 128, 0) for hh in range(2) for (t0, Ts, sp_) in segs]
        gstate = {}

        def ctx_front(gi):
            hh, qsl, nq, kb, G = groups[gi]
            hs = slice(hh * 64, (hh + 1) * 64)
            pcb = Pc[gi % 2]
            pck = ("Pc", gi % 2)
            for c in range(2):
                bc, bckey = nb("ctx", [0, 1])
                mm(bc[:, 0:nq], KsrcT[hs, (kb + c) * 128:(kb + c + 1) * 128], QT[hs, qsl], True, True, [kkeys[0], "QT"], [bckey])
                act(pcb[:, c, 0:nq], bc[:, 0:nq], AF.Exp, [bckey], [pck])
            gstate[gi] = (pcb, pck)

        def ctx_pv(gi):
            hh, qsl, nq, kb, G = groups[gi]
            pcb, pck = gstate[gi]
            bo, bokey = nb("o", [3, 4])
            for c in range(2):
                vk = ("Vc", c, hh) if sample else ("Ve", kb + c, hh)
                mm(bo[:, 0:nq], Vsrc[:, kb + c, hh, :], pcb[:, c, 0:nq], c == 0, (not sample) and c == 1, [vk, pck], [bokey])
            gstate[gi] = (pcb, pck, bo, bokey)

        def finish(gi):
            hh, qsl, nq, kb, G = groups[gi]
            hs = slice(hh * 64, (hh + 1) * 64)
            ds = slice((1 - hh) * 64, (2 - hh) * 64)
            pcb, pck, bo, bokey = gstate[gi]
            rd, tn = rden[gi % 2], tnum[gi % 2]
            act(tn[hs, 0:nq], bo[ds, 0:nq], AF.Ln, [bokey], [("tnum", gi % 2)])
            act(rd[hs, 0:nq], tn[hs, 0:nq], AF.Exp, [("tnum", gi % 2)], [("rden", gi % 2)], scale=-1.0)
            tt(tn[hs, 0:nq], bo[hs, 0:nq], rd[hs, 0:nq], ALU.mult, [bokey, ("rden", gi % 2)], [("tnum", gi % 2)])
            tt(yT[hs, 4 + hp, qsl], tn[hs, 0:nq], gs[hs, qsl], ALU.mult, [("tnum", gi % 2), "gs"], [("yTa", hh)])

        rowl = [(gi, rl) for gi in range(len(groups)) for rl in range(8)] if sample else []
        rstate = {}

        def row_front(n):
            gi, rl = rowl[n]
            hh, qsl_, nq_, kb_, G = groups[gi]
            hs = slice(hh * 64, (hh + 1) * 64)
            r = G * 8 + rl
            rs = min(max(r - 4, 0), 24)
            j = n % NS
            bs_, bskey = nb("s", [2, 5, 7])
            for i in range(4):
                k0 = (rs + 2 * i) * 64
                mm(bs_[:, i * 64:(i + 1) * 64], KT[hs, k0:k0 + 128], QT[hs, r * 64:(r + 1) * 64], True, True,
                   ["KT", "QT"], [bskey])
            idx0 = rs - r + 7
            tt(sbt[j].rearrange("p (a q) -> p a q", q=64), bs_[:, 0:256].rearrange("p (a q) -> p a q", q=64),
               pmh[hh][:, idx0:idx0 + 7:2, :], ALU.add, [bskey, ("pmh", hh)], [("sbt", j)])
            act(plt[j], sbt[j].rearrange("p (a q) -> p a q", q=64), AF.Exp, [("sbt", j)], [("plt", j)])
            rstate[n] = (rs, j)

        def row_back(n):
            gi, rl = rowl[n]
            hh, qsl_, nq_, kb_, G = groups[gi]
            rs, j = rstate[n]
            pcb, pck, bo, bokey = gstate[gi]
            for i in range(4):
                kr = rs + 2 * i
                Vt = Ve[:, kr // 2, hh, :] if kr % 2 == 0 else Vo[:, (kr - 1) // 2, hh, :]
                vk = ("Ve", kr // 2, hh) if kr % 2 == 0 else ("Vo", (kr - 1) // 2, hh)
                mm(bo[:, rl * 64:(rl + 1) * 64], Vt, plt[j][:, i, :], False, (rl == 7 and i == 3),
                   [vk, ("plt", j)], [bokey])

        if sample:
            LOOK = 2
            ctx_front(0)
            for n in range(min(LOOK, len(rowl))):
                row_front(n)
            for n in range(len(rowl)):
                gi, rl = rowl[n]
                if rl == 0:
                    ctx_pv(gi)
                    if gi + 1 < len(groups):
                        ctx_front(gi + 1)
                if n + LOOK < len(rowl):
                    row_front(n + LOOK)
                row_back(n)
                if rl == 7:
                    finish(gi)
        else:
            for gi in range(len(groups)):
                ctx_front(gi)
                ctx_pv(gi)
                finish(gi)

    def kv_out(seq):
        tok0, T, cond, segs = seq
        pidx = segs[0][2]
        phase()
        wsrc = w_in_e.rearrange("(k p) n -> p k n", p=128)
        stage = [LA.f32(512) for _ in range(2)]
        n = 0
        for which, c0, dst in ((0, 3072, o_k), (1, 3584, o_v)):
            w, wk = wslot()
            wload(w, wk, wsrc, c0, 512)
            for i in range(T // 128):
                bk, bkey = nb("proj", [0, 1])
                for k in range(8):
                    mm(bk[:, :], hT[:, k, i * 128:(i + 1) * 128], w[:, k, 0:512], k == 0, k == 7, [wk, "hT"], [bkey])
                st = stage[n % 2]
                sk = ("stage", n % 2)
                n += 1
                cp(st, bk[:, :], [bkey], [sk], eng="act")
                sp_, il = [(sp2, i - t0 // 128) for (t0, Ts, sp2) in segs if t0 // 128 <= i < (t0 + Ts) // 128][0]
                P.dma("sp", dst[sp_, :, il * 128:(il + 1) * 128, :].rearrange("h t d -> t h d"),
                      st.rearrange("p (h d) -> p h d", d=64), [sk], [("okv", which, sp_, il)])

    def rg_frontA(seq, c, gwb, L, par, gate=True):
        tok0, T, cond, segs = seq
        BS = min(512, T)
        NBk = T // BS
        kx = L["kx"]
        w, wk = wget(("rg", tok0, c), rg_loads(c))
        if c + 1 < 8:
            wissue(("rg", tok0, c + 1), rg_loads(c + 1))
        L.setdefault("w", {})[c] = (w, wk)
        xb, xc, xcb, gs = L["xb"], L["xc"][par], L["xcb"][par], L["gs"][par]
        kxc, kxcb, kgs = ("xc", kx, par), ("xcb", kx, par), ("gs", kx, par)
        for b in range(NBk):
            sl = slice(b * BS, (b + 1) * BS)
            bk, bkey = nb("projx", [2, 3])
            proj_fm(bk, bkey, w, wk, 0, b * BS, BS)
            cp(xb[:, sl], bk[:, 0:BS], [bkey], [("xb", kx)])
        cw = vt[:, V_CW:V_CW + 32].rearrange("p (j c) -> p j c", j=4)
        cb_ = vt[:, V_CB + c:V_CB + c + 1]
        ts(xc[:, 0:T], xb[:, 0:T], cw[:, 2, c:c + 1], ALU.mult, [("xb", kx), "vt"], [kxc], s2=cb_, op1=ALU.add)
        for (t0, Ts, sp_) in segs:
            e_ = t0 + Ts
            stt(xc[:, t0 + 2:e_], xb[:, t0:e_ - 2], cw[:, 0, c:c + 1], xc[:, t0 + 2:e_], ALU.mult, ALU.add, [("xb", kx), kxc, "vt"], [kxc])
            stt(xc[:, t0 + 1:e_], xb[:, t0:e_ - 1], cw[:, 1, c:c + 1], xc[:, t0 + 1:e_], ALU.mult, ALU.add, [("xb", kx), kxc, "vt"], [kxc])
            stt(xc[:, t0:e_ - 1], xb[:, t0 + 1:e_], cw[:, 3, c:c + 1], xc[:, t0:e_ - 1], ALU.mult, ALU.add, [("xb", kx), kxc, "vt"], [kxc])
        cp(xcb[:, 0:T], xc[:, 0:T], [kxc], [kxcb], eng="pool" if L["pipe"] else "act")
        if gate:
            rg_gate(seq, c, L, par)

    def rg_gate(seq, c, L, par):
        tok0, T, cond, segs = seq
        BS = min(512, T)
        w, wk = L["w"][c]
        gs = L["gs"][par]
        for b in range(T // BS):
            sl = slice(b * BS, (b + 1) * BS)
            bk, bkey = nb("projx", [2, 3])
            proj_fm(bk, bkey, w, wk, 128, b * BS, BS)
            act(gs[:, sl], bk[:, 0:BS], AF.Silu, [bkey], [("gs", L["kx"], par)])

    def rg_frontB(seq, c, gwb, L, par, part="all"):
        tok0, T, cond, segs = seq
        BS = min(512, T)
        NBk = T // BS
        kx = L["kx"]
        xcb = L["xcb"][par]
        kxcb = ("xcb", kx, par)
        gb = vt[:, V_GB:V_GB + 32].rearrange("p (d g c) -> p d g c", d=2, g=2)

        def sig(d):
            aT, uT = L["aT"][d], L["uT"][d]
            for b in range(NBk):
                sl = slice(b * BS, (b + 1) * BS)
                bk, bkey = nb("proj", [0, 1])
                mm(bk[:, 0:BS], gwb[:, (d * 2 + 0) * 8 + c, :], xcb[:, sl], True, True, ["gwb", kxcb], [bkey])
                act(aT[:, sl], bk[:, 0:BS], AF.Sigmoid, [bkey, "vt"], [("aT", d, kx)], bias=gb[:, d, 0, c:c + 1])
                bk, bkey = nb("proj", [0, 1])
                mm(bk[:, 0:BS], gwb[:, (d * 2 + 1) * 8 + c, :], xcb[:, sl], True, True, ["gwb", kxcb], [bkey])
                act(uT[:, sl], bk[:, 0:BS], AF.Sigmoid, [bkey, "vt"], [("uT", d, kx)], bias=gb[:, d, 1, c:c + 1])

        def expo(d):
            aT, a2 = L["aT"][d], L["a2"][d]
            act(a2[:, 0:T], aT[:, 0:T], AF.Exp, [("aT", d, kx), "cc2"], [L["a2key"][d]], scale=cc2[:, d, c:c + 1])
            act(aT[:, 0:T], aT[:, 0:T], AF.Exp, [("aT", d, kx), "cc1"], [("aT", d, kx)], scale=cc1[:, d, c:c + 1])

        def sqr(d):
            a2 = L["a2"][d]
            act(a2[:, 0:T], a2[:, 0:T], AF.Sqrt, [L["a2key"][d]], [L["a2key"][d]], scale=-1.0, bias=1.0)

        if L["split"]:
            sig(0); sig(1); expo(0); expo(1); sqr(0); sqr(1)
        else:
            if part in ("all", "B1"):
                sig(0); sig(1)
            if part in ("all", "B2"):
                expo(0); sqr(0)
                rg_tail(seq, c, L, 0, par)
                expo(1); sqr(1)
                rg_tail(seq, c, L, 1, par)

    def rg_tail(seq, c, L, d, par):
        tok0, T, cond, segs = seq
        kx = L["kx"]
        sample = segs[0][2] < 0
        xc, hs = L["xc"][par], L["hs"]
        kxc = ("xc", kx, par)
        aT, uT, a2 = L["aT"][d], L["uT"][d], L["a2"][d]
        srg = vt[:, V_SRG:V_SRG + 16].rearrange("p (d c) -> p d c", d=2)
        H = T // 2
        for hi, (hf, eng_) in enumerate(((slice(0, H), "dve"), (slice(H, T), "pool"))):
            tt(uT[:, hf], uT[:, hf], a2[:, hf], ALU.mult, [("uT", d, kx), L["a2key"][d]], [("uT", d, kx, hi)], eng=eng_)
            tt(uT[:, hf], uT[:, hf], xc[:, hf], ALU.mult, [("uT", d, kx, hi), kxc], [("uT", d, kx, hi)], eng=eng_)
        init = srg[:, d, c:c + 1] if sample else 0.0
        ukeys = [("uT", d, kx, 0), ("uT", d, kx, 1), ("uT", d, kx)]
        for (t0, Ts, sp_) in segs:
            e_ = t0 + Ts
            if d == 0:
                scan(hs[0][:, t0:e_], aT[:, t0:e_], uT[:, t0:e_], init, [("aT", 0, kx), "vt"] + ukeys, [("hs", 0, kx)])
                if not sample:
                    cp(rgfin[:, sp_, 0, c:c + 1], hs[0][:, e_ - 1:e_], [("hs", 0, kx)], ["rgfin"])
            else:
                scan(hs[1][:, t0:e_][:, ::-1], aT[:, t0:e_][:, ::-1], uT[:, t0:e_][:, ::-1], init,
                     [("aT", 1, kx), "vt"] + ukeys, [L["hs1key"]])
                if not sample:
                    cp(rgfin[:, sp_, 1, c:c + 1], hs[1][:, t0:t0 + 1], [L["hs1key"]], ["rgfin"])

    def rg_back(seq, c, L, par):
        tok0, T, cond, segs = seq
        kx = L["kx"]
        hs, gs = L["hs"], L["gs"][par]
        kgs = ("gs", kx, par)
        if L["split"]:
            for d in range(2):
                rg_tail(seq, c, L, d, par)
        H = T // 2
        for hf, eng_ in ((slice(0, H), "dve"), (slice(H, T), "pool")):
            tt(hs[0][:, hf], hs[0][:, hf], hs[1][:, hf], ALU.add, [("hs", 0, kx), L["hs1key"]], [("hsum", kx, eng_)], eng=eng_)
            tt(yT[:, c, hf], hs[0][:, hf], gs[:, hf], ALU.mult, [("hsum", kx, eng_), kgs], [("yTc", c, eng_)], eng=eng_)

    def rglru_layer(seq):
        tok0, T, cond, segs = seq
        phase()
        small_ = T <= 512
        gwb = LA.bf16(32 * 128).rearrange("p (n j) -> p n j", j=128)
        P.dma("pool", gwb, gwd.rearrange("n i j -> i n j"), (), ["gwb"])
        if small_:
            sets = []
            for j in range(2):
                L = {"kx": j, "split": True, "pipe": False, "hs1key": ("hs", 1, j)}
                L["xb"] = LA.f32(T)
                L["xc"] = [LA.f32(T)]
                L["aT"] = [LA.f32(T), LA.f32(T)]
                L["uT"] = [LA.f32(T), LA.f32(T)]
                L["a2"] = [LA.f32(T), LA.f32(T)]
                L["a2key"] = [("a2", 0, j), ("a2", 1, j)]
                L["hs"] = [LA.f32(T), LA.f32(T)]
                L["xcb"] = [LA.bf16(T)]
                L["gs"] = [LA.bf16(T)]
                sets.append(L)
            rg_frontA(seq, 0, gwb, sets[0], 0)
            rg_frontB(seq, 0, gwb, sets[0], 0)
            for c in range(8):
                if c + 1 < 8:
                    rg_frontA(seq, c + 1, gwb, sets[(c + 1) % 2], 0)
                    rg_frontB(seq, c + 1, gwb, sets[(c + 1) % 2], 0)
                rg_back(seq, c, sets[c % 2], 0)
        else:
            L = {"kx": 0, "split": False, "pipe": True, "hs1key": ("a2", 0)}
            L["xb"] = LA.f32(T)
            L["xc"] = [LA.f32(T), LA.f32(T)]
            L["aT"] = [LA.f32(T), LA.f32(T)]
            L["uT"] = [LA.f32(T), LA.f32(T)]
            a2_ = LA.f32(T)
            L["a2"] = [a2_, a2_]
            L["a2key"] = [("a2", 0), ("a2", 0)]
            L["hs"] = [LA.f32(T), a2_]
            L["xcb"] = [LA.bf16(T), LA.bf16(T)]
            L["gs"] = [LA.bf16(T), LA.bf16(T)]
            rg_frontA(seq, 0, gwb, L, 0)
            for c in range(8):
                rg_frontB(seq, c, gwb, L, c % 2, part="B1")
                if c + 1 < 8:
                    rg_frontA(seq, c + 1, gwb, L, (c + 1) % 2, gate=False)
                rg_frontB(seq, c, gwb, L, c % 2, part="B2")
                if c + 1 < 8:
                    rg_gate(seq, c + 1, L, (c + 1) % 2)
                rg_back(seq, c, L, c % 2)

    def out_phase(l, seq, w_out):
        tok0, T, cond, segs = seq
        pidx = segs[0][2]
        phase()
        woutg = LA.bf16(8 * 1024).rearrange("p (k n) -> p k n", k=8)
        stg = [LA.f32(1024) for _ in range(2)]
        fg = LA.f32(1024)
        junk = LA.f32(1024)
        gate_t = LA.f32(1024)
        P.dma("sp", gate_t, gscr[l][:, cond * 1024:(cond + 1) * 1024], (), ["gate_t"])
        if l == 1:
            P.dma("sp", fg, rows[:, 2048:3072], (), ["fg"])
        for k in range(8):
            s_ = stg[k % 2]
            sk = ("stg", k % 2)
            P.dma("sp", s_, w_out[k * 128:(k + 1) * 128, :], (), [sk])
            tt(woutg[:, k, :], s_, gate_t, ALU.mult, [sk, "gate_t"], [("woutg", k)])
        src = xall if l == 0 else xres
        xn = [LA.bf16(1024) for _ in range(2)]
        stats = LA.f32(2 * (T // 128))
        nfr = {}

        def nfront(i, xt, xk):
            ssq = stats[:, 2 * i:2 * i + 1]
            rstd = stats[:, 2 * i + 1:2 * i + 2]
            sk = ("stats", i)
            act(junk, xt, AF.Square, [xk], [sk], accum=ssq)
            act(ssq, ssq, AF.Sqrt, [sk], [sk], scale=1.0 / 1024, bias=EPS)
            recip(rstd, ssq, [sk], [sk])
            xnb = xn[i % 2]
            xnk = ("xn", i % 2)
            ts(xnb, xt, rstd, ALU.mult, [xk, sk], [xnk])

        def nback(i):
            xnb = xn[i % 2]
            xnk = ("xn", i % 2)
            bk, bkey = nb("tp", [6, 7])
            bkb = bk.bitcast(BF16)
            for k in range(8):
                tr(bkb[:, k * 128:(k + 1) * 128], xnb[:, k * 128:(k + 1) * 128], ident_b, [xnk, "cbf"], [bkey])
            for k in range(8):
                dst_ = hT[:, k, i * 128:(i + 1) * 128]
                if bkey[1] == 6:
                    act(dst_, bkb[:, k * 128:(k + 1) * 128], AF.Identity, [bkey, "Amod", "fm"], [("hTt", i, k)],
                        scale=Amod[:, 1, cond, k:k + 1], bias=fm[:, 1, k, cond:cond + 1])
                else:
                    ts(dst_, bkb[:, k * 128:(k + 1) * 128], Amod[:, 1, cond, k:k + 1], ALU.mult,
                       [bkey, "Amod", "fm"], [("hTt", i, k)], s2=fm[:, 1, k, cond:cond + 1], op1=ALU.add)

        for jp in range(T // 256):
            rsl = slice(tok0 + jp * 256, tok0 + (jp + 1) * 256)
            x2, xk = xslot()
            P.dma("pool", x2, src[rsl, :].rearrange("(j p) n -> p j n", p=128), (), [xk])
            for t_ in range(2):
                i = jp * 2 + t_
                xt = x2[:, t_, :]
                for half in range(2):
                    bk, bkey = nb("proj", [0, 1])
                    for k in range(8):
                        mm(bk[:, :], yT[:, k, i * 128:(i + 1) * 128], woutg[:, k, half * 512:(half + 1) * 512], k == 0, k == 7,
                           ["yT", ("woutg", k)], [bkey])
                    tt(xt[:, half * 512:(half + 1) * 512], bk[:, :], xt[:, half * 512:(half + 1) * 512], ALU.add, [bkey, xk], [xk])
                if l == 1:
                    ssq = small[:, 8 + 2 * (i % 2):8 + 2 * (i % 2) + 1]
                    rstd = small[:, 9 + 2 * (i % 2):9 + 2 * (i % 2) + 1]
                    sk = ("small2", i % 2)
                    act(junk, xt, AF.Square, [xk], [sk], accum=ssq)
                    act(ssq, ssq, AF.Sqrt, [sk], [sk], scale=1.0 / 1024, bias=EPS)
                    recip(rstd, ssq, [sk], [sk])
                    stt(xt, xt, rstd, fg, ALU.mult, ALU.mult, [xk, sk, "fg"], [xk])
                else:
                    nfront(i, xt, xk)
                    if i > 0:
                        nback(i - 1)
            dst = xres if l == 0 else y_all
            P.dma("sp", dst[rsl, :].rearrange("(j p) n -> p j n", p=128), x2, [xk], [("xo", l, tok0, jp)])
        if l == 0:
            nback(T // 128 - 1)

    plan = []
    plan.append((lambda: mod_phase(0), None))
    plan.append((lambda: mod_phase(1), None))
    for seq in SEQS:
        plan.append((lambda seq=seq: norm_phase(0, seq), None))
        for h in range(4):
            plan.append((lambda seq=seq, h=h: hgrn_head(seq, h), (("hg", seq[0], h), hgrn_loads(h))))
        for hp in range(4):
            plan.append((lambda seq=seq, hp=hp: attn_pair(seq, hp), (("at", seq[0], hp), attn_loads(hp))))
        if seq[3][0][2] >= 0:
            plan.append((lambda seq=seq: kv_out(seq), None))
        plan.append((lambda seq=seq: out_phase(0, seq, w_out_e), None))
        plan.append((lambda seq=seq: rglru_layer(seq), (("rg", seq[0], 0), rg_loads(0))))
        plan.append((lambda seq=seq: out_phase(1, seq, w_out_o), None))
    for j, (fn, wl) in enumerate(plan):
        if j + 1 < len(plan) and plan[j + 1][1] is not None:
            nid, nloads = plan[j + 1][1]
            PF[0] = (lambda nid=nid, nloads=nloads: wissue(nid, nloads))
        fn()
    phase()
    for s_ in range(2):
        for d in range(2):
            P.dma("sp", o_rg[s_, d].rearrange("(c p o) -> p c o", p=128, o=1), rgfin[:, s_, d, :].unsqueeze(2), ["rgfin"], [("o_rg", s_, d)],
                  allow_slow_non_contiguous=True)
    info = P.finalize()
    return nc, info


_CACHE = {}


def _host_consts():
    c = np.zeros((128, 1536), np.float32)
    c[:, 0:128] = np.eye(128, dtype=np.float32)
    c[:, 128:256] = 1.0
    s = np.arange(128)[:, None]
    t = np.arange(128)[None, :]
    same = (s // 32) == (t // 32)
    c[:, 256:384] = (same & (s <= t)).astype(np.float32)
    c[:, 384:512] = (same & (s >= t)).astype(np.float32)
    tt_ = np.arange(512)
    c[:, 512:1024] = (tt_ % 32 != 0).astype(np.float32)[None, :]
    c[:, 1024:1536] = (tt_ % 32 != 31).astype(np.float32)[None, :]
    return c


def _bias_table(rel_bias):
    kc = np.arange(64)[:, None]
    qc = np.arange(64)[None, :]
    ws = np.clip(qc - 8, 0, 48)
    valid = (kc >= ws) & (kc < ws + 16)
    dc = np.clip(kc - qc + 15, 0, 30)
    M = np.full((8, 16, 64, 64), -1e30, np.float32)
    for dd in range(-7, 8):
        g = rel_bias[:, dd + 7][:, dc]
        M[:, dd + 7] = np.where(valid[None], g, np.float32(-1e30))
    PMt = np.empty((8, 14, 128, 64), np.float32)
    for idx in range(14):
        PMt[:, idx, 0:64] = M[:, idx]
        PMt[:, idx, 64:128] = M[:, idx + 1]
    return np.ascontiguousarray(PMt.transpose(2, 0, 1, 3).reshape(128, 8 * 14 * 64))


def fm(v):
    v = np.asarray(v, np.float32)
    lead = v.shape[:-1]
    n = v.shape[-1] // 128
    return np.moveaxis(v.reshape(lead + (n, 128)), -1, 0)


def kernel(x_prompt, x_sample, state_hgrn, cache_na_k, cache_na_v, state_rglru, c, c_ctx, norm_gain,
           w_mod, b_mod, w_in_even, w_out_even, hgrn_lb_logits, hgrn_out_gain, na_rel_bias,
           w_in_odd, w_out_odd, conv_w, conv_b, rg_gate_w, rg_gate_b, rg_lambda, final_gain):
    f = lambda a: np.ascontiguousarray(np.asarray(a, np.float32))
    if "nc" not in _CACHE:
        _CACHE["nc"] = build_nc()
    nc, info = _CACHE["nc"]
    consts = _host_consts()
    pmt = _bias_table(f(na_rel_bias)[0])
    rows = np.empty((128, 3072), np.float32)
    rows[:, 0:1024] = f(b_mod)[0, 2048:3072][None]
    rows[:, 1024:2048] = f(b_mod)[1, 2048:3072][None]
    rows[:, 2048:3072] = f(final_gain)[None]
    in_maps = []
    for b in range(8):
        vecs = np.zeros((128, NV), np.float32)
        cond2 = np.stack([f(c)[b], f(c_ctx)], 0)
        vecs[:, V_COND:V_COND + 16] = fm(cond2).transpose(0, 2, 1).reshape(128, 16)
        vecs[:, V_NG:V_NG + 16] = fm(norm_gain).reshape(128, 16)
        vecs[:, V_BMOD:V_BMOD + 48] = fm(b_mod).reshape(128, 48)
        vecs[:, V_LBL:V_LBL + 16] = fm(hgrn_lb_logits).reshape(128, 16)
        vecs[:, V_OG:V_OG + 4] = f(hgrn_out_gain)[0].T
        vecs[:, V_CW:V_CW + 32] = fm(f(conv_w)[0]).reshape(128, 32)
        vecs[:, V_CB:V_CB + 8] = fm(f(conv_b)[0]).reshape(128, 8)
        vecs[:, V_GB:V_GB + 32] = fm(f(rg_gate_b)[0]).reshape(128, 32)
        vecs[:, V_LAM:V_LAM + 16] = fm(f(rg_lambda)[0]).reshape(128, 16)
        vecs[:, V_SRG:V_SRG + 16] = fm(f(state_rglru)[b, 0]).reshape(128, 16)
        xall = np.concatenate([f(x_sample)[b], f(x_prompt)[2 * b], f(x_prompt)[2 * b + 1]], 0)
        in_maps.append({
            "xall": np.ascontiguousarray(xall),
            "st_h": f(state_hgrn)[b, 0], "ck": f(cache_na_k)[b, 0], "cv": f(cache_na_v)[b, 0],
            "vecs": vecs, "rows": rows, "consts": consts, "pm": pmt,
            "w_mod": f(w_mod), "w_in_e": f(w_in_even)[0], "w_out_e": f(w_out_even)[0],
            "w_in_o": f(w_in_odd)[0], "w_out_o": f(w_out_odd)[0],
            "gw": f(rg_gate_w)[0].reshape(32, 128, 128),
        })
    res = run_bass_kernel_spmd(nc, in_maps, core_ids=list(range(8)))
    R = res.results
    y_sample = np.stack([R[b]["y_all"][0:2048] for b in range(8)], 0)
    y_prompt = np.stack([R[b]["y_all"][2048 + 256 * s:2048 + 256 * (s + 1)] for b in range(8) for s in range(2)], 0)
    n_sh = np.concatenate([R[b]["o_sh"] for b in range(8)], 0)[:, None]
    n_k = np.concatenate([R[b]["o_k"] for b in range(8)], 0)[:, None]
    n_v = np.concatenate([R[b]["o_v"] for b in range(8)], 0)[:, None]
    n_rg = np.concatenate([R[b]["o_rg"] for b in range(8)], 0)[:, None]
    return (y_prompt.astype(np.float32), y_sample.astype(np.float32), n_sh.astype(np.float32),
            n_k.astype(np.float32), n_v.astype(np.float32), n_rg.astype(np.float32))
```

```python
import numpy as np
import concourse.bass as bass
import concourse.mybir as mybir
from concourse.bass_utils import run_bass_kernel_spmd

F32 = mybir.dt.float32
BF16 = mybir.dt.bfloat16
AF = mybir.ActivationFunctionType
ALU = mybir.AluOpType
ENGS = ("pe", "act", "dve", "pool", "sp")
EPS = 1e-6


class Prog:
    def __init__(self, nc, n_dma_sems=24):
        self.nc = nc
        self.ops = []
        self.lastw = {}
        self.readers = {}
        self.n_dma_sems = n_dma_sems
        self.last_eng = {}
        self.open_dmas = []
        self.pending_bar = {}

    def op(self, eng, fn, reads=(), writes=(), dma=False):
        idx = len(self.ops)
        deps = set()
        for r in reads:
            w = self.lastw.get(r)
            if w is not None:
                deps.add(w)
        for w_ in writes:
            w = self.lastw.get(w_)
            if w is not None:
                deps.add(w)
            rd = self.readers.get(w_)
            if rd:
                for k, v in rd.items():
                    if k == "dma":
                        deps.update(v)
                    else:
                        deps.add(v)
        for r in reads:
            rd = self.readers.setdefault(r, {})
            if dma:
                rd.setdefault("dma", []).append(idx)
            else:
                rd[eng] = idx
        for w_ in writes:
            self.lastw[w_] = idx
            self.readers[w_] = {}
        pb = self.pending_bar.pop(eng, None)
        if pb is not None:
            deps.add(pb)
        self.ops.append(dict(eng=eng, fn=fn, deps=deps, dma=dma))
        if dma:
            self.open_dmas.append(idx)
        else:
            self.last_eng[eng] = idx
        return idx

    def dma(self, q, out, in_, reads=(), writes=(), **kw):
        return self.op(q, lambda e: e.dma_start(out=out, in_=in_, **kw), reads, writes, dma=True)

    def barrier(self, scratch):
        deps = set(self.last_eng.values()) | set(self.open_dmas)
        idx = len(self.ops)
        self.ops.append(dict(eng="dve", fn=lambda e: e.memset(scratch, 0.0), deps=deps, dma=False))
        self.open_dmas = []
        self.last_eng = {"dve": idx}
        self.lastw = {"__bar__": idx}
        self.readers = {}
        self.bar = idx
        self.pending_bar = {e: idx for e in ENGS}
        return idx

    def finalize(self):
        nc = self.nc
        ops = self.ops
        needed = [False] * len(ops)
        for i, o in enumerate(ops):
            for d in o["deps"]:
                po = ops[d]
                if po["dma"]:
                    continue
                if po["eng"] == "pe" and o["eng"] == "pe" and not o["dma"]:
                    continue
                needed[d] = True
        cnt = {e: 0 for e in ENGS}
        for i, o in enumerate(ops):
            if o["dma"]:
                continue
            if needed[i]:
                cnt[o["eng"]] += 1
                o["seq"] = cnt[o["eng"]]
            else:
                o["seq"] = None
        half = self.n_dma_sems // 2
        kk = {True: 0, False: 0}
        dcount = [0] * self.n_dma_sems
        for i, o in enumerate(ops):
            if o["dma"]:
                sw = o["eng"] == "pool"
                k = kk[sw] + (half if sw else 0)
                kk[sw] = (kk[sw] + 1) % half
                o["dsem"] = k
                dcount[k] += 1
                o["dval"] = 16 * dcount[k]
        sems = {e: nc.alloc_semaphore(name=f"s_{e}") for e in ENGS}
        dsems = [nc.alloc_semaphore(name=f"s_dma{j}") for j in range(self.n_dma_sems)]
        by_eng = {e: [i for i, o in enumerate(ops) if o["eng"] == e] for e in ENGS}

        def emit_engine(ename, eh):
            waited = {}

            def wait(sem, val, key):
                if waited.get(key, 0) >= val:
                    return
                eh.wait_ge(sem, val)
                waited[key] = val

            for i in by_eng[ename]:
                o = ops[i]
                for d in sorted(o["deps"]):
                    po = ops[d]
                    if po["dma"]:
                        wait(dsems[po["dsem"]], po["dval"], ("d", po["dsem"]))
                    else:
                        if po["eng"] == "pe" and ename == "pe" and not o["dma"]:
                            continue
                        wait(sems[po["eng"]], po["seq"], ("e", po["eng"]))
                if o["dma"]:
                    if o["dval"] > 16:
                        wait(dsems[o["dsem"]], o["dval"] - 16, ("d", o["dsem"]))
                    ins = o["fn"](eh)
                    ins.then_inc(dsems[o["dsem"]], 16)
                else:
                    ins = o["fn"](eh)
                    if o["seq"] is not None:
                        ins.then_inc(sems[ename], 1)
            if ename == "sp":
                for j in range(self.n_dma_sems):
                    if dcount[j] > 0:
                        eh.wait_ge(dsems[j], 16 * dcount[j])
                for e in ENGS:
                    if cnt[e] > 0:
                        eh.wait_ge(sems[e], cnt[e])

        with nc.Block() as block:
            @block.tensor
            def _(e):
                emit_engine("pe", e)

            @block.scalar
            def _(e):
                emit_engine("act", e)

            @block.vector
            def _(e):
                emit_engine("dve", e)

            @block.gpsimd
            def _(e):
                emit_engine("pool", e)

            @block.sync
            def _(e):
                emit_engine("sp", e)
        return dict(n_ops=len(ops), sig=dict(cnt), ndma=sum(dcount))


class Arena:
    def __init__(self, nc, name, nfloats):
        self.t = nc.alloc_sbuf_tensor(name, [128, nfloats], F32).ap()
        self.n = nfloats
        self.off = 0
        self.name = name
        self.gen = 0

    def f32(self, n):
        n = (n + 1) // 2 * 2
        assert self.off + n <= self.n, (self.name, self.off, n, self.n)
        v = self.t[:, self.off:self.off + n]
        self.off += n
        return v

    def bf16(self, n):
        m = (n + 3) // 4 * 2
        assert self.off + m <= self.n, (self.name, self.off, m, self.n)
        v = self.t[:, self.off:self.off + m].bitcast(BF16)
        self.off += m
        return v[:, 0:n]

    def reset(self):
        self.off = 0
        self.gen += 1


V_COND, V_NG, V_BMOD, V_LBL, V_OG, V_CW, V_CB, V_GB, V_LAM, V_SRG, NV = 0, 16, 32, 80, 96, 100, 132, 140, 172, 188, 204
SEQS = [(0, 2048, 0, [(0, 2048, -1)]), (2048, 512, 1, [(0, 256, 0), (256, 256, 1)])]


def build_nc():
    nc = bass.Bass("TRN2", target_bir_lowering=False)

    def din(name, shape):
        return nc.dram_tensor(name, list(shape), F32, kind="ExternalInput").ap()

    def dout(name, shape):
        return nc.dram_tensor(name, list(shape), F32, kind="ExternalOutput").ap()

    xall = din("xall", [2560, 1024])
    st_h = din("st_h", [2, 4, 128, 128])
    ck = din("ck", [8, 256, 64])
    cv = din("cv", [8, 256, 64])
    vecs = din("vecs", [128, NV])
    rows = din("rows", [128, 3072])
    consts = din("consts", [128, 1536])
    pm = din("pm", [128, 8 * 14 * 64])
    w_mod = din("w_mod", [2, 1024, 3072])
    w_in_e = din("w_in_e", [1024, 4608])
    w_out_e = din("w_out_e", [1024, 1024])
    w_in_o = din("w_in_o", [1024, 2048])
    w_out_o = din("w_out_o", [1024, 1024])
    gwd = din("gw", [32, 128, 128])
    y_all = dout("y_all", [2560, 1024])
    o_sh = dout("o_sh", [2, 2, 4, 128, 128])
    o_k = dout("o_k", [2, 8, 256, 64])
    o_v = dout("o_v", [2, 8, 256, 64])
    o_rg = dout("o_rg", [2, 2, 1024])
    xres = nc.dram_tensor("xres", [2560, 1024], F32, kind="Internal").ap()
    gscr = nc.dram_tensor("gscr", [2, 128, 2048], F32, kind="Internal").ap()

    P = Prog(nc)
    banks = [nc.alloc_psum_tensor(f"bank{i}", [128, 512], F32).ap() for i in range(8)]
    rr = {}

    def nb(group, ids):
        j = rr.get(group, 0)
        rr[group] = j + 1
        b = ids[j % len(ids)]
        return banks[b], ("bank", b)

    def mm(out, lhsT, rhs, start, stop, r, w, tp=None):
        kw = {"tile_position": tp} if tp is not None else {}
        P.op("pe", lambda e: e.matmul(out, lhsT=lhsT, rhs=rhs, start=start, stop=stop, **kw), r, w)

    def tr(out, in_, ident, r, w):
        P.op("pe", lambda e: e.transpose(out=out, in_=in_, identity=ident), r, w)

    def act(out, in_, func, r, w, scale=None, bias=None, accum=None):
        kw = {}
        if scale is not None:
            kw["scale"] = scale
        if bias is not None:
            kw["bias"] = bias
        if accum is not None:
            kw["accum_out"] = accum
        P.op("act", lambda e: e.activation(out=out, in_=in_, func=func, **kw), r, w)

    def tt(out, in0, in1, op, r, w, eng="dve"):
        P.op(eng, lambda e: e.tensor_tensor(out=out, in0=in0, in1=in1, op=op), r, w)

    def ts(out, in0, s1, op0, r, w, s2=None, op1=None, eng="dve"):
        if op1 is None:
            P.op(eng, lambda e: e.tensor_scalar(out=out, in0=in0, scalar1=s1, scalar2=None, op0=op0), r, w)
        else:
            P.op(eng, lambda e: e.tensor_scalar(out=out, in0=in0, scalar1=s1, scalar2=s2, op0=op0, op1=op1), r, w)

    def stt(out, in0, scalar, in1, op0, op1, r, w):
        P.op("dve", lambda e: e.scalar_tensor_tensor(out=out, in0=in0, scalar=scalar, in1=in1, op0=op0, op1=op1), r, w)

    def cp(out, in_, r, w, eng="dve"):
        if eng == "act":
            P.op("act", lambda e: e.activation(out=out, in_=in_, func=AF.Copy), r, w)
        else:
            P.op(eng, lambda e: e.tensor_copy(out=out, in_=in_), r, w)

    def recip(out, in_, r, w):
        P.op("dve", lambda e: e.reciprocal(out=out, in_=in_), r, w)

    def recipf(out, in_, r, w):
        P.op("dve", lambda e: e.reciprocal_approx_fast(out=out, in_=in_), r, w)

    def scan(out, d0, d1, initial, r, w):
        P.op("dve", lambda e: e.tensor_tensor_scan(out=out, data0=d0, data1=d1, initial=initial,
                                                   op0=ALU.mult, op1=ALU.add), r, w)

    def memset(ap, val, w, eng="dve"):
        P.op(eng, lambda e: e.memset(ap, val), (), w)

    PA = Arena(nc, "persist", 27900)
    LA = Arena(nc, "phase", 25200)
    barscr = PA.f32(2)
    hT = PA.bf16(8 * 2048).rearrange("p (k t) -> p k t", k=8)
    yT = PA.bf16(8 * 2048).rearrange("p (k t) -> p k t", k=8)
    cst = PA.f32(1536)
    ident_f, ones_f, maskF_f, maskB_f = cst[:, 0:128], cst[:, 128:256], cst[:, 256:384], cst[:, 384:512]
    m01 = [cst[:, 512:1024], cst[:, 1024:1536]]
    cbf = PA.bf16(384)
    ident_b, maskF_b, maskB_b = cbf[:, 0:128], cbf[:, 128:256], cbf[:, 256:384]
    vt = PA.f32(NV)
    siluc = PA.f32(16).rearrange("p (k c) -> p k c", c=2)
    lb = PA.f32(8).rearrange("p (d h) -> p d h", d=2)
    oml = PA.f32(8).rearrange("p (d h) -> p d h", d=2)
    noml = PA.f32(8).rearrange("p (d h) -> p d h", d=2)
    cc1 = PA.f32(16).rearrange("p (d c) -> p d c", d=2)
    cc2 = PA.f32(16).rearrange("p (d c) -> p d c", d=2)
    fm = PA.f32(96).rearrange("p (l j c) -> p l j c", l=2, j=24)
    Amod = PA.f32(32).rearrange("p (l c k) -> p l c k", l=2, c=2)
    small = PA.f32(64)
    rgfin = PA.f32(32).rearrange("p (s d c) -> p s d c", s=2, d=2)
    wbuf = [PA.bf16(8 * 640).rearrange("p (k n) -> p k n", k=8) for _ in range(2)]
    xin = [PA.f32(2048).rearrange("p (j n) -> p j n", j=2) for _ in range(2)]
    wrr = [0]

    def wslot():
        j = wrr[0] % 2
        wrr[0] += 1
        return wbuf[j], ("wbuf", j)

    xrr = [0]

    def xslot():
        j = xrr[0] % 2
        xrr[0] += 1
        return xin[j], ("xin", j)

    wstate = {}
    PF = [None]

    def wissue(wid, loads):
        w, wk = wslot()
        for (src, c0, n, off) in loads:
            wload(w, wk, src, c0, n, off)
        wstate[wid] = (w, wk)

    def wget(wid, loads):
        if wid not in wstate:
            wissue(wid, loads)
        return wstate.pop(wid)

    def hgrn_loads(h):
        wsrc = w_in_e.rearrange("(k p) n -> p k n", p=128)
        return [(wsrc, c0, 128, j * 128) for j, c0 in
                enumerate([h * 128, 512 + h * 128, 1024 + h * 128, 1536 + h * 128, 2048 + h * 128])]

    def attn_loads(hp):
        wsrc = w_in_e.rearrange("(k p) n -> p k n", p=128)
        return [(wsrc, c0, 128, j * 128) for j, c0 in
                enumerate([2560 + hp * 128, 3072 + hp * 128, 3584 + hp * 128, 4096 + hp * 128])]

    def rg_loads(c):
        wsrc = w_in_o.rearrange("(k p) n -> p k n", p=128)
        return [(wsrc, c * 128, 128, 0), (wsrc, 1024 + c * 128, 128, 128)]

    def phase():
        P.barrier(barscr)
        LA.reset()
        if PF[0] is not None:
            f_ = PF[0]
            PF[0] = None
            f_()

    P.dma("sp", cst, consts, (), ["cst"])
    P.dma("sp", vt, vecs, (), ["vt"])
    cp(cbf[:, 0:128], ident_f, ["cst"], ["cbf"])
    cp(cbf[:, 128:384], cst[:, 256:512], ["cst"], ["cbf"])
    condT = vt[:, V_COND:V_COND + 16].rearrange("p (k c) -> p k c", c=2)
    act(siluc, condT, AF.Silu, ["vt"], ["siluc"])
    lbl = vt[:, V_LBL:V_LBL + 16].rearrange("p (d l h) -> p d l h", d=2, l=2)
    tt(lb, lbl[:, :, 0, :], lbl[:, :, 1, :], ALU.subtract, ["vt"], ["lb"])
    act(lb, lb, AF.Sigmoid, ["lb"], ["lb"])
    ts(oml, lb, -1.0, ALU.mult, ["lb"], ["oml"], s2=1.0, op1=ALU.add)
    ts(noml, oml, -1.0, ALU.mult, ["oml"], ["noml"])
    lam = vt[:, V_LAM:V_LAM + 16].rearrange("p (d c) -> p d c", d=2)
    act(cc1, lam, AF.Exp, ["vt"], ["cc1"], scale=-1.0)
    act(cc1, cc1, AF.Ln, ["cc1"], ["cc1"], bias=1.0)
    ts(cc2, cc1, -16.0, ALU.mult, ["cc1"], ["cc2"])
    ts(cc1, cc1, -8.0, ALU.mult, ["cc1", "cc2"], ["cc1"])
    ngain = vt[:, V_NG:V_NG + 16].rearrange("p (l k) -> p l k", l=2)
    bmod = vt[:, V_BMOD:V_BMOD + 48].rearrange("p (l j) -> p l j", l=2)

    def mod_phase(l):
        if l == 0:
            phase()
        gate_bc = LA.f32(2048).rearrange("p (c n) -> p c n", c=2)
        screp = LA.bf16(8 * 2 * 128).rearrange("p (k c m) -> p k c m", k=8, c=2)
        silub = LA.bf16(16).rearrange("p (k c) -> p k c", c=2)
        zer = LA.f32(128)
        rowt = LA.f32(1024)
        wm = [LA.bf16(8 * 512).rearrange("p (k n) -> p k n", k=8) for _ in range(3)]
        memset(zer, 0.0, ["zer"])
        cp(silub, siluc, ["siluc"], ["silub"])
        for k in range(8):
            for c in range(2):
                ts(screp[:, k, c, :], zer, siluc[:, k, c:c + 1], ALU.add, ["zer", "siluc"], ["screp"])
        P.dma("sp", rowt, rows[:, l * 1024:(l + 1) * 1024], (), ["rowt"])
        wsrc = w_mod[l].rearrange("(k p) n -> p k n", p=128)
        for cb in range(6):
            w = wm[cb % 3]
            wk = ("wm", cb % 3)
            P.dma("pool", w, wsrc[:, :, cb * 512:(cb + 1) * 512], (), [wk])
            for jj in range(4):
                j = cb * 4 + jj
                bk, bkey = nb("misc", [7, 6])
                for k in range(8):
                    mm(bk[:, 0:2], w[:, k, jj * 128:(jj + 1) * 128], silub[:, k, :], k == 0, k == 7,
                       [wk, "silub"], [bkey])
                ts(fm[:, l, j, :], bk[:, 0:2], bmod[:, l, j:j + 1], ALU.add, [bkey, "vt"], ["fm"])
            if cb >= 4:
                for c in range(2):
                    bk, bkey = nb("proj", [0, 1])
                    for k in range(8):
                        mm(bk[:, :], screp[:, k, c, :], w[:, k, :], k == 0, k == 7, [wk, "screp"], [bkey])
                    tt(gate_bc[:, c, (cb - 4) * 512:(cb - 3) * 512], bk[:, :],
                       rowt[:, (cb - 4) * 512:(cb - 3) * 512], ALU.add, [bkey, "rowt"], ["gate_bc"])
        for c in range(2):
            stt(Amod[:, l, c, :], fm[:, l, 8:16, c], 1.0, ngain[:, l, :], ALU.add, ALU.mult, ["fm", "vt"], ["Amod"])
        P.dma("sp", gscr[l], gate_bc.rearrange("p c n -> p (c n)"), ["gate_bc"], [("gscr", l)])

    def norm_phase(l, seq):
        tok0, T, cond, segs = seq
        pidx = segs[0][2]
        if l == 0 and tok0 == 0:
            if PF[0] is not None:
                f_ = PF[0]
                PF[0] = None
                f_()
        else:
            phase()
        NT = T // 128
        xn = [LA.bf16(1024) for _ in range(2)]
        junk = LA.f32(1024)
        stats = LA.f32(2 * NT)
        src = xall if l == 0 else xres
        fr = {}
        xp = {}

        def front(i):
            if i % 2 == 0:
                x2, xk = xslot()
                P.dma("sp", x2,
                      src[tok0 + i * 128: tok0 + (i + 2) * 128, :].rearrange("(j p) n -> p j n", p=128), (), [xk])
                xp[i // 2] = (x2, xk)
            x2, xk = xp[i // 2]
            xt = x2[:, i % 2, :]
            ssq = stats[:, 2 * i:2 * i + 1]
            rstd = stats[:, 2 * i + 1:2 * i + 2]
            sk = ("stats", i)
            act(junk, xt, AF.Square, [xk], [sk], accum=ssq)
            act(ssq, ssq, AF.Sqrt, [sk], [sk], scale=1.0 / 1024, bias=EPS)
            recip(rstd, ssq, [sk], [sk])
            xnb = xn[i % 2]
            xnk = ("xn", i % 2)
            ts(xnb, xt, rstd, ALU.mult, [xk, sk], [xnk])
            bk, bkey = nb("tp", [6, 7])
            bkb = bk.bitcast(BF16)
            for k in range(8):
                tr(bkb[:, k * 128:(k + 1) * 128], xnb[:, k * 128:(k + 1) * 128], ident_b, [xnk, "cbf"], [bkey])
            fr[i] = (bkb, bkey)

        def back(i):
            bkb, bkey = fr[i]
            for k in range(8):
                dst = hT[:, k, i * 128:(i + 1) * 128]
                if bkey[1] == 6:
                    act(dst, bkb[:, k * 128:(k + 1) * 128], AF.Identity, [bkey, "Amod", "fm"], [("hTt", i, k)],
                        scale=Amod[:, l, cond, k:k + 1], bias=fm[:, l, k, cond:cond + 1])
                else:
                    ts(dst, bkb[:, k * 128:(k + 1) * 128], Amod[:, l, cond, k:k + 1], ALU.mult,
                       [bkey, "Amod", "fm"], [("hTt", i, k)], s2=fm[:, l, k, cond:cond + 1], op1=ALU.add)

        front(0)
        for i in range(NT):
            if i + 1 < NT:
                front(i + 1)
            back(i)

    def proj_fm(bk, bkey, w, wk, c0, t0, n):
        for k in range(8):
            mm(bk[:, 0:n], w[:, k, c0:c0 + 128], hT[:, k, t0:t0 + n], k == 0, k == 7, [wk, "hT"], [bkey])

    def wload(dst, dkey, wsrc, c0, n, off=0):
        P.dma("pool", dst[:, :, off:off + n], wsrc[:, :, c0:c0 + n], (), [dkey])

    def hgrn_head(seq, h):
        tok0, T, cond, segs = seq
        pidx = segs[0][2]
        phase()
        BS = min(512, T)
        NBk = T // BS
        NT = T // 128
        wsrc = w_in_e.rearrange("(k p) n -> p k n", p=128)
        w, wk = wget(("hg", tok0, h), hgrn_loads(h))
        qdec = [LA.bf16(T) for _ in range(2)]
        kinv = [LA.bf16(T) for _ in range(2)]
        kd = [LA.bf16(T).rearrange("p (i d) -> p i d", d=128) for _ in range(2)]
        vsb = LA.bf16(T).rearrange("p (i d) -> p i d", d=128)
        vx = LA.bf16(4 * T).rearrange("p (i c d) -> p i c d", c=4, d=128)
        oT = LA.f32(T)
        dec = [LA.f32(64) for _ in range(2)]
        S32 = [[LA.f32(128) for _ in range(2)] for _ in range(2)]
        Sbf = [[LA.bf16(128) for _ in range(2)] for _ in range(2)]
        attb = [[LA.bf16(128) for _ in range(2)] for _ in range(2)]
        HB = min(1024, T)
        NHB = T // HB
        sets = [{n_: LA.f32(HB) for n_ in "AKBE"} for _ in range(2)]
        VT = oT[:, 0:T // 2].bitcast(BF16)
        for b in range(NBk):
            sl = slice(b * BS, (b + 1) * BS)
            bk, bkey = nb("proj", [0, 1])
            proj_fm(bk, bkey, w, wk, 384, b * BS, BS)
            cp(VT[:, sl], bk[:, 0:BS], [bkey], ["VT"], eng="act")
        for g0 in range(0, NT, 4):
            ng = min(4, NT - g0)
            bt, btkey = nb("tp", [6, 7])
            btb = bt.bitcast(BF16)
            for j in range(ng):
                tr(btb[:, j * 128:(j + 1) * 128], VT[:, (g0 + j) * 128:(g0 + j + 1) * 128], ident_b, ["VT", "cbf"], [btkey])
            cp(vsb[:, g0:g0 + ng, :], btb[:, 0:ng * 128].rearrange("p (j d) -> p j d", d=128), [btkey],
               [("vsb", g0 + j) for j in range(ng)], eng="act")
            for j in range(ng):
                i = g0 + j
                P.op("pool", lambda e, o_=vx[:, i, :, :], a_=vsb[:, i, :].unsqueeze(1).to_broadcast([128, 4, 128]),
                     b_=maskF_f[:, 31::32].unsqueeze(2).to_broadcast([128, 4, 128]):
                     e.tensor_tensor(out=o_, in0=a_, in1=b_, op=ALU.mult), [("vsb", i), "cst"], [("vx", i)])
        iters = [(d, hb) for d in range(2) for hb in range(NHB)]

        def stage1(n):
            d, hb = iters[n]
            si = n % 2
            st = sets[si]
            A, Kt, Bt, Et = (st[n_] for n_ in "AKBE")
            kA, kK, kB, kE = (("t" + n_, si) for n_ in "AKBE")
            t0 = hb * HB
            nbl = HB // BS
            for b in range(nbl):
                bk, bkey = nb("proj", [0, 1])
                proj_fm(bk, bkey, w, wk, 128 * (1 + d), t0 + b * BS, BS)
                act(A[:, b * BS:(b + 1) * BS], bk[:, 0:BS], AF.Sigmoid, [bkey], [kA])
            ts(Kt, A, noml[:, d, h:h + 1], ALU.mult, [kA, "noml", "oml"], [kK], s2=oml[:, d, h:h + 1], op1=ALU.add)
            act(A, A, AF.Ln, [kA, "oml", "lb"], [kA], scale=oml[:, d, h:h + 1], bias=lb[:, d, h:h + 1])
            for b in range(nbl):
                bsl = slice(b * BS, (b + 1) * BS)
                if d == 0:
                    scan(Bt[:, bsl], m01[0][:, 0:BS], A[:, bsl], 0.0, [kA, "cst"], [kB])
                else:
                    scan(Bt[:, bsl][:, ::-1], m01[1][:, 0:BS][:, ::-1], A[:, bsl][:, ::-1], 0.0, [kA, "cst"], [kB])
            act(Et, Bt, AF.Exp, [kB], [kE])
            act(Bt, Bt, AF.Exp, [kB, kE], [kB], scale=-1.0)

        def stage2(n):
            d, hb = iters[n]
            si = n % 2
            st = sets[si]
            A, Kt, Bt, Et = (st[n_] for n_ in "AKBE")
            kA, kK, kB, kE = (("t" + n_, si) for n_ in "AKBE")
            t0 = hb * HB
            nbl = HB // BS
            for b in range(nbl):
                bq, bqkey = nb("proj", [0, 1])
                proj_fm(bq, bqkey, w, wk, 0, t0 + b * BS, BS)
                tt(qdec[d][:, t0 + b * BS:t0 + (b + 1) * BS], bq[:, 0:BS], Et[:, b * BS:(b + 1) * BS], ALU.mult,
                   [bqkey, kE], [("qdec", d)])
            tt(Kt, Kt, Bt, ALU.mult, [kK, kB], [kK])
            cp(kinv[d][:, t0:t0 + HB], Kt, [kK], [("kinv", d)], eng="act")
            nch = HB // 32
            dsl = dec[d][:, t0 // 32:t0 // 32 + nch]
            cp(dsl, Et.rearrange("p (c i) -> p c i", i=32)[:, :, 31 if d == 0 else 0], [kE], [("dec", d)])
            kdT = A[:, 0:HB // 2].bitcast(BF16)
            tt(kdT.rearrange("p (c i) -> p c i", i=32), Kt.rearrange("p (c i) -> p c i", i=32),
               dsl.unsqueeze(2).to_broadcast([128, nch, 32]), ALU.mult, [kK, ("dec", d), kB], [kA])
            for b in range(nbl):
                bt, btkey = nb("tp", [6, 7])
                btb = bt.bitcast(BF16)
                for j in range(BS // 128):
                    tr(btb[:, j * 128:(j + 1) * 128], kdT[:, b * BS + j * 128:b * BS + (j + 1) * 128], ident_b,
                       [kA, "cbf"], [btkey])
                i0 = (t0 + b * BS) // 128
                cp(kd[d][:, i0:i0 + BS // 128, :],
                   btb[:, 0:BS].rearrange("p (j d) -> p j d", d=128), [btkey], [("kd", d)], eng="act")

        stage1(0)
        for n in range(len(iters)):
            if n + 1 < len(iters):
                stage1(n + 1)
            stage2(n)
        for d in range(2):
            if pidx < 0:
                P.dma("sp", S32[d][0], st_h[d, h], (), [("S32", d, 0)])
            else:
                memset(S32[d][0], 0.0, [("S32", d, 0)])
            cp(Sbf[d][0], S32[d][0], [("S32", d, 0)], [("Sbf", d, 0)], eng="act")
        seg_first = [{t0 // 128: sp_ for (t0, Ts, sp_) in segs}, {(t0 + Ts) // 128 - 1: sp_ for (t0, Ts, sp_) in segs}]
        seg_last = [{(t0 + Ts) // 128 - 1: sp_ for (t0, Ts, sp_) in segs}, {t0 // 128: sp_ for (t0, Ts, sp_) in segs}]
        fr = {}
        KB = [0, 1, 5, 6]
        kvs = [[LA.f32(512).rearrange("p (c d) -> p c d", d=128) for _ in range(2)] for _ in range(2)]

        def front(step):
            for d in range(2):
                i = step if d == 0 else NT - 1 - step
                tsl = slice(i * 128, (i + 1) * 128)
                ba, bakey = nb("att", [2])
                mm(ba[:, 0:128], kinv[d][:, tsl], qdec[d][:, tsl], True, True, [("kinv", d), ("qdec", d)], [bakey])
                ab = attb[d][step % 2]
                abk = ("attb", d, step % 2)
                tt(ab, ba[:, 0:128], maskF_f if d == 0 else maskB_f, ALU.mult, [bakey, "cst"], [abk])
                ks3 = kvs[d][step % 2]
                ksb = [ks3[:, c, :] for c in range(4)]
                kskey = [("kvs", d, step % 2)] * 4
                bkv_, bkvk = nb("kv", [5, 6])
                mm(bkv_[:, :], kd[d][:, i, :], vx[:, i, :, :].rearrange("p c d -> p (c d)"), True, True,
                   [("kd", d), ("vx", i)], [bkvk])
                cp(ks3, bkv_[:, :].rearrange("p (c d) -> p c d", d=128), [bkvk], [kskey[0]], eng="act")
                fr[(step, d)] = (i, tsl, ab, abk, ksb, kskey)

        sbi = [0, 0]

        def back(step):
            bos = []
            for d in range(2):
                i, tsl, ab, abk, bkv, bkvkey = fr[(step, d)]
                bo, bokey = nb("o", [3, 4])
                mm(bo[:, 0:128], vsb[:, i, :], ab, True, False, [("vsb", i), abk], [bokey])
                bos.append((bo, bokey))
            for d in range(2):
                i = fr[(step, d)][0]
                if step > 0 and i in seg_first[d]:
                    cur = sbi[d] % 2
                    memset(S32[d][cur], 0.0, [("S32", d, cur)])
                    memset(Sbf[d][cur], 0.0, [("Sbf", d, cur)])
            for n_ in range(4):
                for d in range(2):
                    i, tsl, ab, abk, bkv, bkvkey = fr[(step, d)]
                    bo, bokey = bos[d]
                    c = n_ if d == 0 else 3 - n_
                    cur = sbi[d] % 2
                    nxt = 1 - cur
                    csl = slice(i * 128 + c * 32, i * 128 + c * 32 + 32)
                    mm(bo[:, c * 32:(c + 1) * 32], Sbf[d][cur], qdec[d][:, csl], False, n_ == 3,
                       [("Sbf", d, cur), ("qdec", d)], [bokey])
                    dcs = dec[d][:, i * 4 + c:i * 4 + c + 1]
                    stt(S32[d][nxt], S32[d][cur], dcs, bkv[c], ALU.mult, ALU.add,
                        [("S32", d, cur), ("dec", d), bkvkey[c]], [("S32", d, nxt)])
                    cp(Sbf[d][nxt], S32[d][nxt], [("S32", d, nxt)], [("Sbf", d, nxt)], eng="act")
                    sbi[d] += 1
            for d in range(2):
                i = fr[(step, d)][0]
                if i in seg_last[d] and seg_last[d][i] >= 0:
                    fin = sbi[d] % 2
                    sp_ = seg_last[d][i]
                    P.dma("sp", o_sh[sp_, d, h], S32[d][fin], [("S32", d, fin)], [("o_sh", sp_, d, h)])
            for d in range(2):
                i, tsl, ab, abk, bkv, bkvkey = fr[(step, d)]
                bo, bokey = bos[d]
                first = (i <= NT - 1 - i) if d == 0 else (NT - 1 - i < i)
                if first:
                    cp(oT[:, tsl], bo[:, 0:128], [bokey], [("oT", i)], eng="act")
                else:
                    tt(oT[:, tsl], bo[:, 0:128], oT[:, tsl], ALU.add, [bokey, ("oT", i)], [("oT", i)])

        front(0)
        for step in range(NT):
            if step + 1 < NT:
                front(step + 1)
            back(step)
        gs = kinv[0]
        for b in range(NBk):
            bk, bkey = nb("proj", [0, 1])
            proj_fm(bk, bkey, w, wk, 512, b * BS, BS)
            act(gs[:, b * BS:(b + 1) * BS], bk[:, 0:BS], AF.Silu, [bkey], [("kinv", 0)])
        ogain = vt[:, V_OG + h:V_OG + h + 1]
        for b in range(NBk):
            sl = slice(b * BS, (b + 1) * BS)
            st = sets[b % 2]
            sq, rs_, t1 = st["A"][:, 0:BS], st["K"][:, 0:BS], st["B"][:, 0:BS]
            kA, kK, kB = (("t" + n_, b % 2) for n_ in "AKB")
            rs2 = st["E"][:, 0:BS]
            kE = ("tE", b % 2)
            okeys = [("oT", i) for i in range(b * BS // 128, (b + 1) * BS // 128)]
            act(sq, oT[:, sl], AF.Square, okeys, [kA])
            bk, bkey = nb("misc", [7])
            mm(bk[:, 0:BS], ones_f, sq, True, True, [kA, "cst"], [bkey])
            act(rs_, bk[:, 0:BS], AF.Ln, [bkey], [kK], scale=1.0 / 128, bias=EPS)
            act(rs2, rs_, AF.Exp, [kK], [kE], scale=-0.5)
            tt(t1, oT[:, sl], rs2, ALU.mult, okeys + [kE], [kB])
            stt(yT[:, h, sl], t1, ogain, gs[:, sl], ALU.mult, ALU.mult, [kB, "vt", ("kinv", 0)], ["yT"])

    def attn_pair(seq, hp):
        tok0, T, cond, segs = seq
        pidx = segs[0][2]
        phase()
        BS = min(512, T)
        NBk = T // BS
        NT = T // 128
        sample = pidx < 0
        w, wk = wget(("at", tok0, hp), attn_loads(hp))
        QT = LA.bf16(T)
        KT = LA.bf16(T)
        gs = LA.bf16(T)
        Ve = LA.bf16(NT * 256).rearrange("p (i h m) -> p i h m", h=2, m=128)
        Vo = LA.bf16(NT * 256).rearrange("p (i h m) -> p i h m", h=2, m=128) if sample else None
        memset(Ve, 1.0, [("Ve", i_, h_) for i_ in range(NT) for h_ in range(2)], eng="pool")
        if sample:
            memset(Vo, 1.0, [("Vo", i_, h_) for i_ in range(NT) for h_ in range(2)], eng="pool")
        if sample:
            kc = LA.bf16(2 * 128).rearrange("p (c m) -> p c m", c=2)
            vc = LA.bf16(2 * 128).rearrange("p (c m) -> p c m", c=2)
            Vc = LA.bf16(2 * 256).rearrange("p (c h m) -> p c h m", c=2, h=2)
            KcT = LA.bf16(256)
            pmh = [LA.f32(14 * 64).rearrange("p (d q) -> p d q", q=64) for _ in range(2)]
            memset(Vc, 1.0, [("Vc", c_, h_) for c_ in range(2) for h_ in range(2)], eng="pool")
            for hh in range(2):
                P.dma("pool", kc[:, :, hh * 64:(hh + 1) * 64], ck[2 * hp + hh].rearrange("(c p) d -> p c d", p=128), (), ["kc"])
                P.dma("pool", vc[:, :, hh * 64:(hh + 1) * 64], cv[2 * hp + hh].rearrange("(c p) d -> p c d", p=128), (), ["vc"])
                P.dma("sp", pmh[hh], pm[:, (2 * hp + hh) * 896:(2 * hp + hh + 1) * 896].rearrange("p (d q) -> p d q", q=64), (), [("pmh", hh)])
        for b in range(NBk):
            sl = slice(b * BS, (b + 1) * BS)
            bk, bkey = nb("proj", [0, 1])
            proj_fm(bk, bkey, w, wk, 0, b * BS, BS)
            act(QT[:, sl], bk[:, 0:BS], AF.Identity, [bkey], ["QT"], scale=0.125)
            bk, bkey = nb("proj", [0, 1])
            proj_fm(bk, bkey, w, wk, 128, b * BS, BS)
            cp(KT[:, sl], bk[:, 0:BS], [bkey], ["KT"])
            bk, bkey = nb("proj", [0, 1])
            proj_fm(bk, bkey, w, wk, 384, b * BS, BS)
            act(gs[:, sl], bk[:, 0:BS], AF.Silu, [bkey], ["gs"])

        VT = LA.bf16(T)
        for b in range(NBk):
            sl = slice(b * BS, (b + 1) * BS)
            bk, bkey = nb("proj", [0, 1])
            proj_fm(bk, bkey, w, wk, 256, b * BS, BS)
            if bkey[1] == 0:
                act(VT[:, sl], bk[:, 0:BS], AF.Identity, [bkey], ["VT"])
            else:
                cp(VT[:, sl], bk[:, 0:BS], [bkey], ["VT"])

        def vtrans(dst, dname, ntile, off):
            for g0 in range(0, ntile, 4):
                ng = min(4, ntile - g0)
                bt, btkey = nb("tp", [6, 7])
                btb = bt.bitcast(BF16)
                for j in range(ng):
                    t0 = off + (g0 + j) * 128
                    tr(btb[:, j * 128:(j + 1) * 128], VT[:, t0:t0 + 128], ident_b, ["VT", "cbf"], [btkey])
                src3 = btb[:, 0:ng * 128].rearrange("p (j m) -> p j m", m=128)
                eng_ = "act" if btkey[1] == 6 else "dve"
                cp(dst[:, g0:g0 + ng, 0, 0:64], src3[:, :, 0:64], [btkey], [(dname, g0 + j, 0) for j in range(ng)], eng=eng_)
                cp(dst[:, g0:g0 + ng, 1, 64:128], src3[:, :, 64:128], [btkey], [(dname, g0 + j, 1) for j in range(ng)], eng=eng_)

        vtrans(Ve, "Ve", NT, 0)
        if sample:
            vtrans(Vo, "Vo", NT - 1, 64)
        if sample:
            bt, btkey = nb("tp", [6])
            btb = bt.bitcast(BF16)
            for c in range(2):
                tr(btb[:, c * 128:(c + 1) * 128], kc[:, c, :], ident_b, ["kc", "cbf"], [btkey])
            cp(KcT, btb[:, 0:256], [btkey], ["KcT"])
            for c in range(2):
                cp(Vc[:, c, 0, 0:64], vc[:, c, 0:64], ["vc"], [("Vc", c, 0)])
                cp(Vc[:, c, 1, 64:128], vc[:, c, 64:128], ["vc"], [("Vc", c, 1)], eng="act")
            KsrcT, Vsrc, kkeys = KcT, Vc, ["KcT", "Vc"]
        else:
            KsrcT, Vsrc, kkeys = KT, Ve, ["KT", "Ve"]
        Pc = [LA.bf16(2 * 512).rearrange("p (c q) -> p c q", c=2) for _ in range(2)]
        NS = 3
        sbt = [LA.f32(256) for _ in range(NS)]
        plt = [LA.bf16(256).rearrange("p (a q) -> p a q", q=64) for _ in range(NS)]
        rden = [LA.f32(512) for _ in range(2)]
        tnum = [LA.f32(512) for _ in range(2)]
        if sample:
            groups = [(hh, slice(G * BS, (G + 1) * BS), BS, 0, G) for hh in range(2) for G in range(NBk)]
        else:
            groups = [(hh, slice(t0, t0 + Ts), Ts, t0 // 128, 0) for hh in range(2) for (t0, Ts, sp_) in segs]
        gstate = {}

        def ctx_front(gi):
            hh, qsl, nq, kb, G = groups[gi]
            hs = slice(hh * 64, (hh + 1) * 64)
            pcb = Pc[gi % 2]
            pck = ("Pc", gi % 2)
            for c in range(2):
                bc, bckey = nb("ctx", [0, 1])
                mm(bc[:, 0:nq], KsrcT[hs, (kb + c) * 128:(kb + c + 1) * 128], QT[hs, qsl], True, True, [kkeys[0], "QT"], [bckey])
                act(pcb[:, c, 0:nq], bc[:, 0:nq], AF.Exp, [bckey], [pck])
            gstate[gi] = (pcb, pck)

        def ctx_pv(gi):
            hh, qsl, nq, kb, G = groups[gi]
            pcb, pck = gstate[gi]
            bo, bokey = nb("o", [3, 4])
            for c in range(2):
                vk = ("Vc", c, hh) if sample else ("Ve", kb + c, hh)
                mm(bo[:, 0:nq], Vsrc[:, kb + c, hh, :], pcb[:, c, 0:nq], c == 0, (not sample) and c == 1, [vk, pck], [bokey])
            gstate[gi] = (pcb, pck, bo, bokey)

        def finish(gi):
            hh, qsl, nq, kb, G = groups[gi]
            hs = slice(hh * 64, (hh + 1) * 64)
            ds = slice((1 - hh) * 64, (2 - hh) * 64)
            pcb, pck, bo, bokey = gstate[gi]
            rd, tn = rden[gi % 2], tnum[gi % 2]
            act(tn[hs, 0:nq], bo[ds, 0:nq], AF.Ln, [bokey], [("tnum", gi % 2)])
            act(rd[hs, 0:nq], tn[hs, 0:nq], AF.Exp, [("tnum", gi % 2)], [("rden", gi % 2)], scale=-1.0)
            tt(tn[hs, 0:nq], bo[hs, 0:nq], rd[hs, 0:nq], ALU.mult, [bokey, ("rden", gi % 2)], [("tnum", gi % 2)])
            tt(yT[hs, 4 + hp, qsl], tn[hs, 0:nq], gs[hs, qsl], ALU.mult, [("tnum", gi % 2), "gs"], [("yTa", hh)])

        rowl = [(gi, rl) for gi in range(len(groups)) for rl in range(8)] if sample else []
        rstate = {}

        def row_front(n):
            gi, rl = rowl[n]
            hh, qsl_, nq_, kb_, G = groups[gi]
            hs = slice(hh * 64, (hh + 1) * 64)
            r = G * 8 + rl
            rs = min(max(r - 4, 0), 24)
            j = n % NS
            bs_, bskey = nb("s", [2, 5, 7])
            for i in range(4):
                k0 = (rs + 2 * i) * 64
                mm(bs_[:, i * 64:(i + 1) * 64], KT[hs, k0:k0 + 128], QT[hs, r * 64:(r + 1) * 64], True, True,
                   ["KT", "QT"], [bskey])
            idx0 = rs - r + 7
            tt(sbt[j].rearrange("p (a q) -> p a q", q=64), bs_[:, 0:256].rearrange("p (a q) -> p a q", q=64),
               pmh[hh][:, idx0:idx0 + 7:2, :], ALU.add, [bskey, ("pmh", hh)], [("sbt", j)])
            act(plt[j], sbt[j].rearrange("p (a q) -> p a q", q=64), AF.Exp, [("sbt", j)], [("plt", j)])
            rstate[n] = (rs, j)

        def row_back(n):
            gi, rl = rowl[n]
            hh, qsl_, nq_, kb_, G = groups[gi]
            rs, j = rstate[n]
            pcb, pck, bo, bokey = gstate[gi]
            for i in range(4):
                kr = rs + 2 * i
                Vt = Ve[:, kr // 2, hh, :] if kr % 2 == 0 else Vo[:, (kr - 1) // 2, hh, :]
                vk = ("Ve", kr // 2, hh) if kr % 2 == 0 else ("Vo", (kr - 1) // 2, hh)
                mm(bo[:, rl * 64:(rl + 1) * 64], Vt, plt[j][:, i, :], False, (rl == 7 and i == 3),
                   [vk, ("plt", j)], [bokey])

        if sample:
            LOOK = 2
            ctx_front(0)
            for n in range(min(LOOK, len(rowl))):
                row_front(n)
            for n in range(len(rowl)):
                gi, rl = rowl[n]
                if rl == 0:
                    ctx_pv(gi)
                    if gi + 1 < len(groups):
                        ctx_front(gi + 1)
                if n + LOOK < len(rowl):
                    row_front(n + LOOK)
                row_back(n)
                if rl == 7:
                    finish(gi)
        else:
            for gi in range(len(groups)):
                ctx_front(gi)
                ctx_pv(gi)
                finish(gi)

    def kv_out(seq):
        tok0, T, cond, segs = seq
        pidx = segs[0][2]
        phase()
        wsrc = w_in_e.rearrange("(k p) n -> p k n", p=128)
        stage = [LA.f32(512) for _ in range(2)]
        n = 0
        for which, c0, dst in ((0, 3072, o_k), (1, 3584, o_v)):
            w, wk = wslot()
            wload(w, wk, wsrc, c0, 512)
            for i in range(T // 128):
                bk, bkey = nb("proj", [0, 1])
                for k in range(8):
                    mm(bk[:, :], hT[:, k, i * 128:(i + 1) * 128], w[:, k, 0:512], k == 0, k == 7, [wk, "hT"], [bkey])
                st = stage[n % 2]
                sk = ("stage", n % 2)
                n += 1
                cp(st, bk[:, :], [bkey], [sk], eng="act")
                sp_, il = [(sp2, i - t0 // 128) for (t0, Ts, sp2) in segs if t0 // 128 <= i < (t0 + Ts) // 128][0]
                P.dma("sp", dst[sp_, :, il * 128:(il + 1) * 128, :].rearrange("h t d -> t h d"),
                      st.rearrange("p (h d) -> p h d", d=64), [sk], [("okv", which, sp_, il)])

    def rg_frontA(seq, c, gwb, L, par, gate=True):
        tok0, T, cond, segs = seq
        BS = min(512, T)
        NBk = T // BS
        kx = L["kx"]
        w, wk = wget(("rg", tok0, c), rg_loads(c))
        if c + 1 < 8:
            wissue(("rg", tok0, c + 1), rg_loads(c + 1))
        L.setdefault("w", {})[c] = (w, wk)
        xb, xc, xcb, gs = L["xb"], L["xc"][par], L["xcb"][par], L["gs"][par]
        kxc, kxcb, kgs = ("xc", kx, par), ("xcb", kx, par), ("gs", kx, par)
        for b in range(NBk):
            sl = slice(b * BS, (b + 1) * BS)
            bk, bkey = nb("projx", [2, 3])
            proj_fm(bk, bkey, w, wk, 0, b * BS, BS)
            cp(xb[:, sl], bk[:, 0:BS], [bkey], [("xb", kx)])
        cw = vt[:, V_CW:V_CW + 32].rearrange("p (j c) -> p j c", j=4)
        cb_ = vt[:, V_CB + c:V_CB + c + 1]
        ts(xc[:, 0:T], xb[:, 0:T], cw[:, 2, c:c + 1], ALU.mult, [("xb", kx), "vt"], [kxc], s2=cb_, op1=ALU.add)
        for (t0, Ts, sp_) in segs:
            e_ = t0 + Ts
            stt(xc[:, t0 + 2:e_], xb[:, t0:e_ - 2], cw[:, 0, c:c + 1], xc[:, t0 + 2:e_], ALU.mult, ALU.add, [("xb", kx), kxc, "vt"], [kxc])
            stt(xc[:, t0 + 1:e_], xb[:, t0:e_ - 1], cw[:, 1, c:c + 1], xc[:, t0 + 1:e_], ALU.mult, ALU.add, [("xb", kx), kxc, "vt"], [kxc])
            stt(xc[:, t0:e_ - 1], xb[:, t0 + 1:e_], cw[:, 3, c:c + 1], xc[:, t0:e_ - 1], ALU.mult, ALU.add, [("xb", kx), kxc, "vt"], [kxc])
        cp(xcb[:, 0:T], xc[:, 0:T], [kxc], [kxcb], eng="pool" if L["pipe"] else "act")
        if gate:
            rg_gate(seq, c, L, par)

    def rg_gate(seq, c, L, par):
        tok0, T, cond, segs = seq
        BS = min(512, T)
        w, wk = L["w"][c]
        gs = L["gs"][par]
        for b in range(T // BS):
            sl = slice(b * BS, (b + 1) * BS)
            bk, bkey = nb("projx", [2, 3])
            proj_fm(bk, bkey, w, wk, 128, b * BS, BS)
            act(gs[:, sl], bk[:, 0:BS], AF.Silu, [bkey], [("gs", L["kx"], par)])

    def rg_frontB(seq, c, gwb, L, par, part="all"):
        tok0, T, cond, segs = seq
        BS = min(512, T)
        NBk = T // BS
        kx = L["kx"]
        xcb = L["xcb"][par]
        kxcb = ("xcb", kx, par)
        gb = vt[:, V_GB:V_GB + 32].rearrange("p (d g c) -> p d g c", d=2, g=2)

        def sig(d):
            aT, uT = L["aT"][d], L["uT"][d]
            for b in range(NBk):
                sl = slice(b * BS, (b + 1) * BS)
                bk, bkey = nb("proj", [0, 1])
                mm(bk[:, 0:BS], gwb[:, (d * 2 + 0) * 8 + c, :], xcb[:, sl], True, True, ["gwb", kxcb], [bkey])
                act(aT[:, sl], bk[:, 0:BS], AF.Sigmoid, [bkey, "vt"], [("aT", d, kx)], bias=gb[:, d, 0, c:c + 1])
                bk, bkey = nb("proj", [0, 1])
                mm(bk[:, 0:BS], gwb[:, (d * 2 + 1) * 8 + c, :], xcb[:, sl], True, True, ["gwb", kxcb], [bkey])
                act(uT[:, sl], bk[:, 0:BS], AF.Sigmoid, [bkey, "vt"], [("uT", d, kx)], bias=gb[:, d, 1, c:c + 1])

        def expo(d):
            aT, a2 = L["aT"][d], L["a2"][d]
            act(a2[:, 0:T], aT[:, 0:T], AF.Exp, [("aT", d, kx), "cc2"], [L["a2key"][d]], scale=cc2[:, d, c:c + 1])
            act(aT[:, 0:T], aT[:, 0:T], AF.Exp, [("aT", d, kx), "cc1"], [("aT", d, kx)], scale=cc1[:, d, c:c + 1])

        def sqr(d):
            a2 = L["a2"][d]
            act(a2[:, 0:T], a2[:, 0:T], AF.Sqrt, [L["a2key"][d]], [L["a2key"][d]], scale=-1.0, bias=1.0)

        if L["split"]:
            sig(0); sig(1); expo(0); expo(1); sqr(0); sqr(1)
        else:
            if part in ("all", "B1"):
                sig(0); sig(1)
            if part in ("all", "B2"):
                expo(0); sqr(0)
                rg_tail(seq, c, L, 0, par)
                expo(1); sqr(1)
                rg_tail(seq, c, L, 1, par)

    def rg_tail(seq, c, L, d, par):
        tok0, T, cond, segs = seq
        kx = L["kx"]
        sample = segs[0][2] < 0
        xc, hs = L["xc"][par], L["hs"]
        kxc = ("xc", kx, par)
        aT, uT, a2 = L["aT"][d], L["uT"][d], L["a2"][d]
        srg = vt[:, V_SRG:V_SRG + 16].rearrange("p (d c) -> p d c", d=2)
        H = T // 2
        for hi, (hf, eng_) in enumerate(((slice(0, H), "dve"), (slice(H, T), "pool"))):
            tt(uT[:, hf], uT[:, hf], a2[:, hf], ALU.mult, [("uT", d, kx), L["a2key"][d]], [("uT", d, kx, hi)], eng=eng_)
            tt(uT[:, hf], uT[:, hf], xc[:, hf], ALU.mult, [("uT", d, kx, hi), kxc], [("uT", d, kx, hi)], eng=eng_)
        init = srg[:, d, c:c + 1] if sample else 0.0
        ukeys = [("uT", d, kx, 0), ("uT", d, kx, 1), ("uT", d, kx)]
        for (t0, Ts, sp_) in segs:
            e_ = t0 + Ts
            if d == 0:
                scan(hs[0][:, t0:e_], aT[:, t0:e_], uT[:, t0:e_], init, [("aT", 0, kx), "vt"] + ukeys, [("hs", 0, kx)])
                if not sample:
                    cp(rgfin[:, sp_, 0, c:c + 1], hs[0][:, e_ - 1:e_], [("hs", 0, kx)], ["rgfin"])
            else:
                scan(hs[1][:, t0:e_][:, ::-1], aT[:, t0:e_][:, ::-1], uT[:, t0:e_][:, ::-1], init,
                     [("aT", 1, kx), "vt"] + ukeys, [L["hs1key"]])
                if not sample:
                    cp(rgfin[:, sp_, 1, c:c + 1], hs[1][:, t0:t0 + 1], [L["hs1key"]], ["rgfin"])

    def rg_back(seq, c, L, par):
        tok0, T, cond, segs = seq
        kx = L["kx"]
        hs, gs = L["hs"], L["gs"][par]
        kgs = ("gs", kx, par)
        if L["split"]:
            for d in range(2):
                rg_tail(seq, c, L, d, par)
        H = T // 2
        for hf, eng_ in ((slice(0, H), "dve"), (slice(H, T), "pool")):
            tt(hs[0][:, hf], hs[0][:, hf], hs[1][:, hf], ALU.add, [("hs", 0, kx), L["hs1key"]], [("hsum", kx, eng_)], eng=eng_)
            tt(yT[:, c, hf], hs[0][:, hf], gs[:, hf], ALU.mult, [("hsum", kx, eng_), kgs], [("yTc", c, eng_)], eng=eng_)

    def rglru_layer(seq):
        tok0, T, cond, segs = seq
        phase()
        small_ = T <= 512
        gwb = LA.bf16(32 * 128).rearrange("p (n j) -> p n j", j=128)
        P.dma("pool", gwb, gwd.rearrange("n i j -> i n j"), (), ["gwb"])
        if small_:
            sets = []
            for j in range(2):
                L = {"kx": j, "split": True, "pipe": False, "hs1key": ("hs", 1, j)}
                L["xb"] = LA.f32(T)
                L["xc"] = [LA.f32(T)]
                L["aT"] = [LA.f32(T), LA.f32(T)]
                L["uT"] = [LA.f32(T), LA.f32(T)]
                L["a2"] = [LA.f32(T), LA.f32(T)]
                L["a2key"] = [("a2", 0, j), ("a2", 1, j)]
                L["hs"] = [LA.f32(T), LA.f32(T)]
                L["xcb"] = [LA.bf16(T)]
                L["gs"] = [LA.bf16(T)]
                sets.append(L)
            rg_frontA(seq, 0, gwb, sets[0], 0)
            rg_frontB(seq, 0, gwb, sets[0], 0)
            for c in range(8):
                if c + 1 < 8:
                    rg_frontA(seq, c + 1, gwb, sets[(c + 1) % 2], 0)
                    rg_frontB(seq, c + 1, gwb, sets[(c + 1) % 2], 0)
                rg_back(seq, c, sets[c % 2], 0)
        else:
            L = {"kx": 0, "split": False, "pipe": True, "hs1key": ("a2", 0)}
            L["xb"] = LA.f32(T)
            L["xc"] = [LA.f32(T), LA.f32(T)]
            L["aT"] = [LA.f32(T), LA.f32(T)]
            L["uT"] = [LA.f32(T), LA.f32(T)]
            a2_ = LA.f32(T)
            L["a2"] = [a2_, a2_]
            L["a2key"] = [("a2", 0), ("a2", 0)]
            L["hs"] = [LA.f32(T), a2_]
            L["xcb"] = [LA.bf16(T), LA.bf16(T)]
            L["gs"] = [LA.bf16(T), LA.bf16(T)]
            rg_frontA(seq, 0, gwb, L, 0)
            for c in range(8):
                rg_frontB(seq, c, gwb, L, c % 2, part="B1")
                if c + 1 < 8:
                    rg_frontA(seq, c + 1, gwb, L, (c + 1) % 2, gate=False)
                rg_frontB(seq, c, gwb, L, c % 2, part="B2")
                if c + 1 < 8:
                    rg_gate(seq, c + 1, L, (c + 1) % 2)
                rg_back(seq, c, L, c % 2)

    def out_phase(l, seq, w_out):
        tok0, T, cond, segs = seq
        pidx = segs[0][2]
        phase()
        woutg = LA.bf16(8 * 1024).rearrange("p (k n) -> p k n", k=8)
        stg = [LA.f32(1024) for _ in range(2)]
        fg = LA.f32(1024)
        junk = LA.f32(1024)
        gate_t = LA.f32(1024)
        P.dma("sp", gate_t, gscr[l][:, cond * 1024:(cond + 1) * 1024], (), ["gate_t"])
        if l == 1:
            P.dma("sp", fg, rows[:, 2048:3072], (), ["fg"])
        for k in range(8):
            s_ = stg[k % 2]
            sk = ("stg", k % 2)
            P.dma("sp", s_, w_out[k * 128:(k + 1) * 128, :], (), [sk])
            tt(woutg[:, k, :], s_, gate_t, ALU.mult, [sk, "gate_t"], [("woutg", k)])
        src = xall if l == 0 else xres
        xn = [LA.bf16(1024) for _ in range(2)]
        stats = LA.f32(2 * (T // 128))
        nfr = {}

        def nfront(i, xt, xk):
            ssq = stats[:, 2 * i:2 * i + 1]
            rstd = stats[:, 2 * i + 1:2 * i + 2]
            sk = ("stats", i)
            act(junk, xt, AF.Square, [xk], [sk], accum=ssq)
            act(ssq, ssq, AF.Sqrt, [sk], [sk], scale=1.0 / 1024, bias=EPS)
            recip(rstd, ssq, [sk], [sk])
            xnb = xn[i % 2]
            xnk = ("xn", i % 2)
            ts(xnb, xt, rstd, ALU.mult, [xk, sk], [xnk])

        def nback(i):
            xnb = xn[i % 2]
            xnk = ("xn", i % 2)
            bk, bkey = nb("tp", [6, 7])
            bkb = bk.bitcast(BF16)
            for k in range(8):
                tr(bkb[:, k * 128:(k + 1) * 128], xnb[:, k * 128:(k + 1) * 128], ident_b, [xnk, "cbf"], [bkey])
            for k in range(8):
                dst_ = hT[:, k, i * 128:(i + 1) * 128]
                if bkey[1] == 6:
                    act(dst_, bkb[:, k * 128:(k + 1) * 128], AF.Identity, [bkey, "Amod", "fm"], [("hTt", i, k)],
                        scale=Amod[:, 1, cond, k:k + 1], bias=fm[:, 1, k, cond:cond + 1])
                else:
                    ts(dst_, bkb[:, k * 128:(k + 1) * 128], Amod[:, 1, cond, k:k + 1], ALU.mult,
                       [bkey, "Amod", "fm"], [("hTt", i, k)], s2=fm[:, 1, k, cond:cond + 1], op1=ALU.add)

        for jp in range(T // 256):
            rsl = slice(tok0 + jp * 256, tok0 + (jp + 1) * 256)
            x2, xk = xslot()
            P.dma("pool", x2, src[rsl, :].rearrange("(j p) n -> p j n", p=128), (), [xk])
            for t_ in range(2):
                i = jp * 2 + t_
                xt = x2[:, t_, :]
                for half in range(2):
                    bk, bkey = nb("proj", [0, 1])
                    for k in range(8):
                        mm(bk[:, :], yT[:, k, i * 128:(i + 1) * 128], woutg[:, k, half * 512:(half + 1) * 512], k == 0, k == 7,
                           ["yT", ("woutg", k)], [bkey])
                    tt(xt[:, half * 512:(half + 1) * 512], bk[:, :], xt[:, half * 512:(half + 1) * 512], ALU.add, [bkey, xk], [xk])
                if l == 1:
                    ssq = small[:, 8 + 2 * (i % 2):8 + 2 * (i % 2) + 1]
                    rstd = small[:, 9 + 2 * (i % 2):9 + 2 * (i % 2) + 1]
                    sk = ("small2", i % 2)
                    act(junk, xt, AF.Square, [xk], [sk], accum=ssq)
                    act(ssq, ssq, AF.Sqrt, [sk], [sk], scale=1.0 / 1024, bias=EPS)
                    recip(rstd, ssq, [sk], [sk])
                    stt(xt, xt, rstd, fg, ALU.mult, ALU.mult, [xk, sk, "fg"], [xk])
                else:
                    nfront(i, xt, xk)
                    if i > 0:
                        nback(i - 1)
            dst = xres if l == 0 else y_all
            P.dma("sp", dst[rsl, :].rearrange("(j p) n -> p j n", p=128), x2, [xk], [("xo", l, tok0, jp)])
        if l == 0:
            nback(T // 128 - 1)

    plan = []
    plan.append((lambda: mod_phase(0), None))
    plan.append((lambda: mod_phase(1), None))
    for seq in SEQS:
        plan.append((lambda seq=seq: norm_phase(0, seq), None))
        for h in range(4):
            plan.append((lambda seq=seq, h=h: hgrn_head(seq, h), (("hg", seq[0], h), hgrn_loads(h))))
        for hp in range(4):
            plan.append((lambda seq=seq, hp=hp: attn_pair(seq, hp), (("at", seq[0], hp), attn_loads(hp))))
        if seq[3][0][2] >= 0:
            plan.append((lambda seq=seq: kv_out(seq), None))
        plan.append((lambda seq=seq: out_phase(0, seq, w_out_e), None))
        plan.append((lambda seq=seq: rglru_layer(seq), (("rg", seq[0], 0), rg_loads(0))))
        plan.append((lambda seq=seq: out_phase(1, seq, w_out_o), None))
    for j, (fn, wl) in enumerate(plan):
        if j + 1 < len(plan) and plan[j + 1][1] is not None:
            nid, nloads = plan[j + 1][1]
            PF[0] = (lambda nid=nid, nloads=nloads: wissue(nid, nloads))
        fn()
    phase()
    for s_ in range(2):
        for d in range(2):
            P.dma("sp", o_rg[s_, d].rearrange("(c p o) -> p c o", p=128, o=1), rgfin[:, s_, d, :].unsqueeze(2), ["rgfin"], [("o_rg", s_, d)],
                  allow_slow_non_contiguous=True)
    info = P.finalize()
    return nc, info


_CACHE = {}


def _host_consts():
    c = np.zeros((128, 1536), np.float32)
    c[:, 0:128] = np.eye(128, dtype=np.float32)
    c[:, 128:256] = 1.0
    s = np.arange(128)[:, None]
    t = np.arange(128)[None, :]
    same = (s // 32) == (t // 32)
    c[:, 256:384] = (same & (s <= t)).astype(np.float32)
    c[:, 384:512] = (same & (s >= t)).astype(np.float32)
    tt_ = np.arange(512)
    c[:, 512:1024] = (tt_ % 32 != 0).astype(np.float32)[None, :]
    c[:, 1024:1536] = (tt_ % 32 != 31).astype(np.float32)[None, :]
    return c


def _bias_table(rel_bias):
    kc = np.arange(64)[:, None]
    qc = np.arange(64)[None, :]
    ws = np.clip(qc - 8, 0, 48)
    valid = (kc >= ws) & (kc < ws + 16)
    dc = np.clip(kc - qc + 15, 0, 30)
    M = np.full((8, 16, 64, 64), -1e30, np.float32)
    for dd in range(-7, 8):
        g = rel_bias[:, dd + 7][:, dc]
        M[:, dd + 7] = np.where(valid[None], g, np.float32(-1e30))
    PMt = np.empty((8, 14, 128, 64), np.float32)
    for idx in range(14):
        PMt[:, idx, 0:64] = M[:, idx]
        PMt[:, idx, 64:128] = M[:, idx + 1]
    return np.ascontiguousarray(PMt.transpose(2, 0, 1, 3).reshape(128, 8 * 14 * 64))


def fm(v):
    v = np.asarray(v, np.float32)
    lead = v.shape[:-1]
    n = v.shape[-1] // 128
    return np.moveaxis(v.reshape(lead + (n, 128)), -1, 0)


def kernel(x_prompt, x_sample, state_hgrn, cache_na_k, cache_na_v, state_rglru, c, c_ctx, norm_gain,
           w_mod, b_mod, w_in_even, w_out_even, hgrn_lb_logits, hgrn_out_gain, na_rel_bias,
           w_in_odd, w_out_odd, conv_w, conv_b, rg_gate_w, rg_gate_b, rg_lambda, final_gain):
    f = lambda a: np.ascontiguousarray(np.asarray(a, np.float32))
    if "nc" not in _CACHE:
        _CACHE["nc"] = build_nc()
    nc, info = _CACHE["nc"]
    consts = _host_consts()
    pmt = _bias_table(f(na_rel_bias)[0])
    rows = np.empty((128, 3072), np.float32)
    rows[:, 0:1024] = f(b_mod)[0, 2048:3072][None]
    rows[:, 1024:2048] = f(b_mod)[1, 2048:3072][None]
    rows[:, 2048:3072] = f(final_gain)[None]
    in_maps = []
    for b in range(8):
        vecs = np.zeros((128, NV), np.float32)
        cond2 = np.stack([f(c)[b], f(c_ctx)], 0)
        vecs[:, V_COND:V_COND + 16] = fm(cond2).transpose(0, 2, 1).reshape(128, 16)
        vecs[:, V_NG:V_NG + 16] = fm(norm_gain).reshape(128, 16)
        vecs[:, V_BMOD:V_BMOD + 48] = fm(b_mod).reshape(128, 48)
        vecs[:, V_LBL:V_LBL + 16] = fm(hgrn_lb_logits).reshape(128, 16)
        vecs[:, V_OG:V_OG + 4] = f(hgrn_out_gain)[0].T
        vecs[:, V_CW:V_CW + 32] = fm(f(conv_w)[0]).reshape(128, 32)
        vecs[:, V_CB:V_CB + 8] = fm(f(conv_b)[0]).reshape(128, 8)
        vecs[:, V_GB:V_GB + 32] = fm(f(rg_gate_b)[0]).reshape(128, 32)
        vecs[:, V_LAM:V_LAM + 16] = fm(f(rg_lambda)[0]).reshape(128, 16)
        vecs[:, V_SRG:V_SRG + 16] = fm(f(state_rglru)[b, 0]).reshape(128, 16)
        xall = np.concatenate([f(x_sample)[b], f(x_prompt)[2 * b], f(x_prompt)[2 * b + 1]], 0)
        in_maps.append({
            "xall": np.ascontiguousarray(xall),
            "st_h": f(state_hgrn)[b, 0], "ck": f(cache_na_k)[b, 0], "cv": f(cache_na_v)[b, 0],
            "vecs": vecs, "rows": rows, "consts": consts, "pm": pmt,
            "w_mod": f(w_mod), "w_in_e": f(w_in_even)[0], "w_out_e": f(w_out_even)[0],
            "w_in_o": f(w_in_odd)[0], "w_out_o": f(w_out_odd)[0],
            "gw": f(rg_gate_w)[0].reshape(32, 128, 128),
        })
    res = run_bass_kernel_spmd(nc, in_maps, core_ids=list(range(8)))
    R = res.results
    y_sample = np.stack([R[b]["y_all"][0:2048] for b in range(8)], 0)
    y_prompt = np.stack([R[b]["y_all"][2048 + 256 * s:2048 + 256 * (s + 1)] for b in range(8) for s in range(2)], 0)
    n_sh = np.concatenate([R[b]["o_sh"] for b in range(8)], 0)[:, None]
    n_k = np.concatenate([R[b]["o_k"] for b in range(8)], 0)[:, None]
    n_v = np.concatenate([R[b]["o_v"] for b in range(8)], 0)[:, None]
    n_rg = np.concatenate([R[b]["o_rg"] for b in range(8)], 0)[:, None]
    return (y_prompt.astype(np.float32), y_sample.astype(np.float32), n_sh.astype(np.float32),
            n_k.astype(np.float32), n_v.astype(np.float32), n_rg.astype(np.float32))
```
